# Optimizing a Trainium2 kernel written in Bass

```python
import math
import jax, jax.numpy as jnp
from jax import lax
import numpy as np

D_MODEL = 1024
BATCH = 4
SEQ = 8192
DEPTH = 1
DEC_BATCH = 16
DEC_SEQ = 32
PAST_LEN = 2048

CHUNK = 64
GLA_HEADS = 4
GLA_DK = 64
GLA_DV = 128
GLA_LR = 16
GLA_GATE_TAU = 16.0
GDN_HEADS = 4
GDN_DK = 128
GDN_DV = 128
CONV_W = 4
GLA_QK = GLA_HEADS * GLA_DK
GLA_V = GLA_HEADS * GLA_DV
GDN_QK = GDN_HEADS * GDN_DK
GDN_V = GDN_HEADS * GDN_DV
GDN_QKV = 2 * GDN_QK + GDN_V
D_MIX = GLA_V + GDN_V
IN_SIZES = (GLA_QK, GLA_QK, GLA_V, GLA_LR, GLA_V, GDN_QKV, GDN_HEADS, GDN_HEADS, GDN_V)
N_IN = GLA_QK * 2 + GLA_V * 2 + GLA_LR + GDN_QKV + 2 * GDN_HEADS + GDN_V
PEER_HEADS = 8
PEER_QDIM = 256
PEER_HALF = PEER_QDIM // 2
N_KEYS = 128
N_EXPERTS = N_KEYS * N_KEYS
PEER_TOPK = 16
PEER_BLOCK = 256
NORM_EPS = 1e-6

kernel_name = "hymba_gla_gdn_peer_stream_step"


def rms_norm(x, w):
    xf = x.astype(jnp.float32)
    y = xf * lax.rsqrt(jnp.mean(xf * xf, axis=-1, keepdims=True) + NORM_EPS)
    return (y * w.astype(jnp.float32)).astype(x.dtype)


def l2_norm(x):
    xf = x.astype(jnp.float32)
    return (xf * lax.rsqrt(jnp.sum(xf * xf, axis=-1, keepdims=True) + NORM_EPS)).astype(x.dtype)


def _split_cols(x, sizes):
    idx, acc = [], 0
    for s in sizes[:-1]:
        acc += s
        idx.append(acc)
    return jnp.split(x, idx, axis=-1)


def _to_chunks(x):
    b, l = x.shape[:2]
    n = -(-l // CHUNK)
    x = jnp.pad(x, [(0, 0), (0, n * CHUNK - l)] + [(0, 0)] * (x.ndim - 2))
    x = x.reshape((b, n, CHUNK) + x.shape[2:])
    return jnp.moveaxis(x, (1, 2), (0, 3))


def _from_chunks(x, l):
    x = jnp.moveaxis(x, (0, 3), (1, 2))
    n, c = x.shape[1], x.shape[2]
    x = x.reshape((x.shape[0], n * c) + x.shape[3:])
    return x[:, :l]


def _gla_scan(q, k, v, log_a, s0):
    l = q.shape[1]
    xs = tuple(_to_chunks(t.astype(jnp.float32)) for t in (q, k, v, log_a))
    tril = jnp.tril(jnp.ones((CHUNK, CHUNK), dtype=bool))[:, :, None]

    def step(S, inp):
        qc, kc, vc, ac = inp
        G = jnp.cumsum(ac, axis=2)
        rel = G[:, :, :, None, :] - G[:, :, None, :, :]
        dec = jnp.where(tril, jnp.exp(jnp.where(tril, rel, 0.0)), 0.0)
        att = jnp.sum(qc[:, :, :, None, :] * kc[:, :, None, :, :] * dec, axis=-1)
        o = (jnp.einsum('bhcd,bhde->bhce', qc * jnp.exp(G), S)
             + jnp.einsum('bhij,bhje->bhie', att, vc))
        g_last = G[:, :, -1:, :]
        S = (S * jnp.exp(g_last[:, :, 0, :, None])
             + jnp.einsum('bhjd,bhje->bhde', kc * jnp.exp(g_last - G), vc))
        return S, o

    s_fin, o = lax.scan(step, s0.astype(jnp.float32), xs)
    return _from_chunks(o, l).astype(q.dtype), s_fin.astype(s0.dtype)


def _gdn_scan(q, k, v, log_a, beta, s0):
    l = q.shape[1]
    dv = v.shape[-1]
    xs = tuple(_to_chunks(t.astype(jnp.float32)) for t in (q, k, v, log_a, beta))
    tril = jnp.tril(jnp.ones((CHUNK, CHUNK), dtype=bool))
    strict = jnp.tril(jnp.ones((CHUNK, CHUNK), dtype=bool), k=-1)
    eye = jnp.eye(CHUNK, dtype=jnp.float32)

    def step(S, inp):
        qc, kc, vc, ac, bc = inp
        G = jnp.cumsum(ac, axis=-1)
        rel = G[..., :, None] - G[..., None, :]
        dec = jnp.where(tril, jnp.exp(jnp.where(tril, rel, 0.0)), 0.0)
        kk = jnp.einsum('bhid,bhjd->bhij', kc, kc)
        low = jnp.where(strict, bc[..., :, None] * kk * dec, 0.0)
        rhs = jnp.concatenate([vc * bc[..., None], kc * (bc * jnp.exp(G))[..., None]], axis=-1)
        sol = lax.linalg.triangular_solve(eye + low, rhs, left_side=True, lower=True,
                                          unit_diagonal=True)
        u_c, w_c = sol[..., :dv], sol[..., dv:]
        v_new = u_c - jnp.einsum('bhcd,bhde->bhce', w_c, S)
        qk = jnp.einsum('bhid,bhjd->bhij', qc, kc) * dec
        o = (jnp.einsum('bhcd,bhde->bhce', qc * jnp.exp(G)[..., None], S)
             + jnp.einsum('bhij,bhje->bhie', qk, v_new))
        g_last = G[..., -1:]
        S = (S * jnp.exp(g_last)[..., None]
             + jnp.einsum('bhjd,bhje->bhde', kc * jnp.exp(g_last - G)[..., None], v_new))
        return S, o

    s_fin, o = lax.scan(step, s0.astype(jnp.float32), xs)
    return _from_chunks(o, l).astype(q.dtype), s_fin.astype(s0.dtype)


def _peer(h, peer_wq, peer_k1, peer_k2, peer_u, peer_v):
    b, l, d = h.shape
    t = b * l
    nb = -(-t // PEER_BLOCK)
    xt = jnp.pad(h.reshape(t, d), ((0, nb * PEER_BLOCK - t), (0, 0)))
    xb = xt.reshape(nb, PEER_BLOCK, d)

    def block(xblk):
        q = (xblk @ peer_wq).reshape(PEER_BLOCK, PEER_HEADS, PEER_QDIM)
        s1 = jnp.einsum('thd,hnd->thn', q[..., :PEER_HALF], peer_k1)
        s2 = jnp.einsum('thd,hnd->thn', q[..., PEER_HALF:], peer_k2)
        v1, i1 = lax.top_k(s1, PEER_TOPK)
        v2, i2 = lax.top_k(s2, PEER_TOPK)
        cand = (v1[..., :, None] + v2[..., None, :]).reshape(PEER_BLOCK, PEER_HEADS, PEER_TOPK * PEER_TOPK)
        cidx = (i1[..., :, None] * N_KEYS + i2[..., None, :]).reshape(PEER_BLOCK, PEER_HEADS, PEER_TOPK * PEER_TOPK)
        sc, pos = lax.top_k(cand, PEER_TOPK)
        eidx = jnp.take_along_axis(cidx, pos, axis=-1)
        gate = jax.nn.softmax(sc.astype(jnp.float32), axis=-1)
        act = jax.nn.gelu(jnp.einsum('thkd,td->thk', peer_u[eidx], xblk))
        wgt = (gate * act.astype(jnp.float32)).astype(xblk.dtype)
        return jnp.einsum('thk,thkd->td', wgt, peer_v[eidx])

    yb = lax.map(block, xb)
    return yb.reshape(nb * PEER_BLOCK, d)[:t].reshape(b, l, d)


def _layer(x, s_gla, s_gdn, conv_buf, norm_mix_w, w_in, gla_w_gk2, gla_b_gk, gla_norm_w,
           gdn_conv_w, gdn_a_log, gdn_dt_bias, gdn_norm_w, w_out, norm_ffn_w,
           peer_wq, peer_k1, peer_k2, peer_u, peer_v):
    b, l, _ = x.shape
    h = rms_norm(x, norm_mix_w)
    proj = h @ w_in
    gq, gk, gv, glr, gg, dqkv, da, db, dz = _split_cols(proj, IN_SIZES)

    q1 = gq.reshape(b, l, GLA_HEADS, GLA_DK) * (GLA_DK ** -0.5)
    k1 = gk.reshape(b, l, GLA_HEADS, GLA_DK)
    v1 = gv.reshape(b, l, GLA_HEADS, GLA_DV)
    log_a1 = jax.nn.log_sigmoid((glr @ gla_w_gk2 + gla_b_gk).astype(jnp.float32)) / GLA_GATE_TAU
    log_a1 = log_a1.reshape(b, l, GLA_HEADS, GLA_DK)
    o1, s_gla_new = _gla_scan(q1, k1, v1, log_a1, s_gla)
    o1 = rms_norm(o1, gla_norm_w) * jax.nn.silu(gg).reshape(b, l, GLA_HEADS, GLA_DV)

    conv_in = jnp.concatenate([conv_buf.astype(dqkv.dtype), dqkv], axis=1)
    conv = conv_in[:, 0:l] * gdn_conv_w[0]
    for i in range(1, CONV_W):
        conv = conv + conv_in[:, i:i + l] * gdn_conv_w[i]
    conv = jax.nn.silu(conv)
    new_buf = conv_in[:, -(CONV_W - 1):]
    cq, ck, cv = _split_cols(conv, (GDN_QK, GDN_QK, GDN_V))
    q2 = l2_norm(cq.reshape(b, l, GDN_HEADS, GDN_DK)) * (GDN_DK ** -0.5)
    k2 = l2_norm(ck.reshape(b, l, GDN_HEADS, GDN_DK))
    v2 = cv.reshape(b, l, GDN_HEADS, GDN_DV)
    beta = jax.nn.sigmoid(db.astype(jnp.float32))
    log_a2 = -jnp.exp(gdn_a_log.astype(jnp.float32)) * jax.nn.softplus((da + gdn_dt_bias).astype(jnp.float32))
    o2, s_gdn_new = _gdn_scan(q2, k2, v2, log_a2, beta, s_gdn)
    o2 = rms_norm(o2, gdn_norm_w) * jax.nn.silu(dz).reshape(b, l, GDN_HEADS, GDN_DV)

    mix = jnp.concatenate([o1.reshape(b, l, GLA_V), o2.reshape(b, l, GDN_V)], axis=-1) @ w_out
    x = x + mix
    x = x + _peer(rms_norm(x, norm_ffn_w), peer_wq, peer_k1, peer_k2, peer_u, peer_v)
    return x, s_gla_new, s_gdn_new, new_buf


def setup_inputs(seed: int = 0) -> dict:
    key = jax.random.key(seed)
    ks = jax.random.split(key, 24)
    f32 = jnp.float32
    nrm = lambda k, shape, scale: jax.random.normal(k, shape, f32) * scale
    dt = jnp.exp(jax.random.uniform(ks[12], (DEPTH, GDN_HEADS), f32, math.log(1e-3), math.log(1e-1)))
    return {
        "x_prompt": nrm(ks[0], (BATCH, SEQ, D_MODEL), 1.0),
        "x_sample": nrm(ks[1], (DEC_BATCH, DEC_SEQ, D_MODEL), 1.0),
        "state_gla": nrm(ks[2], (DEPTH, DEC_BATCH, GLA_HEADS, GLA_DK, GLA_DV), 0.1),
        "state_gdn": nrm(ks[3], (DEPTH, DEC_BATCH, GDN_HEADS, GDN_DK, GDN_DV), 0.1),
        "state_gdn_conv": nrm(ks[4], (DEPTH, DEC_BATCH, CONV_W - 1, GDN_QKV), 1.0),
        "norm_mix_w": 1.0 + nrm(ks[5], (DEPTH, D_MODEL), 0.02),
        "w_in": nrm(ks[6], (DEPTH, D_MODEL, N_IN), D_MODEL ** -0.5),
        "gla_w_gk2": nrm(ks[7], (DEPTH, GLA_LR, GLA_QK), GLA_LR ** -0.5),
        "gla_b_gk": nrm(ks[8], (DEPTH, GLA_QK), 0.1),
        "gla_norm_w": 1.0 + nrm(ks[9], (DEPTH, GLA_DV), 0.02),
        "gdn_conv_w": nrm(ks[10], (DEPTH, CONV_W, GDN_QKV), CONV_W ** -0.5),
        "gdn_a_log": jnp.log(jax.random.uniform(ks[11], (DEPTH, GDN_HEADS), f32, 1.0, 16.0)),
        "gdn_dt_bias": jnp.log(jnp.expm1(dt)),
        "gdn_norm_w": 1.0 + nrm(ks[13], (DEPTH, GDN_DV), 0.02),
        "w_out": nrm(ks[14], (DEPTH, D_MIX, D_MODEL), D_MIX ** -0.5),
        "norm_ffn_w": 1.0 + nrm(ks[15], (DEPTH, D_MODEL), 0.02),
        "peer_wq": nrm(ks[16], (DEPTH, D_MODEL, PEER_HEADS * PEER_QDIM), D_MODEL ** -0.5),
        "peer_k1": nrm(ks[17], (DEPTH, PEER_HEADS, N_KEYS, PEER_HALF), PEER_HALF ** -0.5),
        "peer_k2": nrm(ks[18], (DEPTH, PEER_HEADS, N_KEYS, PEER_HALF), PEER_HALF ** -0.5),
        "peer_u": nrm(ks[19], (DEPTH, N_EXPERTS, D_MODEL), D_MODEL ** -0.5),
        "peer_v": nrm(ks[20], (DEPTH, N_EXPERTS, D_MODEL), D_MODEL ** -0.5),
        "norm_final_w": 1.0 + nrm(ks[21], (D_MODEL,), 0.02),
    }


def reference(x_prompt, x_sample, state_gla, state_gdn, state_gdn_conv, norm_mix_w, w_in,
              gla_w_gk2, gla_b_gk, gla_norm_w, gdn_conv_w, gdn_a_log, gdn_dt_bias, gdn_norm_w,
              w_out, norm_ffn_w, peer_wq, peer_k1, peer_k2, peer_u, peer_v, norm_final_w):
    yp, ys = x_prompt, x_sample
    gla_p, gdn_p, conv_p, gla_s, gdn_s, conv_s = [], [], [], [], [], []
    for l in range(DEPTH):
        lw = (norm_mix_w[l], w_in[l], gla_w_gk2[l], gla_b_gk[l], gla_norm_w[l], gdn_conv_w[l],
              gdn_a_log[l], gdn_dt_bias[l], gdn_norm_w[l], w_out[l], norm_ffn_w[l],
              peer_wq[l], peer_k1[l], peer_k2[l], peer_u[l], peer_v[l])
        z_gla = jnp.zeros((BATCH, GLA_HEADS, GLA_DK, GLA_DV), x_prompt.dtype)
        z_gdn = jnp.zeros((BATCH, GDN_HEADS, GDN_DK, GDN_DV), x_prompt.dtype)
        z_conv = jnp.zeros((BATCH, CONV_W - 1, GDN_QKV), x_prompt.dtype)
        yp, a, b_, c = _layer(yp, z_gla, z_gdn, z_conv, *lw)
        gla_p.append(a); gdn_p.append(b_); conv_p.append(c)
        ys, a, b_, c = _layer(ys, state_gla[l], state_gdn[l], state_gdn_conv[l], *lw)
        gla_s.append(a); gdn_s.append(b_); conv_s.append(c)
    y_prompt = rms_norm(yp, norm_final_w)
    y_sample = rms_norm(ys, norm_final_w)
    return (y_prompt, y_sample, jnp.stack(gla_p), jnp.stack(gdn_p), jnp.stack(conv_p),
            jnp.stack(gla_s), jnp.stack(gdn_s), jnp.stack(conv_s))
```

```python
from contextlib import ExitStack
import threading
import numpy as np
import concourse.bass as bass
import concourse.mybir as mybir
from concourse.bass import IndirectOffsetOnAxis
from concourse.bass_utils import run_bass_kernel_spmd

F32 = mybir.dt.float32
BF16 = mybir.dt.bfloat16
U32 = mybir.dt.uint32
AF = mybir.ActivationFunctionType
ALU = mybir.AluOpType
AX = mybir.AxisListType

D = 1024
NIN = 3608
O_GQ, O_GK, O_GV, O_GLR, O_GG, O_DQKV, O_DA, O_DB, O_DZ = 0, 256, 512, 1024, 1040, 1552, 3088, 3092, 3096
EPS = 1e-6
NEG = -1e30


class Buf:
    __slots__ = ("name", "w", "r", "dsem", "dcnt", "excl")

    def __init__(self, name):
        self.name = name
        self.excl = False
        self.w = None
        self.r = {}
        self.dsem = None
        self.dcnt = 0


class TL:
    def __init__(self, t, b):
        self.t = t
        self.b = b

    def __getitem__(self, k):
        return self.t[k]


class Co:
    def __init__(self):
        self.slots = {}
        self.order = []
        self.n = 0
        self.nid = 0
        self.err = []
        self.t2s = {}

    def _new(self, k, parent):
        sid = self.nid
        self.nid += 1
        self.slots[sid] = dict(sem=threading.Semaphore(0), k=k, run=True, parent=parent, nchild=0)
        return sid

    def _pick(self, after):
        n = len(self.order)
        st = self.order.index(after)
        for d in range(1, n + 1):
            sid = self.order[(st + d) % n]
            if self.slots[sid]["run"]:
                return sid
        return None

    def fork(self, fns, ks):
        tid = threading.get_ident()
        me = self.t2s.get(tid)
        if me is None:
            me = self._new(1, None)
            self.t2s[tid] = me
            self.order.append(me)
        par = self.slots[me]
        par["run"] = False
        par["nchild"] = len(fns)
        pos = self.order.index(me)
        kids = []
        for i, (f, k) in enumerate(zip(fns, ks)):
            sid = self._new(k, me)
            kids.append(sid)
            self.order.insert(pos + 1 + i, sid)
            threading.Thread(target=self._wrap, args=(sid, f), daemon=True).start()
        self.n = 0
        self.slots[kids[0]]["sem"].release()
        par["sem"].acquire()
        if par["parent"] is None and self.err:
            e = self.err[0]
            self.err = []
            raise e

    def _wrap(self, sid, f):
        sl = self.slots[sid]
        sl["sem"].acquire()
        self.t2s[threading.get_ident()] = sid
        try:
            f()
        except BaseException as ex:
            self.err.append(ex)
        finally:
            sl["run"] = False
            par = self.slots[sl["parent"]]
            par["nchild"] -= 1
            if par["nchild"] == 0:
                par["run"] = True
            nxt = self._pick(sid)
            self.order.remove(sid)
            self.n = 0
            self.slots[nxt]["sem"].release()

    def tick(self):
        sid = self.t2s.get(threading.get_ident())
        if sid is None or len(self.order) <= 1:
            return
        self.n += 1
        if self.n >= self.slots[sid]["k"]:
            self.n = 0
            nxt = self._pick(sid)
            if nxt is not None and nxt != sid:
                self.slots[nxt]["sem"].release()
                self.slots[sid]["sem"].acquire()


class PEProxy:
    def __init__(self, fw, pe):
        self.fw = fw
        self.pe = pe
        self.mode = None

    @staticmethod
    def _r(n):
        return 32 if n <= 32 else (64 if n <= 64 else 128)

    def _sw(self, lhsT):
        shp = lhsT.shape
        mode = (self._r(shp[0]), self._r(int(np.prod(shp[1:]))))
        if self.mode is not None and mode != self.mode:
            fw = self.fw
            if fw.cnt["pe"] > 0 and fw.seen["pe"].get("pe", 0) < fw.cnt["pe"]:
                self.pe.wait_ge(fw.sem["pe"], fw.cnt["pe"])
                fw.seen["pe"]["pe"] = fw.cnt["pe"]
        self.mode = mode

    def matmul(self, out, lhsT, rhs, **kw):
        self._sw(lhsT)
        return self.pe.matmul(out, lhsT=lhsT, rhs=rhs, **kw)

    def transpose(self, out, in_, identity, **kw):
        self._sw(in_)
        return self.pe.transpose(out=out, in_=in_, identity=identity, **kw)


class FW:
    ENGS = ("pe", "dve", "act", "pool", "sp")

    def __init__(self, nc, stack):
        self.nc = nc
        self.stack = stack
        self.eng = {"pe": nc.tensor, "dve": nc.vector, "act": nc.scalar, "pool": nc.gpsimd, "sp": nc.sync}
        self.pex = PEProxy(self, nc.tensor)
        self.sem = {e: stack.enter_context(nc.semaphore("s_" + e)) for e in self.ENGS}
        self.cnt = {e: 0 for e in self.ENGS}
        self.seen = {e: {} for e in self.ENGS}
        self.dseen = {e: {} for e in self.ENGS}
        self.nins = 0
        self.dbufs = []
        self.co = Co()

    def sb(self, name, shape, dt=F32, dma=False):
        t = self.stack.enter_context(self.nc.sbuf_tensor(name, list(shape), dt))
        return TL(t, self.buf(name, dma))

    def ps(self, name, shape, dt=F32):
        t = self.stack.enter_context(self.nc.psum_tensor(name, list(shape), dt))
        tl = TL(t, self.buf(name))
        tl.b.excl = True
        return tl

    def buf(self, name, dma=False):
        b = Buf(name)
        if dma:
            b.dsem = self.stack.enter_context(self.nc.semaphore("d_" + name))
            self.dbufs.append(b)
        return b

    def _wait(self, e, other, idx):
        if idx <= 0 or self.seen[e].get(other, 0) >= idx:
            return
        self.eng[e].wait_ge(self.sem[other], idx)
        self.seen[e][other] = idx

    def _wait_dma(self, e, b):
        if b.dsem is None or b.dcnt == 0 or self.dseen[e].get(b.name, 0) >= b.dcnt:
            return
        self.eng[e].wait_ge(b.dsem, b.dcnt)
        self.dseen[e][b.name] = b.dcnt

    def _deps(self, e, reads, writes):
        for b in reads:
            if b.w is not None:
                self._wait(e, b.w[0], b.w[1])
            if b.excl:
                for re_, ri in b.r.items():
                    if re_ != e:
                        self._wait(e, re_, ri)
            self._wait_dma(e, b)
        for b in writes:
            if b.w is not None:
                self._wait(e, b.w[0], b.w[1])
            for re_, ri in b.r.items():
                self._wait(e, re_, ri)
            self._wait_dma(e, b)

    def op(self, e, fn, reads=(), writes=()):
        reads = [x.b if isinstance(x, TL) else x for x in reads]
        writes = [x.b if isinstance(x, TL) else x for x in writes]
        self._deps(e, reads, writes)
        ins = fn(self.pex if e == "pe" else self.eng[e])
        self.cnt[e] += 1
        idx = self.cnt[e]
        ins.then_inc(self.sem[e], 1)
        self.nins += 1
        for b in reads:
            b.r[e] = idx
        for b in writes:
            b.w = (e, idx)
            b.r = {}
        self.co.tick()
        return ins

    def dma(self, e, fn, reads=(), writes=(), owner=None):
        reads = [x.b if isinstance(x, TL) else x for x in reads]
        writes = [x.b if isinstance(x, TL) else x for x in writes]
        self._deps(e, reads, writes)
        ins = fn(self.eng[e])
        self.nins += 1
        if owner is not None:
            b = owner.b if isinstance(owner, TL) else owner
        else:
            sems = [b for b in reads + writes if b.dsem is not None]
            assert len(sems) == 1, [b.name for b in reads + writes]
            b = sems[0]
        b.dcnt += 16
        ins.then_inc(b.dsem, 16)
        self.co.tick()
        return ins

    def barrier(self):
        for e in self.ENGS:
            for o in self.ENGS:
                if o != e:
                    self._wait(e, o, self.cnt[o])
            for b in self.dbufs:
                self._wait_dma(e, b)

    def finish(self, e="sp"):
        for b in self.dbufs:
            self._wait_dma(e, b)
        for o in self.ENGS:
            if o != e:
                self._wait(e, o, self.cnt[o])


def make_consts():
    c = np.zeros((128, 7, 128), np.float32)
    i = np.arange(128)
    same = (i[:, None] // 64) == (i[None, :] // 64)
    c[:, 0, :] = np.eye(128)
    c[:, 1, :] = (i[:, None] <= i[None, :]) & same
    c[:, 2, :] = (i[:, None] > i[None, :]) & same
    c[:, 3, :] = (i[None, :] < i[:, None]) & same
    c[:, 4, :] = (i[:, None] <= i[None, :]) & same
    c[:, 5, :] = 1.0
    c[:64, 6, 0] = 1.0
    c[64:, 6, 1] = 1.0
    c[:, 6, 16:32] = np.arange(16, dtype=np.float32)[None, :]
    return c


def build(NPRE, NMAIN, NSAMP=2, dbg=False, limit=99):
    nc = bass.Bass("TRN2", target_bir_lowering=False)
    dt_in = lambda n, s: nc.dram_tensor(n, list(s), F32, kind="ExternalInput").ap()
    dt_out = lambda n, s: nc.dram_tensor(n, list(s), F32, kind="ExternalOutput").ap()
    xpre = dt_in("xpre", [max(NPRE, 1) * 128, D])
    xmain = dt_in("xmain", [NMAIN * 128, D])
    xs = dt_in("xs", [NSAMP * 32, D])
    sgla_in = dt_in("sgla_in", [NSAMP, 4, 64, 128])
    sgdn_in = dt_in("sgdn_in", [NSAMP, 4, 128, 128])
    sconv_in = dt_in("sconv_in", [NSAMP, 3, 1536])
    cst_in = dt_in("cst", [128, 7, 128])
    norm_mix_w = dt_in("norm_mix_w", [D])
    w_in = dt_in("w_in", [D, NIN])
    gla_w_gk2 = dt_in("gla_w_gk2", [16, 256])
    gla_b_gk = dt_in("gla_b_gk", [256])
    gla_norm_w = dt_in("gla_norm_w", [128])
    gdn_conv_w = dt_in("gdn_conv_w", [4, 1536])
    gdn_a_log = dt_in("gdn_a_log", [4])
    gdn_dt_bias = dt_in("gdn_dt_bias", [4])
    gdn_norm_w = dt_in("gdn_norm_w", [128])
    w_out = dt_in("w_out", [D, D])
    norm_ffn_w = dt_in("norm_ffn_w", [D])
    peer_wq = dt_in("peer_wq", [D, 2048])
    peer_k1 = dt_in("peer_k1", [8, 128, 128])
    peer_k2 = dt_in("peer_k2", [8, 128, 128])
    peer_u = dt_in("peer_u", [16384, D])
    peer_v = dt_in("peer_v", [16384, D])
    norm_final_w = dt_in("norm_final_w", [D])

    y_main = dt_out("y_main", [NMAIN * 128, D])
    y_s = dt_out("y_s", [NSAMP * 32, D])
    gla_st = dt_out("gla_st", [4, 64, 128])
    gdn_st = dt_out("gdn_st", [4, 128, 128])
    conv_st = dt_out("conv_st", [3, 1536])
    gla_s = dt_out("gla_s", [NSAMP, 4, 64, 128])
    gdn_s = dt_out("gdn_s", [NSAMP, 4, 128, 128])
    conv_s = dt_out("conv_s", [NSAMP, 3, 1536])
    dbg_outs = {}

    with ExitStack() as st:
        fw = FW(nc, st)
        V = lambda fn, r, w: fw.op("dve", fn, r, w)
        A = lambda fn, r, w: fw.op("act", fn, r, w)
        P = lambda fn, r, w: fw.op("pe", fn, r, w)
        GP = lambda fn, r, w: fw.op("pool", fn, r, w)

        cst = fw.sb("cstsb", [128, 7, 128], dma=True)
        fw.dma("sp", lambda e: e.dma_start(out=cst[:], in_=cst_in), writes=[cst])
        IDN, U1, LST, STRICT, TRIU, ONES, BLK = (cst[:, i, :] for i in range(7))

        def cs(i, r0, n, c0=None, m=None):
            c0 = r0 if c0 is None else c0
            m = n if m is None else m
            return cst[r0:r0 + n, i, c0:c0 + m]

        win = fw.sb("win", [128, 8, NIN], BF16)
        wi_v = w_in.rearrange("(k p) n -> p k n", p=128)
        wo_v = w_out.rearrange("(k p) n -> p k n", p=128)
        wq_v = peer_wq.rearrange("(k p) n -> p k n", p=128)
        nmwT = fw.sb("nmwT", [128, 8], dma=True)
        for k in range(0, 8, 2):
            fw.dma("sp", lambda e: e.dma_start(out=nmwT[:, k:k + 2], in_=norm_mix_w.rearrange("(k p) -> p k", p=128)[:, k:k + 2], allow_slow_non_contiguous=True), writes=[nmwT])
        nfw = fw.sb("nfw", [128, D], dma=True)
        nzw = fw.sb("nzw", [128, D], dma=True)
        for t_, src in ((nfw, norm_ffn_w), (nzw, norm_final_w)):
            fw.dma("sp", lambda e: e.dma_start(out=t_[:], in_=src.partition_broadcast(128)), writes=[t_])
        gnw1 = fw.sb("gnw1", [128, 128], dma=True)
        gnw2 = fw.sb("gnw2", [128, 128], dma=True)
        fw.dma("sp", lambda e: e.dma_start(out=gnw1[:], in_=gla_norm_w.partition_broadcast(128)), writes=[gnw1])
        fw.dma("sp", lambda e: e.dma_start(out=gnw2[:], in_=gdn_norm_w.partition_broadcast(128)), writes=[gnw2])
        wgk = fw.sb("wgk", [16, 256], dma=True)
        fw.dma("sp", lambda e: e.dma_start(out=wgk[:], in_=gla_w_gk2), writes=[wgk])
        bgk = fw.sb("bgk", [1, 256], dma=True)
        fw.dma("sp", lambda e: e.dma_start(out=bgk[:], in_=gla_b_gk.rearrange("(o n) -> o n", o=1)), writes=[bgk])
        wc = fw.sb("wc", [128, 12, 4], dma=True)
        for i in range(4):
            for b0_ in range(0, 12, 3):
                fw.dma("sp", lambda e: e.dma_start(out=wc[:, b0_:b0_ + 3, i:i + 1],
                                                   in_=gdn_conv_w[i:i + 1, :].rearrange("o (b p) -> p b o", p=128)[:, b0_:b0_ + 3, :],
                                                   allow_slow_non_contiguous=True), writes=[wc])
        dtb = fw.sb("dtb", [128, 4], dma=True)
        fw.dma("sp", lambda e: e.dma_start(out=dtb[:], in_=gdn_dt_bias.partition_broadcast(128)), writes=[dtb])
        alog = fw.sb("alog", [128, 4], dma=True)
        fw.dma("sp", lambda e: e.dma_start(out=alog[:], in_=gdn_a_log.partition_broadcast(128)), writes=[alog])
        negA = fw.sb("negA", [128, 4])
        A(lambda e: e.activation(out=negA[:], in_=alog[:], func=AF.Exp), [alog], [negA])
        V(lambda e: e.tensor_scalar(out=negA[:], in0=negA[:], scalar1=-1.0, scalar2=None, op0=ALU.mult), [negA], [negA])

        pbanks = [fw.ps("pb%d" % i, [128, 512]) for i in range(7)]
        pbf = fw.ps("pbf", [128, 8, 128], BF16)
        pctr = [0]

        def pbank():
            p = pbanks[pctr[0] % 5]
            pctr[0] += 1
            return p

        def v4(p, a=4, b=128, rows=slice(None)):
            return p[rows, 0:a * b].rearrange("p (a b) -> p a b", b=b)

        ARW = 17408
        arena = st.enter_context(nc.sbuf_tensor("arena", [128, ARW], F32))
        aoff = {"m": 0, "p": 0}

        def av(phase, name, shape, dt=F32, buf=None, dma=False):
            n = int(np.prod(shape[1:]))
            words = n if dt == F32 else n // 2
            o = aoff[phase]
            aoff[phase] += words
            assert aoff[phase] <= ARW, (phase, name, aoff[phase])
            ap = arena[:, o:o + words]
            if dt != F32:
                ap = ap.bitcast(dt)
            if len(shape) == 3:
                ap = ap.rearrange("p (a b) -> p a b", b=shape[2])
            return TL(ap, buf if buf is not None else fw.buf(name, dma))

        M = lambda name, shape, dt=F32: av("m", name, shape, dt)
        gqk = M("gqk", [128, 512]); gv = M("gv", [128, 512]); sgg = M("sgg", [128, 512]); sdz = M("sdz", [128, 512])
        a1 = M("a1", [128, 256]); eG = M("eG", [128, 256]); enG = M("enG", [128, 256]); edG = M("edG", [128, 256])
        qks = M("qks", [128, 512]); kh = M("kh", [128, 256]); qkT = M("qkT", [128, 4, 128]); attS = M("attS", [128, 4, 64])
        o12 = M("o12", [128, 8, 128]); cv = M("cv", [128, 12, 128]); cvt = M("cvt", [128, 12, 128]); rq = M("rq", [128, 8, 128])
        qT = M("qT", [128, 4, 128]); kTg = M("kTg", [128, 4, 128]); qgT = M("qgT", [128, 4, 128])
        ktok = M("ktok", [128, 4, 128]); vtok = M("vtok", [128, 4, 128])
        big = M("big", [128, 4, 128]); decS = M("decS", [128, 4, 128]); decT = M("decT", [128, 4, 128])
        Mc = M("Mc", [128, 4, 64]); MTc = M("MTc", [128, 4, 64]); Xc = M("Xc", [128, 4, 64])
        vb = M("vb", [128, 4, 128]); kbg = M("kbg", [128, 4, 128]); kd = M("kd", [128, 4, 128])
        nwT = M("nwT", [128, 4, 64]); vnew = M("vnew", [128, 4, 128]); ATc = M("ATc", [128, 4, 64])
        wq = av("p", "wq", [128, 8, 2048], BF16, dma=True)
        pqT = av("p", "pqT", [128, 16, 128]); s12 = av("p", "s12", [128, 16, 128])
        wob = fw.buf("wout_cand", dma=True)
        o_c = aoff["p"]
        cand = av("p", "cand", [128, 8, 256], buf=wob); cidx = av("p", "cidx", [128, 8, 256], buf=wob)
        wout = TL(arena[:, o_c:o_c + 4096].bitcast(BF16).rearrange("p (a b) -> p a b", b=D), wob)

        xt = fw.sb("xt", [128, D], dma=True)
        kT = fw.sb("kT", [128, 16, 128])
        junk = fw.sb("junk", [128, D], BF16)
        ss = fw.sb("ss", [128, 1])
        ssg = fw.sb("ssg", [128, 1])
        xn = fw.sb("xn", [128, D], BF16)
        hT = fw.sb("hT", [128, 8, 128], BF16)
        dab = fw.sb("dab", [128, 8])
        glrT = fw.sb("glrT", [16, 128])
        convin = fw.sb("convin", [128, 12, 131], dma=True)
        egl = fw.sb("egl", [128, 2, 2])
        mixb = xn
        mixT = hT
        g4 = fw.sb("g4", [128, 10, 4])
        a2blk = fw.sb("a2blk", [128, 4, 2])
        egl2 = fw.sb("egl2", [128, 4, 2])
        x2s = [fw.sb("x2_%d" % i, [128, D], dma=True) for i in range(2)]
        h2bs = [fw.sb("h2b_%d" % i, [128, D], BF16) for i in range(2)]
        eids = [fw.sb("eid_%d" % i, [128, 128], U32) for i in range(2)]
        gates = [fw.sb("gate_%d" % i, [128, 8, 16]) for i in range(2)]
        wk = fw.sb("wk", [128, 256])
        v12 = fw.sb("v12", [128, 16, 16])
        i12 = fw.sb("i12", [128, 16, 16], U32)
        i12f = fw.sb("i12f", [128, 16, 16])
        sc = fw.sb("sc", [128, 8, 16])
        eidf = fw.sb("eidf", [128, 128])
        zz = fw.sb("zz", [128, 8])
        actv = fw.sb("actv", [128, 128])
        wgt = fw.sb("wgt", [128, 128])
        tmpa = fw.sb("tmpa", [128, 128])
        NSLOT = 8 if dbg else 9
        ring = [fw.sb("ring%d" % i, [128, D], BF16, dma=True) for i in range(NSLOT)]
        rctr = [0]
        dg = [fw.sb("dg%d" % i, [128, 128], BF16) for i in range(4)]
        _sg = fw.sb("Sgla_p", [128, 2, 128], dma=True)
        _sd = fw.sb("Sgdn_p", [128, 4, 128], dma=True)
        Sgla = {k: _sg for k in ["p"] + ["s%d" % i for i in range(NSAMP)]}
        Sgdn = {k: _sd for k in ["p"] + ["s%d" % i for i in range(NSAMP)]}
        GP(lambda e: e.memset(Sgla["p"][:], 0.0), [], [Sgla["p"]])
        GP(lambda e: e.memset(Sgdn["p"][:], 0.0), [], [Sgdn["p"]])
        GP(lambda e: e.memset(convin[:], 0.0), [], [convin])
        wstg = [TL(arena[:, i * NIN:(i + 1) * NIN], fw.buf("wstg%d" % i, dma=True)) for i in range(2)]
        for k in range(8):
            ws = wstg[k % 2]
            fw.dma("sp", lambda e: e.dma_start(out=ws[:, :], in_=wi_v[:, k, :]), writes=[ws])
            V(lambda e: e.tensor_scalar(out=win[:, k, :], in0=ws[:, :], scalar1=nmwT[:, k:k + 1], scalar2=None, op0=ALU.mult), [ws, nmwT], [win])
        fw.barrier()
        wq_b = nc.dram_tensor("wq_b", [128, 8, 2048], BF16, kind="Internal").ap()
        wo_b = nc.dram_tensor("wo_b", [128, 8, D], BF16, kind="Internal").ap()
        wq_sw = TL(wq.t, fw.buf("wq_sw", dma=True))
        wout_sw = TL(wout.t, fw.buf("wout_sw", dma=True))
        scr_wq = fw.buf("scr_wq", dma=True)
        scr_wo = fw.buf("scr_wo", dma=True)
        for k in range(8):
            fw.dma("pool", lambda e: e.dma_start(out=wq_sw[:, k, :], in_=wq_v[:, k, :]), writes=[wq_sw])
        fw.dma("sp", lambda e: e.dma_start(out=wq_b, in_=wq_sw[:, :, :]), reads=[wq_sw], writes=[scr_wq], owner=scr_wq)
        for k in range(8):
            fw.dma("pool", lambda e: e.dma_start(out=wout_sw[:, k, :], in_=wo_v[:, k, :]), writes=[wout_sw])
        fw.dma("sp", lambda e: e.dma_start(out=wo_b, in_=wout_sw[:, :, :]), reads=[wout_sw], writes=[scr_wo], owner=scr_wo)
        fw.barrier()
        kraw = TL(arena[:, 0:2048].rearrange("p (a b) -> p a b", b=128), fw.buf("kraw", dma=True))
        fw.dma("sp", lambda e: e.dma_start(out=kraw[:, 0:8, :], in_=peer_k1.rearrange("h n d -> n h d")), writes=[kraw])
        fw.dma("sp", lambda e: e.dma_start(out=kraw[:, 8:16, :], in_=peer_k2.rearrange("h n d -> n h d")), writes=[kraw])
        for g in range(4):
            pk = pbank()
            for j in range(4):
                P(lambda e: e.transpose(out=pk[:, j * 128:(j + 1) * 128], in_=kraw[:, g * 4 + j, :], identity=IDN), [kraw, cst], [pk])
            V(lambda e: e.tensor_copy(out=kT[:, g * 4:(g + 1) * 4, :], in_=v4(pk)), [pk], [kT])

        fw.barrier()

        stg = fw.sb("stg", [128, D], BF16, dma=True) if dbg else None

        def dump(name, tl, ap, shape):
            if not dbg:
                return
            if tl in x2s:
                o = nc.dram_tensor("dbg_" + name, list(shape), F32, kind="ExternalOutput").ap()
                fw.dma("sp", lambda e: e.dma_start(out=o, in_=ap), reads=[tl])
                return
            o = nc.dram_tensor("dbg_" + name, list(shape), BF16, kind="ExternalOutput").ap()
            V(lambda e: e.tensor_copy(out=stg[:shape[0], :shape[1]], in_=ap), [tl], [stg])
            fw.dma("sp", lambda e: e.dma_start(out=o, in_=stg[:shape[0], :shape[1]]), reads=[stg])

        def rmsnorm_rs(src, T):
            A(lambda e: e.activation(out=junk[:T, :], in_=src[:T, :], func=AF.Square, accum_out=ss[:T, :]), [src], [junk, ss])
            A(lambda e: e.activation(out=ss[:T, :], in_=ss[:T, :], func=AF.Sqrt, scale=1.0 / D, bias=EPS), [ss], [ss])
            V(lambda e: e.reciprocal(out=ss[:T, :], in_=ss[:T, :]), [ss], [ss])

        def transpose_bf(src, dst, T):
            for k in range(8):
                P(lambda e: e.transpose(out=pbf[:, k, :T], in_=src[:T, k * 128:(k + 1) * 128], identity=idb[:T, :T]), [src, idb], [pbf])
            A(lambda e: e.copy(out=dst[:, :, :T], in_=pbf[:, :, :T]), [pbf], [dst])

        idb = fw.sb("idb", [128, 128], BF16)
        V(lambda e: e.tensor_copy(out=idb[:], in_=IDN), [cst], [idb])

        jscr = [fw.sb("jscr%d" % i, [128, 128], BF16) for i in range(2)]
        ssq2 = [fw.sb("ssq%d" % i, [128, 4]) for i in range(2)]
        o12h = [TL(o12.t[:, 0:4, :], fw.buf("o12a")), TL(o12.t[:, 4:8, :], fw.buf("o12b"))]

        def gated_norm(which, gw, sg, T, col0):
            oo, sq_, js = o12h[which], ssq2[which], jscr[which]
            for h in range(4):
                A(lambda e: e.activation(out=js[:T, :], in_=oo[:T, h, :], func=AF.Square, accum_out=sq_[:T, h:h + 1]), [oo], [js, sq_])
            A(lambda e: e.activation(out=sq_[:T, :], in_=sq_[:T, :], func=AF.Sqrt, scale=1.0 / 128, bias=EPS), [sq_], [sq_])
            V(lambda e: e.reciprocal(out=sq_[:T, :], in_=sq_[:T, :]), [sq_], [sq_])
            mv = oo[:T, :, :]
            V(lambda e: e.tensor_tensor(out=mv, in0=mv, in1=sq_[:T, :].unsqueeze(2).to_broadcast([T, 4, 128]), op=ALU.mult), [oo, sq_], [oo])
            V(lambda e: e.tensor_tensor(out=mv, in0=mv, in1=gw[:T, :].unsqueeze(1).to_broadcast([T, 4, 128]), op=ALU.mult), [oo, gw], [oo])
            V(lambda e: e.tensor_tensor(out=mixb[:T, col0:col0 + 512].rearrange("p (a b) -> p a b", b=128), in0=mv,
                                        in1=sg[:T, :].rearrange("p (a b) -> p a b", b=128), op=ALU.mult), [oo, sg], [mixb])

        McC = [TL(Mc.t, fw.buf("Mc%d" % c)) for c in range(2)]
        MTcC = [TL(MTc.t, fw.buf("MTc%d" % c)) for c in range(2)]
        XcC = [TL(Xc.t, fw.buf("Xc%d" % c)) for c in range(2)]

        tile_ctr = [0]

        def do_tile(xsrc, T, C, nch, full, sk, par=0, dname=None, need_q=True):
            tile_ctr[0] += 1
            SG, SD = Sgla[sk], Sgdn[sk]
            x2, h2b, eid, gate = x2s[par], h2bs[par], eids[par], gates[par]
            fw.dma("sp", lambda e: e.dma_start(out=xt[:T, :], in_=xsrc), writes=[xt])
            rmsnorm_rs(xt, T)
            V(lambda e: e.tensor_scalar(out=xn[:T, :], in0=xt[:T, :], scalar1=ss[:T, 0:1], scalar2=None, op0=ALU.mult), [xt, ss], [xn])
            transpose_bf(xn, hT, T)

            def proj_tok(c0, ncols, evac):
                pp = pbank()
                for k in range(8):
                    P(lambda e: e.matmul(pp[:T, 0:ncols], lhsT=hT[:, k, :T], rhs=win[:, k, c0:c0 + ncols], start=(k == 0), stop=(k == 7)), [hT, win], [pp])
                evac(pp)
            if full:
                proj_tok(O_GQ, 512, lambda pp: V(lambda e: e.tensor_copy(out=gqk[:T, :], in_=pp[:T, :]), [pp], [gqk]))
            else:
                proj_tok(O_GK, 256, lambda pp: V(lambda e: e.tensor_copy(out=gqk[:T, 256:512], in_=pp[:T, 0:256]), [pp], [gqk]))
            proj_tok(O_GV, 512, lambda pp: A(lambda e: e.copy(out=gv[:T, :], in_=pp[:T, :]), [pp], [gv]))
            proj_tok(O_DA, 8, lambda pp: V(lambda e: e.tensor_copy(out=dab[:T, :], in_=pp[:T, 0:8]), [pp], [dab]))
            if full:
                proj_tok(O_GG, 512, lambda pp: A(lambda e: e.activation(out=sgg[:T, :], in_=pp[:T, :], func=AF.Silu), [pp], [sgg]))
                proj_tok(O_DZ, 512, lambda pp: A(lambda e: e.activation(out=sdz[:T, :], in_=pp[:T, :], func=AF.Silu), [pp], [sdz]))
            pp = pbank()
            for k in range(8):
                P(lambda e: e.matmul(pp[0:16, :T], lhsT=win[:, k, O_GLR:O_GLR + 16], rhs=hT[:, k, :T], start=(k == 0), stop=(k == 7)), [hT, win], [pp])
            V(lambda e: e.tensor_copy(out=glrT[:, :T], in_=pp[0:16, :T]), [pp], [glrT])
            for g in range(0 if need_q else 1, 3):
                pp = pbank()
                for j in range(4):
                    blk = g * 4 + j
                    for k in range(8):
                        P(lambda e: e.matmul(pp[:, j * 128:j * 128 + T], lhsT=win[:, k, O_DQKV + blk * 128:O_DQKV + (blk + 1) * 128], rhs=hT[:, k, :T],
                                             start=(k == 0), stop=(k == 7)), [hT, win], [pp])
                (A if g % 2 else V)(lambda e: (e.copy if g % 2 else e.tensor_copy)(convin[:, g * 4:(g + 1) * 4, 3:3 + T], v4(pp)[:, :, :T]), [pp], [convin])

            def gla_chain():
                pp = pbank()
                P(lambda e: e.matmul(pp[:T, 0:256], lhsT=glrT[:, :T], rhs=wgk[:], start=True, stop=False), [glrT, wgk], [pp])
                P(lambda e: e.matmul(pp[:T, 0:256], lhsT=cst[0:1, 5, :T], rhs=bgk[:], start=False, stop=True), [cst, bgk], [pp])
                A(lambda e: e.activation(out=a1[:T, :], in_=pp[:T, 0:256], func=AF.Exp, scale=-1.0), [pp], [a1])
                A(lambda e: e.activation(out=a1[:T, :], in_=a1[:T, :], func=AF.Ln, bias=1.0), [a1], [a1])
                V(lambda e: e.tensor_scalar(out=a1[:T, :], in0=a1[:T, :], scalar1=-1.0 / 16, scalar2=None, op0=ALU.mult), [a1], [a1])
                pp = pbank()
                P(lambda e: e.matmul(pp[:T, 0:256], lhsT=cs(1, 0, T), rhs=a1[:T, :], start=True, stop=True), [cst, a1], [pp])
                P(lambda e: e.matmul(pp[:T, 256:512], lhsT=cs(2, 0, T), rhs=a1[:T, :], start=True, stop=True), [cst, a1], [pp])
                if full:
                    A(lambda e: e.activation(out=eG[:T, :], in_=pp[:T, 0:256], func=AF.Exp), [pp], [eG])
                A(lambda e: e.activation(out=enG[:T, :], in_=pp[:T, 0:256], func=AF.Exp, scale=-1.0), [pp], [enG])
                A(lambda e: e.activation(out=edG[:T, :], in_=pp[:T, 256:512], func=AF.Exp), [pp], [edG])
                if full:
                    V(lambda e: e.scalar_tensor_tensor(out=qks[:T, 0:256], in0=gqk[:T, 0:256], scalar=0.125, in1=eG[:T, :], op0=ALU.mult, op1=ALU.mult), [gqk, eG], [qks])
                    V(lambda e: e.tensor_tensor(out=qks[:T, 256:512], in0=gqk[:T, 256:512], in1=enG[:T, :], op=ALU.mult), [gqk, enG], [qks])
                V(lambda e: e.tensor_tensor(out=kh[:T, :], in0=gqk[:T, 256:512], in1=edG[:T, :], op=ALU.mult), [gqk, edG], [kh])
                pp = pbank()
                for p_ in range(2):
                    P(lambda e: e.matmul(pp[:, p_ * 2:p_ * 2 + nch], lhsT=a1[:T, p_ * 128:(p_ + 1) * 128], rhs=cst[:T, 6, 0:nch], start=True, stop=True), [a1, cst], [pp])
                A(lambda e: e.activation(out=egl[:, :, 0:nch], in_=pp[:, 0:4].rearrange("p (a b) -> p a b", b=2)[:, :, 0:nch], func=AF.Exp), [pp], [egl])
                if full:
                    pp = pbank()
                    for j in range(4):
                        P(lambda e: e.transpose(out=pp[:, j * 128:j * 128 + T], in_=qks[:T, j * 128:(j + 1) * 128], identity=cs(0, 0, T)), [qks, cst], [pp])
                    V(lambda e: e.tensor_copy(out=qkT[:, :, :T], in_=v4(pp)[:, :, :T]), [pp], [qkT])
                for c in range(nch):
                    r0 = c * C
                    rs_ = slice(r0, r0 + C)
                    if full:
                        pa = pbank()
                        for h in range(4):
                            hs = slice((h % 2) * 64, (h % 2) * 64 + 64)
                            P(lambda e: e.matmul(pa[rs_, h * 64:h * 64 + C], lhsT=qkT[hs, 2 + h // 2, rs_], rhs=qkT[hs, h // 2, rs_], start=True, stop=True), [qkT], [pa])
                        V(lambda e: e.tensor_tensor(out=attS[rs_, :, :C], in0=v4(pa, 4, 64, rs_)[:, :, :C],
                                                    in1=cs(4, r0, C).unsqueeze(1).to_broadcast([C, 4, C]), op=ALU.mult), [pa, cst], [attS])
                        po = pbank()
                        for h in range(4):
                            hs = slice((h % 2) * 64, (h % 2) * 64 + 64)
                            P(lambda e: e.matmul(po[rs_, h * 128:(h + 1) * 128], lhsT=qkT[hs, h // 2, rs_], rhs=SG[hs, h // 2, :], start=True, stop=False), [qkT, SG], [po])
                            P(lambda e: e.matmul(po[rs_, h * 128:(h + 1) * 128], lhsT=attS[rs_, h, :C], rhs=gv[rs_, h * 128:(h + 1) * 128], start=False, stop=True), [attS, gv], [po])
                        A(lambda e: e.copy(out=o12h[0][rs_, :, :], in_=v4(po, rows=rs_)), [po], [o12h[0]])
                    pst = pbank()
                    for h in range(4):
                        hs = slice((h % 2) * 64, (h % 2) * 64 + 64)
                        P(lambda e: e.matmul(pst[hs, (h // 2) * 128:(h // 2 + 1) * 128], lhsT=kh[rs_, h * 64:(h + 1) * 64], rhs=gv[rs_, h * 128:(h + 1) * 128], start=True, stop=True), [kh, gv], [pst])
                    for p_ in range(2):
                        V(lambda e: e.scalar_tensor_tensor(out=SG[:, p_, :], in0=SG[:, p_, :], scalar=egl[:, p_, c:c + 1], in1=pst[:, p_ * 128:(p_ + 1) * 128],
                                                           op0=ALU.mult, op1=ALU.add), [SG, egl, pst], [SG])
                if full:
                    gated_norm(0, gnw1, sgg, T, 0)

            def gdn_chain():
                b0 = 0 if full else 4
                nb = 12 - b0
                V(lambda e: e.tensor_tensor(out=cv[:, b0:, :T], in0=convin[:, b0:, 0:T], in1=wc[:, b0:, 0:1].to_broadcast([128, nb, T]), op=ALU.mult), [convin, wc], [cv])
                for i in range(1, 4):
                    V(lambda e: e.tensor_tensor(out=cvt[:, b0:, :T], in0=convin[:, b0:, i:i + T], in1=wc[:, b0:, i:i + 1].to_broadcast([128, nb, T]), op=ALU.mult), [convin, wc], [cvt])
                    V(lambda e: e.tensor_tensor(out=cv[:, b0:, :T], in0=cv[:, b0:, :T], in1=cvt[:, b0:, :T], op=ALU.add), [cv, cvt], [cv])
                A(lambda e: e.copy(out=cvt[:, :, 0:3], in_=convin[:, :, T:T + 3]), [convin], [cvt])
                A(lambda e: e.copy(out=convin[:, :, 0:3], in_=cvt[:, :, 0:3]), [cvt], [convin])
                A(lambda e: e.activation(out=cv[:, b0:, :T], in_=cv[:, b0:, :T], func=AF.Silu), [cv], [cv])
                if dname:
                    dump(dname + "_convin", convin, convin[:, 0, 3:3 + T], [128, T])
                    dump(dname + "_cv", cv, cv[:, 4, :T], [128, T])
                V(lambda e: e.tensor_tensor(out=cvt[:, b0:8, :T], in0=cv[:, b0:8, :T], in1=cv[:, b0:8, :T], op=ALU.mult), [cv], [cvt])
                for g in range(0 if full else 1, 2):
                    pp = pbank()
                    for j in range(4):
                        P(lambda e: e.matmul(pp[:, j * 128:j * 128 + T], lhsT=ONES, rhs=cvt[:, g * 4 + j, :T], start=True, stop=True), [cst, cvt], [pp])
                    A(lambda e: e.activation(out=rq[:, g * 4:(g + 1) * 4, :T], in_=v4(pp)[:, :, :T], func=AF.Sqrt, bias=EPS), [pp], [rq])
                V(lambda e: e.reciprocal(out=rq[:, b0:8, :T], in_=rq[:, b0:8, :T]), [rq], [rq])
                if full:
                    V(lambda e: e.scalar_tensor_tensor(out=qT[:, :, :T], in0=cv[:, 0:4, :T], scalar=128.0 ** -0.5, in1=rq[:, 0:4, :T], op0=ALU.mult, op1=ALU.mult), [cv, rq], [qT])
                V(lambda e: e.tensor_tensor(out=kTg[:, :, :T], in0=cv[:, 4:8, :T], in1=rq[:, 4:8, :T], op=ALU.mult), [cv, rq], [kTg])
                pp = pbank()
                for h in range(4):
                    P(lambda e: e.transpose(out=pp[:T, h * 128:(h + 1) * 128], in_=kTg[:, h, :T], identity=IDN), [kTg, cst], [pp])
                A(lambda e: e.copy(out=ktok[:T], in_=v4(pp, rows=slice(0, T))), [pp], [ktok])
                pp = pbank()
                for h in range(4):
                    P(lambda e: e.transpose(out=pp[:T, h * 128:(h + 1) * 128], in_=cv[:, 8 + h, :T], identity=IDN), [cv, cst], [pp])
                A(lambda e: e.copy(out=vtok[:T], in_=v4(pp, rows=slice(0, T))), [pp], [vtok])
                if dname:
                    dump(dname + "_kTg", kTg, kTg[:, 0, :T], [128, T])
                    dump(dname + "_ktok", ktok, ktok[:T, 0, :], [T, 128])
                V(lambda e: e.tensor_tensor(out=g4[:T, 0, :], in0=dab[:T, 0:4], in1=dtb[:T, :], op=ALU.add), [dab, dtb], [g4])
                A(lambda e: e.activation(out=g4[:T, 0, :], in_=g4[:T, 0, :], func=AF.Exp), [g4], [g4])
                A(lambda e: e.activation(out=g4[:T, 0, :], in_=g4[:T, 0, :], func=AF.Ln, bias=1.0), [g4], [g4])
                V(lambda e: e.tensor_tensor(out=g4[:T, 1, :], in0=g4[:T, 0, :], in1=negA[:T, :], op=ALU.mult), [g4, negA], [g4])
                A(lambda e: e.activation(out=g4[:T, 2, :], in_=dab[:T, 4:8], func=AF.Sigmoid), [dab], [g4])
                V(lambda e: e.tensor_scalar(out=g4[:T, 3, :], in0=g4[:T, 2, :], scalar1=-1.0, scalar2=None, op0=ALU.mult), [g4], [g4])
                pp = pbank()
                P(lambda e: e.matmul(pp[:T, 0:4], lhsT=cs(1, 0, T), rhs=g4[:T, 1, :], start=True, stop=True), [cst, g4], [pp])
                P(lambda e: e.matmul(pp[:T, 4:8], lhsT=cs(2, 0, T), rhs=g4[:T, 1, :], start=True, stop=True), [cst, g4], [pp])
                A(lambda e: e.activation(out=g4[:T, 4:6, :], in_=pp[:T, 0:8].rearrange("p (a b) -> p a b", b=4), func=AF.Exp), [pp], [g4])
                V(lambda e: e.tensor_tensor(out=g4[:T, 6, :], in0=g4[:T, 2, :], in1=g4[:T, 4, :], op=ALU.mult), [g4], [g4])
                V(lambda e: e.tensor_tensor(out=a2blk[:T, :, 0:nch], in0=g4[:T, 1, :].unsqueeze(2).to_broadcast([T, 4, nch]),
                                            in1=cst[:T, 6, 0:nch].unsqueeze(1).to_broadcast([T, 4, nch]), op=ALU.mult), [g4, cst], [a2blk])
                pp = pbank()
                for h in range(4):
                    P(lambda e: e.matmul(pp[:, h * 2:h * 2 + nch], lhsT=cst[:T, 5, :], rhs=a2blk[:T, h, 0:nch], start=True, stop=True), [cst, a2blk], [pp])
                A(lambda e: e.activation(out=egl2[:, :, 0:nch], in_=pp[:, 0:8].rearrange("p (a b) -> p a b", b=2)[:, :, 0:nch], func=AF.Exp), [pp], [egl2])
                if full:
                    V(lambda e: e.tensor_tensor(out=big[:T, :, :T], in0=cs(0, 0, T).unsqueeze(1).to_broadcast([T, 4, T]),
                                                in1=g4[:T, 4, :].unsqueeze(2).to_broadcast([T, 4, T]), op=ALU.mult), [cst, g4], [big])
                    pp = pbank()
                    for h in range(4):
                        P(lambda e: e.matmul(pp[:, h * 128:h * 128 + T], lhsT=cst[:T, 5, :], rhs=big[:T, h, :T], start=True, stop=True), [cst, big], [pp])
                    V(lambda e: e.tensor_tensor(out=qgT[:, :, :T], in0=qT[:, :, :T], in1=v4(pp)[:, :, :T], op=ALU.mult), [qT, pp], [qgT])
                V(lambda e: e.tensor_tensor(out=big[:T, :, :T], in0=cs(2, 0, T).unsqueeze(1).to_broadcast([T, 4, T]),
                                            in1=g4[:T, 1, :].unsqueeze(2).to_broadcast([T, 4, T]), op=ALU.mult), [cst, g4], [big])
                pp = pbank()
                for h in range(4):
                    P(lambda e: e.matmul(pp[:T, h * 128:h * 128 + T], lhsT=cs(1, 0, T), rhs=big[:T, h, :T], start=True, stop=True), [cst, big], [pp])
                A(lambda e: e.activation(out=decS[:T, :, :T], in_=v4(pp, rows=slice(0, T))[:, :, :T], func=AF.Exp), [pp], [decS])
                V(lambda e: e.tensor_tensor(out=decS[:T, :, :T], in0=decS[:T, :, :T], in1=cs(3, 0, T).unsqueeze(1).to_broadcast([T, 4, T]), op=ALU.mult), [decS, cst], [decS])
                if full:
                    V(lambda e: e.tensor_tensor(out=big[:T, :, :T], in0=cs(1, 0, T).unsqueeze(1).to_broadcast([T, 4, T]),
                                                in1=g4[:T, 1, :].unsqueeze(2).to_broadcast([T, 4, T]), op=ALU.mult), [cst, g4], [big])
                    pp = pbank()
                    for h in range(4):
                        P(lambda e: e.matmul(pp[:T, h * 128:h * 128 + T], lhsT=cs(2, 0, T), rhs=big[:T, h, :T], start=True, stop=True), [cst, big], [pp])
                    A(lambda e: e.activation(out=decT[:T, :, :T], in_=v4(pp, rows=slice(0, T))[:, :, :T], func=AF.Exp), [pp], [decT])
                    V(lambda e: e.tensor_tensor(out=decT[:T, :, :T], in0=decT[:T, :, :T], in1=cs(4, 0, T).unsqueeze(1).to_broadcast([T, 4, T]), op=ALU.mult), [decT, cst], [decT])
                if dname:
                    dump(dname + "_g4", g4, g4[:T, 0:7, :].rearrange("p a b -> p (a b)"), [T, 28])
                    dump(dname + "_decS", decS, decS[:T, 0, :T], [T, T])
                bc4 = lambda row: g4[:T, row, :].unsqueeze(2).to_broadcast([T, 4, 128])
                V(lambda e: e.tensor_tensor(out=vb[:T], in0=vtok[:T], in1=bc4(2), op=ALU.mult), [vtok, g4], [vb])
                V(lambda e: e.tensor_tensor(out=kbg[:T], in0=ktok[:T], in1=bc4(6), op=ALU.mult), [ktok, g4], [kbg])
                V(lambda e: e.tensor_tensor(out=kd[:T], in0=ktok[:T], in1=bc4(5), op=ALU.mult), [ktok, g4], [kd])
                nlev = {64: 5, 32: 4}[C]

                def solve_chunk(c):
                    r0 = c * C
                    rs_ = slice(r0, r0 + C)
                    Mc, MTc, Xc = McC[c], MTcC[c], XcC[c]
                    pk = pbank()
                    for h in range(4):
                        P(lambda e: e.matmul(pk[rs_, h * 64:h * 64 + C], lhsT=kTg[:, h, rs_], rhs=kTg[:, h, rs_], start=True, stop=True), [kTg], [pk])
                    V(lambda e: e.tensor_tensor(out=Mc[rs_, :, :C], in0=v4(pk, 4, 64, rs_)[:, :, :C], in1=decS[rs_, :, rs_], op=ALU.mult), [pk, decS], [Mc])
                    V(lambda e: e.tensor_tensor(out=Mc[rs_, :, :C], in0=Mc[rs_, :, :C], in1=g4[rs_, 3, :].unsqueeze(2).to_broadcast([C, 4, C]), op=ALU.mult), [Mc, g4], [Mc])
                    pt = pbank()
                    for h in range(4):
                        P(lambda e: e.matmul(pt[rs_, h * 64:h * 64 + C], lhsT=Mc[rs_, h, :C], rhs=cs(0, r0, C), start=True, stop=True), [Mc, cst], [pt])
                    A(lambda e: e.copy(out=MTc[rs_, :, :C], in_=v4(pt, 4, 64, rs_)[:, :, :C]), [pt], [MTc])
                    V(lambda e: e.tensor_tensor(out=Xc[rs_, :, :C], in0=MTc[rs_, :, :C], in1=cs(0, r0, C).unsqueeze(1).to_broadcast([C, 4, C]), op=ALU.add), [MTc, cst], [Xc])
                    for lev in range(nlev):
                        last = lev == nlev - 1
                        p1 = pbank()
                        for h in range(4):
                            P(lambda e: e.matmul(p1[rs_, h * 64:h * 64 + C], lhsT=MTc[rs_, h, :C], rhs=Mc[rs_, h, :C], start=True, stop=True), [MTc, Mc], [p1])
                        if not last:
                            p2 = pbank()
                            for h in range(4):
                                P(lambda e: e.matmul(p2[rs_, h * 64:h * 64 + C], lhsT=Mc[rs_, h, :C], rhs=MTc[rs_, h, :C], start=True, stop=True), [MTc, Mc], [p2])
                        A(lambda e: e.copy(out=Mc[rs_, :, :C], in_=v4(p1, 4, 64, rs_)[:, :, :C]), [p1], [Mc])
                        if not last:
                            V(lambda e: e.tensor_copy(out=MTc[rs_, :, :C], in_=v4(p2, 4, 64, rs_)[:, :, :C]), [p2], [MTc])
                        p3 = pbank()
                        for h in range(4):
                            P(lambda e: e.matmul(p3[rs_, h * 64:h * 64 + C], lhsT=Mc[rs_, h, :C], rhs=Xc[rs_, h, :C], start=True, stop=True), [Mc, Xc], [p3])
                        V(lambda e: e.tensor_tensor(out=Xc[rs_, :, :C], in0=Xc[rs_, :, :C], in1=v4(p3, 4, 64, rs_)[:, :, :C], op=ALU.add), [Xc, p3], [Xc])

                if nch == 2:
                    fw.co.fork([lambda: solve_chunk(0), lambda: solve_chunk(1)], [4, 4])
                else:
                    solve_chunk(0)
                for c in range(nch):
                    r0 = c * C
                    rs_ = slice(r0, r0 + C)
                    Xc = XcC[c]
                    pw = pbank()
                    for h in range(4):
                        P(lambda e: e.matmul(pw[:, h * 64:h * 64 + C], lhsT=kbg[rs_, h, :], rhs=Xc[rs_, h, :C], start=True, stop=True), [kbg, Xc], [pw])
                    A(lambda e: e.mul(out=nwT[:, :, :C], in_=v4(pw, 4, 64)[:, :, :C], mul=-1.0), [pw], [nwT])
                    pv = pbank()
                    for h in range(4):
                        P(lambda e: e.matmul(pv[rs_, h * 128:(h + 1) * 128], lhsT=Xc[rs_, h, :C], rhs=vb[rs_, h, :], start=True, stop=False), [Xc, vb], [pv])
                        P(lambda e: e.matmul(pv[rs_, h * 128:(h + 1) * 128], lhsT=nwT[:, h, :C], rhs=SD[:, h, :], start=False, stop=True), [nwT, SD], [pv])
                    A(lambda e: e.copy(out=vnew[rs_], in_=v4(pv, rows=rs_)), [pv], [vnew])
                    if full:
                        pq = pbank()
                        for h in range(4):
                            P(lambda e: e.matmul(pq[rs_, h * 64:h * 64 + C], lhsT=kTg[:, h, rs_], rhs=qT[:, h, rs_], start=True, stop=True), [kTg, qT], [pq])
                        V(lambda e: e.tensor_tensor(out=ATc[rs_, :, :C], in0=v4(pq, 4, 64, rs_)[:, :, :C], in1=decT[rs_, :, rs_], op=ALU.mult), [pq, decT], [ATc])
                        po2 = pbank()
                        for h in range(4):
                            P(lambda e: e.matmul(po2[rs_, h * 128:(h + 1) * 128], lhsT=qgT[:, h, rs_], rhs=SD[:, h, :], start=True, stop=False), [qgT, SD], [po2])
                            P(lambda e: e.matmul(po2[rs_, h * 128:(h + 1) * 128], lhsT=ATc[rs_, h, :C], rhs=vnew[rs_, h, :], start=False, stop=True), [ATc, vnew], [po2])
                        A(lambda e: e.copy(out=o12h[1][rs_, :, :], in_=v4(po2, rows=rs_)), [po2], [o12h[1]])
                    ps2 = pbank()
                    for h in range(4):
                        P(lambda e: e.matmul(ps2[:, h * 128:(h + 1) * 128], lhsT=kd[rs_, h, :], rhs=vnew[rs_, h, :], start=True, stop=True), [kd, vnew], [ps2])
                    V(lambda e: e.tensor_tensor(out=SD[:], in0=SD[:], in1=egl2[:, :, c:c + 1].to_broadcast([128, 4, 128]), op=ALU.mult), [SD, egl2], [SD])
                    V(lambda e: e.tensor_tensor(out=SD[:], in0=SD[:], in1=v4(ps2), op=ALU.add), [SD, ps2], [SD])
                if not full:
                    return
                gated_norm(1, gnw2, sdz, T, 512)

            fw.co.fork([gla_chain, gdn_chain], [3, 6])
            if True:
                if not full:
                    return
                if dname:
                    dump(dname + "_mix", mixb, mixb[:T, :], [T, D])

            fw.barrier()
            fw.dma("sp", lambda e: e.dma_start(out=wout[:, :, :], in_=wo_b), reads=[scr_wo], writes=[wout], owner=wout)
            for k in range(0, 8, 2):
                fw.dma("sp", lambda e: e.dma_start(out=wq[:, k:k + 2, :], in_=wq_b[:, k:k + 2, :]), reads=[scr_wq], writes=[wq], owner=wq)
            transpose_bf(mixb, mixT, T)
            for n in range(2):
                pp = pbank()
                for k in range(8):
                    P(lambda e: e.matmul(pp[:T, :], lhsT=mixT[:, k, :T], rhs=wout[:, k, n * 512:(n + 1) * 512], start=(k == 0), stop=(k == 7)), [mixT, wout], [pp])
                V(lambda e: e.tensor_tensor(out=x2[:T, n * 512:(n + 1) * 512], in0=xt[:T, n * 512:(n + 1) * 512], in1=pp[:T, :], op=ALU.add), [xt, pp], [x2])
            if dname:
                dump(dname + "_x2", x2, x2[:T, :], [T, D])

            if limit < 3:
                fw.barrier()
                return
            rmsnorm_rs(x2, T)
            V(lambda e: e.scalar_tensor_tensor(out=h2b[:T, :], in0=x2[:T, :], scalar=ss[:T, 0:1], in1=nfw[:T, :], op0=ALU.mult, op1=ALU.mult), [x2, ss, nfw], [h2b])
            transpose_bf(h2b, hT, T)
            for g in range(4):
                pp = pbank()
                for j in range(4):
                    cb = g * 4 + j
                    for k in range(8):
                        P(lambda e: e.matmul(pp[:, j * 128:j * 128 + T], lhsT=wq[:, k, cb * 128:(cb + 1) * 128], rhs=hT[:, k, :T], start=(k == 0), stop=(k == 7)), [wq, hT], [pp])
                (A if g % 2 else V)(lambda e: (e.copy if g % 2 else e.tensor_copy)(pqT[:, g * 4:(g + 1) * 4, :T], v4(pp)[:, :, :T]), [pp], [pqT])
            for half in range(2):
                for g in range(2):
                    pp = pbank()
                    for j in range(4):
                        h = g * 4 + j
                        P(lambda e: e.matmul(pp[:T, j * 128:(j + 1) * 128], lhsT=pqT[:, 2 * h + half, :T], rhs=kT[:, half * 8 + h, :], start=True, stop=True), [pqT, kT], [pp])
                    (A if g % 2 else V)(lambda e: (e.copy if g % 2 else e.tensor_copy)(s12[:T, half * 8 + g * 4:half * 8 + g * 4 + 4, :], v4(pp, rows=slice(0, T))), [pp], [s12])
            for s in range(16):
                V(lambda e: e.max(out=v12[:T, s, 0:8], in_=s12[:T, s, :]), [s12], [v12])
                V(lambda e: e.max_index(out=i12[:T, s, 0:8], in_max=v12[:T, s, 0:8], in_values=s12[:T, s, :]), [s12, v12], [i12])
                V(lambda e: e.match_replace(out=wk[:T, 0:128], in_to_replace=v12[:T, s, 0:8], in_values=s12[:T, s, :], imm_value=NEG), [s12, v12], [wk])
                V(lambda e: e.max(out=v12[:T, s, 8:16], in_=wk[:T, 0:128]), [wk], [v12])
                V(lambda e: e.max_index(out=i12[:T, s, 8:16], in_max=v12[:T, s, 8:16], in_values=wk[:T, 0:128]), [wk, v12], [i12])
            V(lambda e: e.tensor_copy(out=i12f[:T], in_=i12[:T]), [i12], [i12f])
            V(lambda e: e.tensor_scalar(out=i12f[:T, 0:8, :], in0=i12f[:T, 0:8, :], scalar1=128.0, scalar2=None, op0=ALU.mult), [i12f], [i12f])
            c4 = lambda t_: t_[:T].rearrange("p h (a b) -> p h a b", b=16)
            b1 = lambda t_: t_[:T, 0:8, :].unsqueeze(3).to_broadcast([T, 8, 16, 16])
            b2 = lambda t_: t_[:T, 8:16, :].unsqueeze(2).to_broadcast([T, 8, 16, 16])
            V(lambda e: e.tensor_tensor(out=c4(cand), in0=b1(v12), in1=b2(v12), op=ALU.add), [v12], [cand])
            pos = i12[:T, 0:8, :]
            pau = i12[:T, 8:16, :]
            for h in range(8):
                V(lambda e: e.max(out=sc[:T, h, 0:8], in_=cand[:T, h, :]), [cand], [sc])
                V(lambda e: e.max_index(out=i12[:T, h, 0:8], in_max=sc[:T, h, 0:8], in_values=cand[:T, h, :]), [cand, sc], [i12])
                V(lambda e: e.match_replace(out=wk[:T, :], in_to_replace=sc[:T, h, 0:8], in_values=cand[:T, h, :], imm_value=NEG), [cand, sc], [wk])
                V(lambda e: e.max(out=sc[:T, h, 8:16], in_=wk[:T, :]), [wk], [sc])
                V(lambda e: e.max_index(out=i12[:T, h, 8:16], in_max=sc[:T, h, 8:16], in_values=wk[:T, :]), [wk, sc], [i12])
            V(lambda e: e.tensor_single_scalar(out=pau, in_=pos, scalar=4, op=ALU.logical_shift_right), [i12], [i12])
            V(lambda e: e.tensor_single_scalar(out=pos, in_=pos, scalar=15, op=ALU.bitwise_and), [i12], [i12])
            V(lambda e: e.tensor_copy(out=v12[:T, 0:8, :], in_=pau), [i12], [v12])
            V(lambda e: e.tensor_copy(out=v12[:T, 8:16, :], in_=pos), [i12], [v12])
            iota4 = cst[:T, 6, 16:32].unsqueeze(1).unsqueeze(1).to_broadcast([T, 8, 16, 16])
            sel = [eidf[:T, :].rearrange("p (a b) -> p a b", b=16), wk[:T, 0:128].rearrange("p (a b) -> p a b", b=16)]
            for w_ in range(2):
                V(lambda e: e.tensor_tensor(out=c4(cidx), in0=v12[:T, w_ * 8:(w_ + 1) * 8, :].unsqueeze(3).to_broadcast([T, 8, 16, 16]), in1=iota4, op=ALU.is_equal), [v12, cst], [cidx])
                V(lambda e: e.tensor_tensor(out=c4(cidx), in0=c4(cidx), in1=i12f[:T, w_ * 8:(w_ + 1) * 8, :].unsqueeze(2).to_broadcast([T, 8, 16, 16]), op=ALU.mult), [cidx, i12f], [cidx])
                V(lambda e: e.tensor_reduce(out=sel[w_], in_=c4(cidx), axis=AX.X, op=ALU.add), [cidx], [eidf if w_ == 0 else wk])
            V(lambda e: e.tensor_tensor(out=eidf[:T, :], in0=eidf[:T, :], in1=wk[:T, 0:128], op=ALU.add), [eidf, wk], [eidf])
            V(lambda e: e.tensor_scalar(out=eidf[:T, :], in0=eidf[:T, :], scalar1=16383.0, scalar2=None, op0=ALU.min), [eidf], [eidf])
            V(lambda e: e.tensor_copy(out=eid[:T, :], in_=eidf[:T, :]), [eidf], [eid])
            V(lambda e: e.tensor_tensor(out=gate[:T], in0=sc[:T], in1=sc[:T, :, 0:1].to_broadcast([T, 8, 16]), op=ALU.subtract), [sc], [gate])
            A(lambda e: e.activation(out=gate[:T], in_=gate[:T], func=AF.Exp), [gate], [gate])
            V(lambda e: e.tensor_reduce(out=zz[:T, :], in_=gate[:T], axis=AX.X, op=ALU.add), [gate], [zz])
            V(lambda e: e.reciprocal(out=zz[:T, :], in_=zz[:T, :]), [zz], [zz])
            V(lambda e: e.tensor_tensor(out=gate[:T], in0=gate[:T], in1=zz[:T, :].unsqueeze(2).to_broadcast([T, 8, 16]), op=ALU.mult), [gate, zz], [gate])
            fw.barrier()

        def tile_G(par, T, ydst):
            x2, h2b, eid, gate = x2s[par], h2bs[par], eids[par], gates[par]
            LA = NSLOT - 1
            slot = {}

            def gather(i, table):
                r_ = ring[rctr[0] % NSLOT]
                rctr[0] += 1
                slot[i] = r_
                fw.dma("pool", lambda e: e.indirect_dma_start(out=r_[:T, :], out_offset=None, in_=table,
                                                              in_offset=IndirectOffsetOnAxis(ap=eid[:T, i:i + 1], axis=0)), reads=[eid], writes=[r_])
            for i in range(128 + LA):
                if i < 128:
                    gather(i, peer_u)
                j = i - LA
                if j >= 0:
                    u_ = slot.pop(j)
                    V(lambda e: e.scalar_tensor_tensor(out=u_[:T, :], in0=u_[:T, :], scalar=1.0, in1=h2b[:T, :], op0=ALU.mult, op1=ALU.mult,
                                                       accum_out=actv[:T, j:j + 1]), [u_, h2b], [u_, actv])
            V(lambda e: e.tensor_tensor(out=tmpa[:T, :], in0=actv[:T, :], in1=actv[:T, :], op=ALU.mult), [actv], [tmpa])
            V(lambda e: e.tensor_scalar(out=tmpa[:T, :], in0=tmpa[:T, :], scalar1=0.044715, scalar2=1.0, op0=ALU.mult, op1=ALU.add), [tmpa], [tmpa])
            V(lambda e: e.tensor_tensor(out=tmpa[:T, :], in0=tmpa[:T, :], in1=actv[:T, :], op=ALU.mult), [tmpa, actv], [tmpa])
            A(lambda e: e.activation(out=tmpa[:T, :], in_=tmpa[:T, :], func=AF.Sigmoid, scale=1.5957691216057308), [tmpa], [tmpa])
            V(lambda e: e.tensor_tensor(out=wgt[:T, :], in0=tmpa[:T, :], in1=actv[:T, :], op=ALU.mult), [tmpa, actv], [wgt])
            V(lambda e: e.tensor_tensor(out=wgt[:T, :], in0=wgt[:T, :], in1=gate[:T].rearrange("p a b -> p (a b)"), op=ALU.mult), [wgt, gate], [wgt])
            py = [pbanks[5], pbanks[6]]
            for i in range(128 + LA):
                if i < 128:
                    gather(i, peer_v)
                j = i - LA
                if j >= 0:
                    v_ = slot.pop(j)
                    d_ = dg[j % 4]
                    A(lambda e: e.activation(out=d_[:T, :T], in_=cs(0, 0, T), func=AF.Copy, scale=wgt[:T, j:j + 1]), [cst, wgt], [d_])
                    for n in range(2):
                        P(lambda e: e.matmul(py[n][:T, :], lhsT=d_[:T, :T], rhs=v_[:T, n * 512:(n + 1) * 512], start=(j == 0), stop=(j == 127)), [d_, v_], [py[n]])
            for n in range(2):
                V(lambda e: e.tensor_tensor(out=x2[:T, n * 512:(n + 1) * 512], in0=x2[:T, n * 512:(n + 1) * 512], in1=py[n][:T, :], op=ALU.add), [x2, py[n]], [x2])
            A(lambda e: e.activation(out=h2b[:T, :], in_=x2[:T, :], func=AF.Square, accum_out=ssg[:T, :]), [x2], [h2b, ssg])
            A(lambda e: e.activation(out=ssg[:T, :], in_=ssg[:T, :], func=AF.Sqrt, scale=1.0 / D, bias=EPS), [ssg], [ssg])
            V(lambda e: e.reciprocal(out=ssg[:T, :], in_=ssg[:T, :]), [ssg], [ssg])
            V(lambda e: e.scalar_tensor_tensor(out=x2[:T, :], in0=x2[:T, :], scalar=ssg[:T, 0:1], in1=nzw[:T, :], op0=ALU.mult, op1=ALU.mult), [x2, ssg, nzw], [x2])
            fw.dma("sp", lambda e: e.dma_start(out=ydst, in_=x2[:T, :]), reads=[x2])

        def store_states(sk, o_gla, o_gdn, o_conv, T):
            gl = o_gla.rearrange("h k v -> (h k) v")
            for p_ in range(2):
                fw.dma("sp", lambda e: e.dma_start(out=gl[p_ * 128:(p_ + 1) * 128, :], in_=Sgla[sk][:, p_, :]), reads=[Sgla[sk]])
            fw.dma("sp", lambda e: e.dma_start(out=o_gdn.rearrange("h k v -> k h v"), in_=Sgdn[sk][:]), reads=[Sgdn[sk]])
            for r in range(3):
                for b0_ in range(0, 12, 3):
                    fw.dma("sp", lambda e: e.dma_start(out=o_conv[r:r + 1, :].rearrange("o (b p) -> p b o", p=128)[:, b0_:b0_ + 3, :], in_=convin[:, b0_:b0_ + 3, r:r + 1],
                                                       allow_slow_non_contiguous=True), reads=[convin])

        KM, KG = 5, 4
        pend = [None]
        fullctr = [0]

        def run_pair(Mf):
            if pend[0] is not None:
                fw.co.fork([Mf, pend[0]], [KM, KG])
            else:
                Mf()

        for i in range(NPRE):
            do_tile(xpre[i * 128:(i + 1) * 128, :], 128, 64, 2, False, "p", need_q=(i == NPRE - 1))
        for i in range(NMAIN):
            par = fullctr[0] % 2
            fullctr[0] += 1
            run_pair(lambda i=i, par=par: do_tile(xmain[i * 128:(i + 1) * 128, :], 128, 64, 2, True, "p", par, dname=("m%d" % i) if dbg else None))
            pend[0] = (lambda i=i, par=par: tile_G(par, 128, y_main[i * 128:(i + 1) * 128, :])) if limit >= 3 else None
        store_states("p", gla_st, gdn_st, conv_st, 128)
        for j in range(NSAMP):
            sk = "s%d" % j
            gl = sgla_in[j].rearrange("h k v -> (h k) v")
            for p_ in range(2):
                fw.dma("sp", lambda e: e.dma_start(out=Sgla[sk][:, p_, :], in_=gl[p_ * 128:(p_ + 1) * 128, :]), writes=[Sgla[sk]])
            fw.dma("sp", lambda e: e.dma_start(out=Sgdn[sk][:], in_=sgdn_in[j].rearrange("h k v -> k h v")), writes=[Sgdn[sk]])
            for r in range(3):
                for b0_ in range(0, 12, 3):
                    fw.dma("sp", lambda e: e.dma_start(out=convin[:, b0_:b0_ + 3, r:r + 1], in_=sconv_in[j, r:r + 1, :].rearrange("o (b p) -> p b o", p=128)[:, b0_:b0_ + 3, :],
                                                       allow_slow_non_contiguous=True), writes=[convin])
            par = fullctr[0] % 2
            fullctr[0] += 1
            run_pair(lambda j=j, par=par, sk=sk: do_tile(xs[j * 32:(j + 1) * 32, :], 32, 32, 1, True, sk, par, dname=("s%d" % j) if dbg else None))
            pend[0] = (lambda j=j, par=par: tile_G(par, 32, y_s[j * 32:(j + 1) * 32, :])) if limit >= 3 else None
            store_states(sk, gla_s[j], gdn_s[j], conv_s[j], 32)
        if pend[0] is not None:
            pend[0]()
        fw.finish("sp")
        print("instructions:", fw.nins, {e: fw.cnt[e] for e in fw.ENGS})
    return nc


N_CORES = 8
WNAMES = ["norm_mix_w", "w_in", "gla_w_gk2", "gla_b_gk", "gla_norm_w", "gdn_conv_w", "gdn_a_log", "gdn_dt_bias",
          "gdn_norm_w", "w_out", "norm_ffn_w", "peer_wq", "peer_k1", "peer_k2", "peer_u", "peer_v"]


def kernel(x_prompt, x_sample, state_gla, state_gdn, state_gdn_conv, norm_mix_w, w_in, gla_w_gk2, gla_b_gk,
           gla_norm_w, gdn_conv_w, gdn_a_log, gdn_dt_bias, gdn_norm_w, w_out, norm_ffn_w, peer_wq, peer_k1,
           peer_k2, peer_u, peer_v, norm_final_w):
    f = lambda a: np.ascontiguousarray(np.asarray(a, dtype=np.float32))
    x_prompt, x_sample = f(x_prompt), f(x_sample)
    B, L, _ = x_prompt.shape
    HALF = L // 2
    NT = HALF // 128
    loc = dict(norm_mix_w=norm_mix_w, w_in=w_in, gla_w_gk2=gla_w_gk2, gla_b_gk=gla_b_gk, gla_norm_w=gla_norm_w,
               gdn_conv_w=gdn_conv_w, gdn_a_log=gdn_a_log, gdn_dt_bias=gdn_dt_bias, gdn_norm_w=gdn_norm_w, w_out=w_out,
               norm_ffn_w=norm_ffn_w, peer_wq=peer_wq, peer_k1=peer_k1, peer_k2=peer_k2, peer_u=peer_u, peer_v=peer_v)
    shared = {k: f(v)[0] for k, v in loc.items()}
    shared["norm_final_w"] = f(norm_final_w)
    shared["cst"] = make_consts()
    nc = build(NT, NT, 2)
    in_maps = []
    zeros = np.zeros((HALF, D), np.float32)
    for c in range(N_CORES):
        b, half = c // 2, c % 2
        m = dict(shared)
        m["xpre"] = x_prompt[b, :HALF] if half == 1 else zeros
        m["xmain"] = x_prompt[b, half * HALF:(half + 1) * HALF]
        m["xs"] = x_sample[2 * c:2 * c + 2].reshape(64, D)
        m["sgla_in"] = f(state_gla)[0, 2 * c:2 * c + 2]
        m["sgdn_in"] = f(state_gdn)[0, 2 * c:2 * c + 2]
        m["sconv_in"] = f(state_gdn_conv)[0, 2 * c:2 * c + 2]
        in_maps.append({k: np.ascontiguousarray(v) for k, v in m.items()})
    res = run_bass_kernel_spmd(nc, in_maps, core_ids=list(range(N_CORES))).results
    y_prompt = np.zeros((B, L, D), np.float32)
    y_sample = np.zeros((16, 32, D), np.float32)
    gla_p = np.zeros((1, B, 4, 64, 128), np.float32)
    gdn_p = np.zeros((1, B, 4, 128, 128), np.float32)
    conv_p = np.zeros((1, B, 3, 1536), np.float32)
    gla_sm = np.zeros((1, 16, 4, 64, 128), np.float32)
    gdn_sm = np.zeros((1, 16, 4, 128, 128), np.float32)
    conv_sm = np.zeros((1, 16, 3, 1536), np.float32)
    for c in range(N_CORES):
        b, half = c // 2, c % 2
        r = res[c]
        y_prompt[b, half * HALF:(half + 1) * HALF] = r["y_main"]
        y_sample[2 * c:2 * c + 2] = r["y_s"].reshape(2, 32, D)
        if half == 1:
            gla_p[0, b] = r["gla_st"]
            gdn_p[0, b] = r["gdn_st"]
            conv_p[0, b] = r["conv_st"]
        gla_sm[0, 2 * c:2 * c + 2] = r["gla_s"]
        gdn_sm[0, 2 * c:2 * c + 2] = r["gdn_s"]
        conv_sm[0, 2 * c:2 * c + 2] = r["conv_s"]
    return (y_prompt, y_sample, gla_p, gdn_p, conv_p, gla_sm, gdn_sm, conv_sm)
```

```python
from contextlib import ExitStack
import threading
import numpy as np
import concourse.bass as bass
import concourse.mybir as mybir
from concourse.bass import IndirectOffsetOnAxis
from concourse.bass_utils import run_bass_kernel_spmd

F32 = mybir.dt.float32
BF16 = mybir.dt.bfloat16
U32 = mybir.dt.uint32
AF = mybir.ActivationFunctionType
ALU = mybir.AluOpType
AX = mybir.AxisListType

D = 1024
NIN = 3608
O_GQ, O_GK, O_GV, O_GLR, O_GG, O_DQKV, O_DA, O_DB, O_DZ = 0, 256, 512, 1024, 1040, 1552, 3088, 3092, 3096
EPS = 1e-6
NEG = -1e30


class Buf:
    __slots__ = ("name", "w", "r", "dsem", "dcnt", "excl")

    def __init__(self, name):
        self.name = name
        self.excl = False
        self.w = None
        self.r = {}
        self.dsem = None
        self.dcnt = 0


class TL:
    def __init__(self, t, b):
        self.t = t
        self.b = b

    def __getitem__(self, k):
        return self.t[k]


class Co:
    def __init__(self):
        self.slots = {}
        self.order = []
        self.n = 0
        self.nid = 0
        self.err = []
        self.t2s = {}

    def _new(self, k, parent):
        sid = self.nid
        self.nid += 1
        self.slots[sid] = dict(sem=threading.Semaphore(0), k=k, run=True, parent=parent, nchild=0)
        return sid

    def _pick(self, after):
        n = len(self.order)
        st = self.order.index(after)
        for d in range(1, n + 1):
            sid = self.order[(st + d) % n]
            if self.slots[sid]["run"]:
                return sid
        return None

    def fork(self, fns, ks):
        tid = threading.get_ident()
        me = self.t2s.get(tid)
        if me is None:
            me = self._new(1, None)
            self.t2s[tid] = me
            self.order.append(me)
        par = self.slots[me]
        par["run"] = False
        par["nchild"] = len(fns)
        pos = self.order.index(me)
        kids = []
        for i, (f, k) in enumerate(zip(fns, ks)):
            sid = self._new(k, me)
            kids.append(sid)
            self.order.insert(pos + 1 + i, sid)
            threading.Thread(target=self._wrap, args=(sid, f), daemon=True).start()
        self.n = 0
        self.slots[kids[0]]["sem"].release()
        par["sem"].acquire()
        if par["parent"] is None and self.err:
            e = self.err[0]
            self.err = []
            raise e

    def _wrap(self, sid, f):
        sl = self.slots[sid]
        sl["sem"].acquire()
        self.t2s[threading.get_ident()] = sid
        try:
            f()
        except BaseException as ex:
            self.err.append(ex)
        finally:
            sl["run"] = False
            par = self.slots[sl["parent"]]
            par["nchild"] -= 1
            if par["nchild"] == 0:
                par["run"] = True
            nxt = self._pick(sid)
            self.order.remove(sid)
            self.n = 0
            self.slots[nxt]["sem"].release()

    def tick(self):
        sid = self.t2s.get(threading.get_ident())
        if sid is None or len(self.order) <= 1:
            return
        self.n += 1
        if self.n >= self.slots[sid]["k"]:
            self.n = 0
            nxt = self._pick(sid)
            if nxt is not None and nxt != sid:
                self.slots[nxt]["sem"].release()
                self.slots[sid]["sem"].acquire()


class PEProxy:
    def __init__(self, fw, pe):
        self.fw = fw
        self.pe = pe
        self.mode = None

    @staticmethod
    def _r(n):
        return 32 if n <= 32 else (64 if n <= 64 else 128)

    def _sw(self, lhsT):
        shp = lhsT.shape
        mode = (self._r(shp[0]), self._r(int(np.prod(shp[1:]))))
        if self.mode is not None and mode != self.mode:
            fw = self.fw
            if fw.cnt["pe"] > 0 and fw.seen["pe"].get("pe", 0) < fw.cnt["pe"]:
                self.pe.wait_ge(fw.sem["pe"], fw.cnt["pe"])
                fw.seen["pe"]["pe"] = fw.cnt["pe"]
        self.mode = mode

    def matmul(self, out, lhsT, rhs, **kw):
        self._sw(lhsT)
        return self.pe.matmul(out, lhsT=lhsT, rhs=rhs, **kw)

    def transpose(self, out, in_, identity, **kw):
        self._sw(in_)
        return self.pe.transpose(out=out, in_=in_, identity=identity, **kw)


class FW:
    ENGS = ("pe", "dve", "act", "pool", "sp")

    def __init__(self, nc, stack):
        self.nc = nc
        self.stack = stack
        self.eng = {"pe": nc.tensor, "dve": nc.vector, "act": nc.scalar, "pool": nc.gpsimd, "sp": nc.sync}
        self.pex = PEProxy(self, nc.tensor)
        self.sem = {e: stack.enter_context(nc.semaphore("s_" + e)) for e in self.ENGS}
        self.cnt = {e: 0 for e in self.ENGS}
        self.seen = {e: {} for e in self.ENGS}
        self.dseen = {e: {} for e in self.ENGS}
        self.nins = 0
        self.dbufs = []
        self.co = Co()

    def sb(self, name, shape, dt=F32, dma=False):
        t = self.stack.enter_context(self.nc.sbuf_tensor(name, list(shape), dt))
        return TL(t, self.buf(name, dma))

    def ps(self, name, shape, dt=F32):
        t = self.stack.enter_context(self.nc.psum_tensor(name, list(shape), dt))
        tl = TL(t, self.buf(name))
        tl.b.excl = True
        return tl

    def buf(self, name, dma=False):
        b = Buf(name)
        if dma:
            b.dsem = self.stack.enter_context(self.nc.semaphore("d_" + name))
            self.dbufs.append(b)
        return b

    def _wait(self, e, other, idx):
        if idx <= 0 or self.seen[e].get(other, 0) >= idx:
            return
        self.eng[e].wait_ge(self.sem[other], idx)
        self.seen[e][other] = idx

    def _wait_dma(self, e, b):
        if b.dsem is None or b.dcnt == 0 or self.dseen[e].get(b.name, 0) >= b.dcnt:
            return
        self.eng[e].wait_ge(b.dsem, b.dcnt)
        self.dseen[e][b.name] = b.dcnt

    def _deps(self, e, reads, writes):
        for b in reads:
            if b.w is not None:
                self._wait(e, b.w[0], b.w[1])
            if b.excl:
                for re_, ri in b.r.items():
                    if re_ != e:
                        self._wait(e, re_, ri)
            self._wait_dma(e, b)
        for b in writes:
            if b.w is not None:
                self._wait(e, b.w[0], b.w[1])
            for re_, ri in b.r.items():
                self._wait(e, re_, ri)
            self._wait_dma(e, b)

    def op(self, e, fn, reads=(), writes=()):
        reads = [x.b if isinstance(x, TL) else x for x in reads]
        writes = [x.b if isinstance(x, TL) else x for x in writes]
        self._deps(e, reads, writes)
        ins = fn(self.pex if e == "pe" else self.eng[e])
        self.cnt[e] += 1
        idx = self.cnt[e]
        ins.then_inc(self.sem[e], 1)
        self.nins += 1
        for b in reads:
            b.r[e] = idx
        for b in writes:
            b.w = (e, idx)
            b.r = {}
        self.co.tick()
        return ins

    def dma(self, e, fn, reads=(), writes=(), owner=None):
        reads = [x.b if isinstance(x, TL) else x for x in reads]
        writes = [x.b if isinstance(x, TL) else x for x in writes]
        self._deps(e, reads, writes)
        ins = fn(self.eng[e])
        self.nins += 1
        if owner is not None:
            b = owner.b if isinstance(owner, TL) else owner
        else:
            sems = [b for b in reads + writes if b.dsem is not None]
            assert len(sems) == 1, [b.name for b in reads + writes]
            b = sems[0]
        b.dcnt += 16
        ins.then_inc(b.dsem, 16)
        self.co.tick()
        return ins

    def barrier(self):
        for e in self.ENGS:
            for o in self.ENGS:
                if o != e:
                    self._wait(e, o, self.cnt[o])
            for b in self.dbufs:
                self._wait_dma(e, b)

    def finish(self, e="sp"):
        for b in self.dbufs:
            self._wait_dma(e, b)
        for o in self.ENGS:
            if o != e:
                self._wait(e, o, self.cnt[o])


def make_consts():
    c = np.zeros((128, 7, 128), np.float32)
    i = np.arange(128)
    same = (i[:, None] // 64) == (i[None, :] // 64)
    c[:, 0, :] = np.eye(128)
    c[:, 1, :] = (i[:, None] <= i[None, :]) & same
    c[:, 2, :] = (i[:, None] > i[None, :]) & same
    c[:, 3, :] = (i[None, :] < i[:, None]) & same
    c[:, 4, :] = (i[:, None] <= i[None, :]) & same
    c[:, 5, :] = 1.0
    c[:64, 6, 0] = 1.0
    c[64:, 6, 1] = 1.0
    c[:, 6, 16:32] = np.arange(16, dtype=np.float32)[None, :]
    return c


def build(NPRE, NMAIN, NSAMP=2, dbg=False, limit=99):
    nc = bass.Bass("TRN2", target_bir_lowering=False)
    dt_in = lambda n, s: nc.dram_tensor(n, list(s), F32, kind="ExternalInput").ap()
    dt_out = lambda n, s: nc.dram_tensor(n, list(s), F32, kind="ExternalOutput").ap()
    xpre = dt_in("xpre", [max(NPRE, 1) * 128, D])
    xmain = dt_in("xmain", [NMAIN * 128, D])
    xs = dt_in("xs", [NSAMP * 32, D])
    sgla_in = dt_in("sgla_in", [NSAMP, 4, 64, 128])
    sgdn_in = dt_in("sgdn_in", [NSAMP, 4, 128, 128])
    sconv_in = dt_in("sconv_in", [NSAMP, 3, 1536])
    cst_in = dt_in("cst", [128, 7, 128])
    norm_mix_w = dt_in("norm_mix_w", [D])
    w_in = dt_in("w_in", [D, NIN])
    gla_w_gk2 = dt_in("gla_w_gk2", [16, 256])
    gla_b_gk = dt_in("gla_b_gk", [256])
    gla_norm_w = dt_in("gla_norm_w", [128])
    gdn_conv_w = dt_in("gdn_conv_w", [4, 1536])
    gdn_a_log = dt_in("gdn_a_log", [4])
    gdn_dt_bias = dt_in("gdn_dt_bias", [4])
    gdn_norm_w = dt_in("gdn_norm_w", [128])
    w_out = dt_in("w_out", [D, D])
    norm_ffn_w = dt_in("norm_ffn_w", [D])
    peer_wq = dt_in("peer_wq", [D, 2048])
    peer_k1 = dt_in("peer_k1", [8, 128, 128])
    peer_k2 = dt_in("peer_k2", [8, 128, 128])
    peer_u = dt_in("peer_u", [16384, D])
    peer_v = dt_in("peer_v", [16384, D])
    norm_final_w = dt_in("norm_final_w", [D])

    y_main = dt_out("y_main", [NMAIN * 128, D])
    y_s = dt_out("y_s", [NSAMP * 32, D])
    gla_st = dt_out("gla_st", [4, 64, 128])
    gdn_st = dt_out("gdn_st", [4, 128, 128])
    conv_st = dt_out("conv_st", [3, 1536])
    gla_s = dt_out("gla_s", [NSAMP, 4, 64, 128])
    gdn_s = dt_out("gdn_s", [NSAMP, 4, 128, 128])
    conv_s = dt_out("conv_s", [NSAMP, 3, 1536])
    dbg_outs = {}

    with ExitStack() as st:
        fw = FW(nc, st)
        V = lambda fn, r, w: fw.op("dve", fn, r, w)
        A = lambda fn, r, w: fw.op("act", fn, r, w)
        P = lambda fn, r, w: fw.op("pe", fn, r, w)
        GP = lambda fn, r, w: fw.op("pool", fn, r, w)

        cst = fw.sb("cstsb", [128, 7, 128], dma=True)
        fw.dma("sp", lambda e: e.dma_start(out=cst[:], in_=cst_in), writes=[cst])
        IDN, U1, LST, STRICT, TRIU, ONES, BLK = (cst[:, i, :] for i in range(7))

        def cs(i, r0, n, c0=None, m=None):
            c0 = r0 if c0 is None else c0
            m = n if m is None else m
            return cst[r0:r0 + n, i, c0:c0 + m]

        win = fw.sb("win", [128, 8, NIN], BF16)
        wi_v = w_in.rearrange("(k p) n -> p k n", p=128)
        wo_v = w_out.rearrange("(k p) n -> p k n", p=128)
        wq_v = peer_wq.rearrange("(k p) n -> p k n", p=128)
        nmwT = fw.sb("nmwT", [128, 8], dma=True)
        for k in range(0, 8, 2):
            fw.dma("sp", lambda e: e.dma_start(out=nmwT[:, k:k + 2], in_=norm_mix_w.rearrange("(k p) -> p k", p=128)[:, k:k + 2], allow_slow_non_contiguous=True), writes=[nmwT])
        nfw = fw.sb("nfw", [128, D], dma=True)
        nzw = fw.sb("nzw", [128, D], dma=True)
        for t_, src in ((nfw, norm_ffn_w), (nzw, norm_final_w)):
            fw.dma("sp", lambda e: e.dma_start(out=t_[:], in_=src.partition_broadcast(128)), writes=[t_])
        gnw1 = fw.sb("gnw1", [128, 128], dma=True)
        gnw2 = fw.sb("gnw2", [128, 128], dma=True)
        fw.dma("sp", lambda e: e.dma_start(out=gnw1[:], in_=gla_norm_w.partition_broadcast(128)), writes=[gnw1])
        fw.dma("sp", lambda e: e.dma_start(out=gnw2[:], in_=gdn_norm_w.partition_broadcast(128)), writes=[gnw2])
        wgk = fw.sb("wgk", [16, 256], dma=True)
        fw.dma("sp", lambda e: e.dma_start(out=wgk[:], in_=gla_w_gk2), writes=[wgk])
        bgk = fw.sb("bgk", [1, 256], dma=True)
        fw.dma("sp", lambda e: e.dma_start(out=bgk[:], in_=gla_b_gk.rearrange("(o n) -> o n", o=1)), writes=[bgk])
        wc = fw.sb("wc", [128, 12, 4], dma=True)
        for i in range(4):
            for b0_ in range(0, 12, 3):
                fw.dma("sp", lambda e: e.dma_start(out=wc[:, b0_:b0_ + 3, i:i + 1],
                                                   in_=gdn_conv_w[i:i + 1, :].rearrange("o (b p) -> p b o", p=128)[:, b0_:b0_ + 3, :],
                                                   allow_slow_non_contiguous=True), writes=[wc])
        dtb = fw.sb("dtb", [128, 4], dma=True)
        fw.dma("sp", lambda e: e.dma_start(out=dtb[:], in_=gdn_dt_bias.partition_broadcast(128)), writes=[dtb])
        alog = fw.sb("alog", [128, 4], dma=True)
        fw.dma("sp", lambda e: e.dma_start(out=alog[:], in_=gdn_a_log.partition_broadcast(128)), writes=[alog])
        negA = fw.sb("negA", [128, 4])
        A(lambda e: e.activation(out=negA[:], in_=alog[:], func=AF.Exp), [alog], [negA])
        V(lambda e: e.tensor_scalar(out=negA[:], in0=negA[:], scalar1=-1.0, scalar2=None, op0=ALU.mult), [negA], [negA])

        pbanks = [fw.ps("pb%d" % i, [128, 512]) for i in range(7)]
        pbf = fw.ps("pbf", [128, 8, 128], BF16)
        pctr = [0]

        def pbank():
            p = pbanks[pctr[0] % 5]
            pctr[0] += 1
            return p

        def v4(p, a=4, b=128, rows=slice(None)):
            return p[rows, 0:a * b].rearrange("p (a b) -> p a b", b=b)

        ARW = 17408
        arena = st.enter_context(nc.sbuf_tensor("arena", [128, ARW], F32))
        aoff = {"m": 0, "p": 0}

        def av(phase, name, shape, dt=F32, buf=None, dma=False):
            n = int(np.prod(shape[1:]))
            words = n if dt == F32 else n // 2
            o = aoff[phase]
            aoff[phase] += words
            assert aoff[phase] <= ARW, (phase, name, aoff[phase])
            ap = arena[:, o:o + words]
            if dt != F32:
                ap = ap.bitcast(dt)
            if len(shape) == 3:
                ap = ap.rearrange("p (a b) -> p a b", b=shape[2])
            return TL(ap, buf if buf is not None else fw.buf(name, dma))

        M = lambda name, shape, dt=F32: av("m", name, shape, dt)
        gqk = M("gqk", [128, 512]); gv = M("gv", [128, 512]); sgg = M("sgg", [128, 512]); sdz = M("sdz", [128, 512])
        a1 = M("a1", [128, 256]); eG = M("eG", [128, 256]); enG = M("enG", [128, 256]); edG = M("edG", [128, 256])
        qks = M("qks", [128, 512]); kh = M("kh", [128, 256]); qkT = M("qkT", [128, 4, 128]); attS = M("attS", [128, 4, 64])
        o12 = M("o12", [128, 8, 128]); cv = M("cv", [128, 12, 128]); cvt = M("cvt", [128, 12, 128]); rq = M("rq", [128, 8, 128])
        qT = M("qT", [128, 4, 128]); kTg = M("kTg", [128, 4, 128]); qgT = M("qgT", [128, 4, 128])
        ktok = M("ktok", [128, 4, 128]); vtok = M("vtok", [128, 4, 128])
        big = M("big", [128, 4, 128]); decS = M("decS", [128, 4, 128]); decT = M("decT", [128, 4, 128])
        Mc = M("Mc", [128, 4, 64]); MTc = M("MTc", [128, 4, 64]); Xc = M("Xc", [128, 4, 64])
        vb = M("vb", [128, 4, 128]); kbg = M("kbg", [128, 4, 128]); kd = M("kd", [128, 4, 128])
        nwT = M("nwT", [128, 4, 64]); vnew = M("vnew", [128, 4, 128]); ATc = M("ATc", [128, 4, 64])
        wq = av("p", "wq", [128, 8, 2048], BF16, dma=True)
        pqT = av("p", "pqT", [128, 16, 128]); s12 = av("p", "s12", [128, 16, 128])
        wob = fw.buf("wout_cand", dma=True)
        o_c = aoff["p"]
        cand = av("p", "cand", [128, 8, 256], buf=wob); cidx = av("p", "cidx", [128, 8, 256], buf=wob)
        wout = TL(arena[:, o_c:o_c + 4096].bitcast(BF16).rearrange("p (a b) -> p a b", b=D), wob)

        xt = fw.sb("xt", [128, D], dma=True)
        kT = fw.sb("kT", [128, 16, 128])
        junk = fw.sb("junk", [128, D], BF16)
        ss = fw.sb("ss", [128, 1])
        ssg = fw.sb("ssg", [128, 1])
        xn = fw.sb("xn", [128, D], BF16)
        hT = fw.sb("hT", [128, 8, 128], BF16)
        dab = fw.sb("dab", [128, 8])
        glrT = fw.sb("glrT", [16, 128])
        convin = fw.sb("convin", [128, 12, 131], dma=True)
        egl = fw.sb("egl", [128, 2, 2])
        mixb = xn
        mixT = hT
        g4 = fw.sb("g4", [128, 10, 4])
        a2blk = fw.sb("a2blk", [128, 4, 2])
        egl2 = fw.sb("egl2", [128, 4, 2])
        x2s = [fw.sb("x2_%d" % i, [128, D], dma=True) for i in range(2)]
        h2bs = [fw.sb("h2b_%d" % i, [128, D], BF16) for i in range(2)]
        eids = [fw.sb("eid_%d" % i, [128, 128], U32) for i in range(2)]
        gates = [fw.sb("gate_%d" % i, [128, 8, 16]) for i in range(2)]
        wk = fw.sb("wk", [128, 256])
        v12 = fw.sb("v12", [128, 16, 16])
        i12 = fw.sb("i12", [128, 16, 16], U32)
        i12f = fw.sb("i12f", [128, 16, 16])
        sc = fw.sb("sc", [128, 8, 16])
        eidf = fw.sb("eidf", [128, 128])
        zz = fw.sb("zz", [128, 8])
        actv = fw.sb("actv", [128, 128])
        wgt = fw.sb("wgt", [128, 128])
        tmpa = fw.sb("tmpa", [128, 128])
        NSLOT = 8 if dbg else 9
        ring = [fw.sb("ring%d" % i, [128, D], BF16, dma=True) for i in range(NSLOT)]
        rctr = [0]
        dg = [fw.sb("dg%d" % i, [128, 128], BF16) for i in range(4)]
        _sg = fw.sb("Sgla_p", [128, 2, 128], dma=True)
        _sd = fw.sb("Sgdn_p", [128, 4, 128], dma=True)
        Sgla = {k: _sg for k in ["p"] + ["s%d" % i for i in range(NSAMP)]}
        Sgdn = {k: _sd for k in ["p"] + ["s%d" % i for i in range(NSAMP)]}
        GP(lambda e: e.memset(Sgla["p"][:], 0.0), [], [Sgla["p"]])
        GP(lambda e: e.memset(Sgdn["p"][:], 0.0), [], [Sgdn["p"]])
        GP(lambda e: e.memset(convin[:], 0.0), [], [convin])
        wstg = [TL(arena[:, i * NIN:(i + 1) * NIN], fw.buf("wstg%d" % i, dma=True)) for i in range(2)]
        for k in range(8):
            ws = wstg[k % 2]
            fw.dma("sp", lambda e: e.dma_start(out=ws[:, :], in_=wi_v[:, k, :]), writes=[ws])
            V(lambda e: e.tensor_scalar(out=win[:, k, :], in0=ws[:, :], scalar1=nmwT[:, k:k + 1], scalar2=None, op0=ALU.mult), [ws, nmwT], [win])
        fw.barrier()
        wq_b = nc.dram_tensor("wq_b", [128, 8, 2048], BF16, kind="Internal").ap()
        wo_b = nc.dram_tensor("wo_b", [128, 8, D], BF16, kind="Internal").ap()
        wq_sw = TL(wq.t, fw.buf("wq_sw", dma=True))
        wout_sw = TL(wout.t, fw.buf("wout_sw", dma=True))
        scr_wq = fw.buf("scr_wq", dma=True)
        scr_wo = fw.buf("scr_wo", dma=True)
        for k in range(8):
            fw.dma("pool", lambda e: e.dma_start(out=wq_sw[:, k, :], in_=wq_v[:, k, :]), writes=[wq_sw])
        fw.dma("sp", lambda e: e.dma_start(out=wq_b, in_=wq_sw[:, :, :]), reads=[wq_sw], writes=[scr_wq], owner=scr_wq)
        for k in range(8):
            fw.dma("pool", lambda e: e.dma_start(out=wout_sw[:, k, :], in_=wo_v[:, k, :]), writes=[wout_sw])
        fw.dma("sp", lambda e: e.dma_start(out=wo_b, in_=wout_sw[:, :, :]), reads=[wout_sw], writes=[scr_wo], owner=scr_wo)
        fw.barrier()
        kraw = TL(arena[:, 0:2048].rearrange("p (a b) -> p a b", b=128), fw.buf("kraw", dma=True))
        fw.dma("sp", lambda e: e.dma_start(out=kraw[:, 0:8, :], in_=peer_k1.rearrange("h n d -> n h d")), writes=[kraw])
        fw.dma("sp", lambda e: e.dma_start(out=kraw[:, 8:16, :], in_=peer_k2.rearrange("h n d -> n h d")), writes=[kraw])
        for g in range(4):
            pk = pbank()
            for j in range(4):
                P(lambda e: e.transpose(out=pk[:, j * 128:(j + 1) * 128], in_=kraw[:, g * 4 + j, :], identity=IDN), [kraw, cst], [pk])
            V(lambda e: e.tensor_copy(out=kT[:, g * 4:(g + 1) * 4, :], in_=v4(pk)), [pk], [kT])

        fw.barrier()

        stg = fw.sb("stg", [128, D], BF16, dma=True) if dbg else None

        def dump(name, tl, ap, shape):
            if not dbg:
                return
            if tl in x2s:
                o = nc.dram_tensor("dbg_" + name, list(shape), F32, kind="ExternalOutput").ap()
                fw.dma("sp", lambda e: e.dma_start(out=o, in_=ap), reads=[tl])
                return
            o = nc.dram_tensor("dbg_" + name, list(shape), BF16, kind="ExternalOutput").ap()
            V(lambda e: e.tensor_copy(out=stg[:shape[0], :shape[1]], in_=ap), [tl], [stg])
            fw.dma("sp", lambda e: e.dma_start(out=o, in_=stg[:shape[0], :shape[1]]), reads=[stg])

        def rmsnorm_rs(src, T):
            A(lambda e: e.activation(out=junk[:T, :], in_=src[:T, :], func=AF.Square, accum_out=ss[:T, :]), [src], [junk, ss])
            A(lambda e: e.activation(out=ss[:T, :], in_=ss[:T, :], func=AF.Sqrt, scale=1.0 / D, bias=EPS), [ss], [ss])
            V(lambda e: e.reciprocal(out=ss[:T, :], in_=ss[:T, :]), [ss], [ss])

        def transpose_bf(src, dst, T):
            for k in range(8):
                P(lambda e: e.transpose(out=pbf[:, k, :T], in_=src[:T, k * 128:(k + 1) * 128], identity=idb[:T, :T]), [src, idb], [pbf])
            A(lambda e: e.copy(out=dst[:, :, :T], in_=pbf[:, :, :T]), [pbf], [dst])

        idb = fw.sb("idb", [128, 128], BF16)
        V(lambda e: e.tensor_copy(out=idb[:], in_=IDN), [cst], [idb])

        jscr = [fw.sb("jscr%d" % i, [128, 128], BF16) for i in range(2)]
        ssq2 = [fw.sb("ssq%d" % i, [128, 4]) for i in range(2)]
        o12h = [TL(o12.t[:, 0:4, :], fw.buf("o12a")), TL(o12.t[:, 4:8, :], fw.buf("o12b"))]

        def gated_norm(which, gw, sg, T, col0):
            oo, sq_, js = o12h[which], ssq2[which], jscr[which]
            for h in range(4):
                A(lambda e: e.activation(out=js[:T, :], in_=oo[:T, h, :], func=AF.Square, accum_out=sq_[:T, h:h + 1]), [oo], [js, sq_])
            A(lambda e: e.activation(out=sq_[:T, :], in_=sq_[:T, :], func=AF.Sqrt, scale=1.0 / 128, bias=EPS), [sq_], [sq_])
            V(lambda e: e.reciprocal(out=sq_[:T, :], in_=sq_[:T, :]), [sq_], [sq_])
            mv = oo[:T, :, :]
            V(lambda e: e.tensor_tensor(out=mv, in0=mv, in1=sq_[:T, :].unsqueeze(2).to_broadcast([T, 4, 128]), op=ALU.mult), [oo, sq_], [oo])
            V(lambda e: e.tensor_tensor(out=mv, in0=mv, in1=gw[:T, :].unsqueeze(1).to_broadcast([T, 4, 128]), op=ALU.mult), [oo, gw], [oo])
            V(lambda e: e.tensor_tensor(out=mixb[:T, col0:col0 + 512].rearrange("p (a b) -> p a b", b=128), in0=mv,
                                        in1=sg[:T, :].rearrange("p (a b) -> p a b", b=128), op=ALU.mult), [oo, sg], [mixb])

        McC = [TL(Mc.t, fw.buf("Mc%d" % c)) for c in range(2)]
        MTcC = [TL(MTc.t, fw.buf("MTc%d" % c)) for c in range(2)]
        XcC = [TL(Xc.t, fw.buf("Xc%d" % c)) for c in range(2)]

        tile_ctr = [0]

        def do_tile(xsrc, T, C, nch, full, sk, par=0, dname=None, need_q=True):
            tile_ctr[0] += 1
            SG, SD = Sgla[sk], Sgdn[sk]
            x2, h2b, eid, gate = x2s[par], h2bs[par], eids[par], gates[par]
            fw.dma("sp", lambda e: e.dma_start(out=xt[:T, :], in_=xsrc), writes=[xt])
            rmsnorm_rs(xt, T)
            V(lambda e: e.tensor_scalar(out=xn[:T, :], in0=xt[:T, :], scalar1=ss[:T, 0:1], scalar2=None, op0=ALU.mult), [xt, ss], [xn])
            transpose_bf(xn, hT, T)

            def proj_tok(c0, ncols, evac):
                pp = pbank()
                for k in range(8):
                    P(lambda e: e.matmul(pp[:T, 0:ncols], lhsT=hT[:, k, :T], rhs=win[:, k, c0:c0 + ncols], start=(k == 0), stop=(k == 7)), [hT, win], [pp])
                evac(pp)
            if full:
                proj_tok(O_GQ, 512, lambda pp: V(lambda e: e.tensor_copy(out=gqk[:T, :], in_=pp[:T, :]), [pp], [gqk]))
            else:
                proj_tok(O_GK, 256, lambda pp: V(lambda e: e.tensor_copy(out=gqk[:T, 256:512], in_=pp[:T, 0:256]), [pp], [gqk]))
            proj_tok(O_GV, 512, lambda pp: A(lambda e: e.copy(out=gv[:T, :], in_=pp[:T, :]), [pp], [gv]))
            proj_tok(O_DA, 8, lambda pp: V(lambda e: e.tensor_copy(out=dab[:T, :], in_=pp[:T, 0:8]), [pp], [dab]))
            if full:
                proj_tok(O_GG, 512, lambda pp: A(lambda e: e.activation(out=sgg[:T, :], in_=pp[:T, :], func=AF.Silu), [pp], [sgg]))
                proj_tok(O_DZ, 512, lambda pp: A(lambda e: e.activation(out=sdz[:T, :], in_=pp[:T, :], func=AF.Silu), [pp], [sdz]))
            pp = pbank()
            for k in range(8):
                P(lambda e: e.matmul(pp[0:16, :T], lhsT=win[:, k, O_GLR:O_GLR + 16], rhs=hT[:, k, :T], start=(k == 0), stop=(k == 7)), [hT, win], [pp])
            V(lambda e: e.tensor_copy(out=glrT[:, :T], in_=pp[0:16, :T]), [pp], [glrT])
            for g in range(0 if need_q else 1, 3):
                pp = pbank()
                for j in range(4):
                    blk = g * 4 + j
                    for k in range(8):
                        P(lambda e: e.matmul(pp[:, j * 128:j * 128 + T], lhsT=win[:, k, O_DQKV + blk * 128:O_DQKV + (blk + 1) * 128], rhs=hT[:, k, :T],
                                             start=(k == 0), stop=(k == 7)), [hT, win], [pp])
                (A if g % 2 else V)(lambda e: (e.copy if g % 2 else e.tensor_copy)(convin[:, g * 4:(g + 1) * 4, 3:3 + T], v4(pp)[:, :, :T]), [pp], [convin])

            def gla_chain():
                pp = pbank()
                P(lambda e: e.matmul(pp[:T, 0:256], lhsT=glrT[:, :T], rhs=wgk[:], start=True, stop=False), [glrT, wgk], [pp])
                P(lambda e: e.matmul(pp[:T, 0:256], lhsT=cst[0:1, 5, :T], rhs=bgk[:], start=False, stop=True), [cst, bgk], [pp])
                A(lambda e: e.activation(out=a1[:T, :], in_=pp[:T, 0:256], func=AF.Exp, scale=-1.0), [pp], [a1])
                A(lambda e: e.activation(out=a1[:T, :], in_=a1[:T, :], func=AF.Ln, bias=1.0), [a1], [a1])
                V(lambda e: e.tensor_scalar(out=a1[:T, :], in0=a1[:T, :], scalar1=-1.0 / 16, scalar2=None, op0=ALU.mult), [a1], [a1])
                pp = pbank()
                P(lambda e: e.matmul(pp[:T, 0:256], lhsT=cs(1, 0, T), rhs=a1[:T, :], start=True, stop=True), [cst, a1], [pp])
                P(lambda e: e.matmul(pp[:T, 256:512], lhsT=cs(2, 0, T), rhs=a1[:T, :], start=True, stop=True), [cst, a1], [pp])
                if full:
                    A(lambda e: e.activation(out=eG[:T, :], in_=pp[:T, 0:256], func=AF.Exp), [pp], [eG])
                A(lambda e: e.activation(out=enG[:T, :], in_=pp[:T, 0:256], func=AF.Exp, scale=-1.0), [pp], [enG])
                A(lambda e: e.activation(out=edG[:T, :], in_=pp[:T, 256:512], func=AF.Exp), [pp], [edG])
                if full:
                    V(lambda e: e.scalar_tensor_tensor(out=qks[:T, 0:256], in0=gqk[:T, 0:256], scalar=0.125, in1=eG[:T, :], op0=ALU.mult, op1=ALU.mult), [gqk, eG], [qks])
                    V(lambda e: e.tensor_tensor(out=qks[:T, 256:512], in0=gqk[:T, 256:512], in1=enG[:T, :], op=ALU.mult), [gqk, enG], [qks])
                V(lambda e: e.tensor_tensor(out=kh[:T, :], in0=gqk[:T, 256:512], in1=edG[:T, :], op=ALU.mult), [gqk, edG], [kh])
                pp = pbank()
                for p_ in range(2):
                    P(lambda e: e.matmul(pp[:, p_ * 2:p_ * 2 + nch], lhsT=a1[:T, p_ * 128:(p_ + 1) * 128], rhs=cst[:T, 6, 0:nch], start=True, stop=True), [a1, cst], [pp])
                A(lambda e: e.activation(out=egl[:, :, 0:nch], in_=pp[:, 0:4].rearrange("p (a b) -> p a b", b=2)[:, :, 0:nch], func=AF.Exp), [pp], [egl])
                if full:
                    pp = pbank()
                    for j in range(4):
                        P(lambda e: e.transpose(out=pp[:, j * 128:j * 128 + T], in_=qks[:T, j * 128:(j + 1) * 128], identity=cs(0, 0, T)), [qks, cst], [pp])
                    V(lambda e: e.tensor_copy(out=qkT[:, :, :T], in_=v4(pp)[:, :, :T]), [pp], [qkT])
                for c in range(nch):
                    r0 = c * C
                    rs_ = slice(r0, r0 + C)
                    if full:
                        pa = pbank()
                        for h in range(4):
                            hs = slice((h % 2) * 64, (h % 2) * 64 + 64)
                            P(lambda e: e.matmul(pa[rs_, h * 64:h * 64 + C], lhsT=qkT[hs, 2 + h // 2, rs_], rhs=qkT[hs, h // 2, rs_], start=True, stop=True), [qkT], [pa])
                        V(lambda e: e.tensor_tensor(out=attS[rs_, :, :C], in0=v4(pa, 4, 64, rs_)[:, :, :C],
                                                    in1=cs(4, r0, C).unsqueeze(1).to_broadcast([C, 4, C]), op=ALU.mult), [pa, cst], [attS])
                        po = pbank()
                        for h in range(4):
                            hs = slice((h % 2) * 64, (h % 2) * 64 + 64)
                            P(lambda e: e.matmul(po[rs_, h * 128:(h + 1) * 128], lhsT=qkT[hs, h // 2, rs_], rhs=SG[hs, h // 2, :], start=True, stop=False), [qkT, SG], [po])
                            P(lambda e: e.matmul(po[rs_, h * 128:(h + 1) * 128], lhsT=attS[rs_, h, :C], rhs=gv[rs_, h * 128:(h + 1) * 128], start=False, stop=True), [attS, gv], [po])
                        A(lambda e: e.copy(out=o12h[0][rs_, :, :], in_=v4(po, rows=rs_)), [po], [o12h[0]])
                    pst = pbank()
                    for h in range(4):
                        hs = slice((h % 2) * 64, (h % 2) * 64 + 64)
                        P(lambda e: e.matmul(pst[hs, (h // 2) * 128:(h // 2 + 1) * 128], lhsT=kh[rs_, h * 64:(h + 1) * 64], rhs=gv[rs_, h * 128:(h + 1) * 128], start=True, stop=True), [kh, gv], [pst])
                    for p_ in range(2):
                        V(lambda e: e.scalar_tensor_tensor(out=SG[:, p_, :], in0=SG[:, p_, :], scalar=egl[:, p_, c:c + 1], in1=pst[:, p_ * 128:(p_ + 1) * 128],
                                                           op0=ALU.mult, op1=ALU.add), [SG, egl, pst], [SG])
                if full:
                    gated_norm(0, gnw1, sgg, T, 0)

            def gdn_chain():
                b0 = 0 if full else 4
                nb = 12 - b0
                V(lambda e: e.tensor_tensor(out=cv[:, b0:, :T], in0=convin[:, b0:, 0:T], in1=wc[:, b0:, 0:1].to_broadcast([128, nb, T]), op=ALU.mult), [convin, wc], [cv])
                for i in range(1, 4):
                    V(lambda e: e.tensor_tensor(out=cvt[:, b0:, :T], in0=convin[:, b0:, i:i + T], in1=wc[:, b0:, i:i + 1].to_broadcast([128, nb, T]), op=ALU.mult), [convin, wc], [cvt])
                    V(lambda e: e.tensor_tensor(out=cv[:, b0:, :T], in0=cv[:, b0:, :T], in1=cvt[:, b0:, :T], op=ALU.add), [cv, cvt], [cv])
                A(lambda e: e.copy(out=cvt[:, :, 0:3], in_=convin[:, :, T:T + 3]), [convin], [cvt])
                A(lambda e: e.copy(out=convin[:, :, 0:3], in_=cvt[:, :, 0:3]), [cvt], [convin])
                A(lambda e: e.activation(out=cv[:, b0:, :T], in_=cv[:, b0:, :T], func=AF.Silu), [cv], [cv])
                if dname:
                    dump(dname + "_convin", convin, convin[:, 0, 3:3 + T], [128, T])
                    dump(dname + "_cv", cv, cv[:, 4, :T], [128, T])
                V(lambda e: e.tensor_tensor(out=cvt[:, b0:8, :T], in0=cv[:, b0:8, :T], in1=cv[:, b0:8, :T], op=ALU.mult), [cv], [cvt])
                for g in range(0 if full else 1, 2):
                    pp = pbank()
                    for j in range(4):
                        P(lambda e: e.matmul(pp[:, j * 128:j * 128 + T], lhsT=ONES, rhs=cvt[:, g * 4 + j, :T], start=True, stop=True), [cst, cvt], [pp])
                    A(lambda e: e.activation(out=rq[:, g * 4:(g + 1) * 4, :T], in_=v4(pp)[:, :, :T], func=AF.Sqrt, bias=EPS), [pp], [rq])
                V(lambda e: e.reciprocal(out=rq[:, b0:8, :T], in_=rq[:, b0:8, :T]), [rq], [rq])
                if full:
                    V(lambda e: e.scalar_tensor_tensor(out=qT[:, :, :T], in0=cv[:, 0:4, :T], scalar=128.0 ** -0.5, in1=rq[:, 0:4, :T], op0=ALU.mult, op1=ALU.mult), [cv, rq], [qT])
                V(lambda e: e.tensor_tensor(out=kTg[:, :, :T], in0=cv[:, 4:8, :T], in1=rq[:, 4:8, :T], op=ALU.mult), [cv, rq], [kTg])
                pp = pbank()
                for h in range(4):
                    P(lambda e: e.transpose(out=pp[:T, h * 128:(h + 1) * 128], in_=kTg[:, h, :T], identity=IDN), [kTg, cst], [pp])
                A(lambda e: e.copy(out=ktok[:T], in_=v4(pp, rows=slice(0, T))), [pp], [ktok])
                pp = pbank()
                for h in range(4):
                    P(lambda e: e.transpose(out=pp[:T, h * 128:(h + 1) * 128], in_=cv[:, 8 + h, :T], identity=IDN), [cv, cst], [pp])
                A(lambda e: e.copy(out=vtok[:T], in_=v4(pp, rows=slice(0, T))), [pp], [vtok])
                if dname:
                    dump(dname + "_kTg", kTg, kTg[:, 0, :T], [128, T])
                    dump(dname + "_ktok", ktok, ktok[:T, 0, :], [T, 128])
                V(lambda e: e.tensor_tensor(out=g4[:T, 0, :], in0=dab[:T, 0:4], in1=dtb[:T, :], op=ALU.add), [dab, dtb], [g4])
                A(lambda e: e.activation(out=g4[:T, 0, :], in_=g4[:T, 0, :], func=AF.Exp), [g4], [g4])
                A(lambda e: e.activation(out=g4[:T, 0, :], in_=g4[:T, 0, :], func=AF.Ln, bias=1.0), [g4], [g4])
                V(lambda e: e.tensor_tensor(out=g4[:T, 1, :], in0=g4[:T, 0, :], in1=negA[:T, :], op=ALU.mult), [g4, negA], [g4])
                A(lambda e: e.activation(out=g4[:T, 2, :], in_=dab[:T, 4:8], func=AF.Sigmoid), [dab], [g4])
                V(lambda e: e.tensor_scalar(out=g4[:T, 3, :], in0=g4[:T, 2, :], scalar1=-1.0, scalar2=None, op0=ALU.mult), [g4], [g4])
                pp = pbank()
                P(lambda e: e.matmul(pp[:T, 0:4], lhsT=cs(1, 0, T), rhs=g4[:T, 1, :], start=True, stop=True), [cst, g4], [pp])
                P(lambda e: e.matmul(pp[:T, 4:8], lhsT=cs(2, 0, T), rhs=g4[:T, 1, :], start=True, stop=True), [cst, g4], [pp])
                A(lambda e: e.activation(out=g4[:T, 4:6, :], in_=pp[:T, 0:8].rearrange("p (a b) -> p a b", b=4), func=AF.Exp), [pp], [g4])
                V(lambda e: e.tensor_tensor(out=g4[:T, 6, :], in0=g4[:T, 2, :], in1=g4[:T, 4, :], op=ALU.mult), [g4], [g4])
                V(lambda e: e.tensor_tensor(out=a2blk[:T, :, 0:nch], in0=g4[:T, 1, :].unsqueeze(2).to_broadcast([T, 4, nch]),
                                            in1=cst[:T, 6, 0:nch].unsqueeze(1).to_broadcast([T, 4, nch]), op=ALU.mult), [g4, cst], [a2blk])
                pp = pbank()
                for h in range(4):
                    P(lambda e: e.matmul(pp[:, h * 2:h * 2 + nch], lhsT=cst[:T, 5, :], rhs=a2blk[:T, h, 0:nch], start=True, stop=True), [cst, a2blk], [pp])
                A(lambda e: e.activation(out=egl2[:, :, 0:nch], in_=pp[:, 0:8].rearrange("p (a b) -> p a b", b=2)[:, :, 0:nch], func=AF.Exp), [pp], [egl2])
                if full:
                    V(lambda e: e.tensor_tensor(out=big[:T, :, :T], in0=cs(0, 0, T).unsqueeze(1).to_broadcast([T, 4, T]),
                                                in1=g4[:T, 4, :].unsqueeze(2).to_broadcast([T, 4, T]), op=ALU.mult), [cst, g4], [big])
                    pp = pbank()
                    for h in range(4):
                        P(lambda e: e.matmul(pp[:, h * 128:h * 128 + T], lhsT=cst[:T, 5, :], rhs=big[:T, h, :T], start=True, stop=True), [cst, big], [pp])
                    V(lambda e: e.tensor_tensor(out=qgT[:, :, :T], in0=qT[:, :, :T], in1=v4(pp)[:, :, :T], op=ALU.mult), [qT, pp], [qgT])
                V(lambda e: e.tensor_tensor(out=big[:T, :, :T], in0=cs(2, 0, T).unsqueeze(1).to_broadcast([T, 4, T]),
                                            in1=g4[:T, 1, :].unsqueeze(2).to_broadcast([T, 4, T]), op=ALU.mult), [cst, g4], [big])
                pp = pbank()
                for h in range(4):
                    P(lambda e: e.matmul(pp[:T, h * 128:h * 128 + T], lhsT=cs(1, 0, T), rhs=big[:T, h, :T], start=True, stop=True), [cst, big], [pp])
                A(lambda e: e.activation(out=decS[:T, :, :T], in_=v4(pp, rows=slice(0, T))[:, :, :T], func=AF.Exp), [pp], [decS])
                V(lambda e: e.tensor_tensor(out=decS[:T, :, :T], in0=decS[:T, :, :T], in1=cs(3, 0, T).unsqueeze(1).to_broadcast([T, 4, T]), op=ALU.mult), [decS, cst], [decS])
                if full:
                    V(lambda e: e.tensor_tensor(out=big[:T, :, :T], in0=cs(1, 0, T).unsqueeze(1).to_broadcast([T, 4, T]),
                                                in1=g4[:T, 1, :].unsqueeze(2).to_broadcast([T, 4, T]), op=ALU.mult), [cst, g4], [big])
                    pp = pbank()
                    for h in range(4):
                        P(lambda e: e.matmul(pp[:T, h * 128:h * 128 + T], lhsT=cs(2, 0, T), rhs=big[:T, h, :T], start=True, stop=True), [cst, big], [pp])
                    A(lambda e: e.activation(out=decT[:T, :, :T], in_=v4(pp, rows=slice(0, T))[:, :, :T], func=AF.Exp), [pp], [decT])
                    V(lambda e: e.tensor_tensor(out=decT[:T, :, :T], in0=decT[:T, :, :T], in1=cs(4, 0, T).unsqueeze(1).to_broadcast([T, 4, T]), op=ALU.mult), [decT, cst], [decT])
                if dname:
                    dump(dname + "_g4", g4, g4[:T, 0:7, :].rearrange("p a b -> p (a b)"), [T, 28])
                    dump(dname + "_decS", decS, decS[:T, 0, :T], [T, T])
                bc4 = lambda row: g4[:T, row, :].unsqueeze(2).to_broadcast([T, 4, 128])
                V(lambda e: e.tensor_tensor(out=vb[:T], in0=vtok[:T], in1=bc4(2), op=ALU.mult), [vtok, g4], [vb])
                V(lambda e: e.tensor_tensor(out=kbg[:T], in0=ktok[:T], in1=bc4(6), op=ALU.mult), [ktok, g4], [kbg])
                V(lambda e: e.tensor_tensor(out=kd[:T], in0=ktok[:T], in1=bc4(5), op=ALU.mult), [ktok, g4], [kd])
                nlev = {64: 5, 32: 4}[C]

                def solve_chunk(c):
                    r0 = c * C
                    rs_ = slice(r0, r0 + C)
                    Mc, MTc, Xc = McC[c], MTcC[c], XcC[c]
                    pk = pbank()
                    for h in range(4):
                        P(lambda e: e.matmul(pk[rs_, h * 64:h * 64 + C], lhsT=kTg[:, h, rs_], rhs=kTg[:, h, rs_], start=True, stop=True), [kTg], [pk])
                    V(lambda e: e.tensor_tensor(out=Mc[rs_, :, :C], in0=v4(pk, 4, 64, rs_)[:, :, :C], in1=decS[rs_, :, rs_], op=ALU.mult), [pk, decS], [Mc])
                    V(lambda e: e.tensor_tensor(out=Mc[rs_, :, :C], in0=Mc[rs_, :, :C], in1=g4[rs_, 3, :].unsqueeze(2).to_broadcast([C, 4, C]), op=ALU.mult), [Mc, g4], [Mc])
                    pt = pbank()
                    for h in range(4):
                        P(lambda e: e.matmul(pt[rs_, h * 64:h * 64 + C], lhsT=Mc[rs_, h, :C], rhs=cs(0, r0, C), start=True, stop=True), [Mc, cst], [pt])
                    A(lambda e: e.copy(out=MTc[rs_, :, :C], in_=v4(pt, 4, 64, rs_)[:, :, :C]), [pt], [MTc])
                    V(lambda e: e.tensor_tensor(out=Xc[rs_, :, :C], in0=MTc[rs_, :, :C], in1=cs(0, r0, C).unsqueeze(1).to_broadcast([C, 4, C]), op=ALU.add), [MTc, cst], [Xc])
                    for lev in range(nlev):
                        last = lev == nlev - 1
                        p1 = pbank()
                        for h in range(4):
                            P(lambda e: e.matmul(p1[rs_, h * 64:h * 64 + C], lhsT=MTc[rs_, h, :C], rhs=Mc[rs_, h, :C], start=True, stop=True), [MTc, Mc], [p1])
                        if not last:
                            p2 = pbank()
                            for h in range(4):
                                P(lambda e: e.matmul(p2[rs_, h * 64:h * 64 + C], lhsT=Mc[rs_, h, :C], rhs=MTc[rs_, h, :C], start=True, stop=True), [MTc, Mc], [p2])
                        A(lambda e: e.copy(out=Mc[rs_, :, :C], in_=v4(p1, 4, 64, rs_)[:, :, :C]), [p1], [Mc])
                        if not last:
                            V(lambda e: e.tensor_copy(out=MTc[rs_, :, :C], in_=v4(p2, 4, 64, rs_)[:, :, :C]), [p2], [MTc])
                        p3 = pbank()
                        for h in range(4):
                            P(lambda e: e.matmul(p3[rs_, h * 64:h * 64 + C], lhsT=Mc[rs_, h, :C], rhs=Xc[rs_, h, :C], start=True, stop=True), [Mc, Xc], [p3])
                        V(lambda e: e.tensor_tensor(out=Xc[rs_, :, :C], in0=Xc[rs_, :, :C], in1=v4(p3, 4, 64, rs_)[:, :, :C], op=ALU.add), [Xc, p3], [Xc])

                if nch == 2:
                    fw.co.fork([lambda: solve_chunk(0), lambda: solve_chunk(1)], [4, 4])
                else:
                    solve_chunk(0)
                for c in range(nch):
                    r0 = c * C
                    rs_ = slice(r0, r0 + C)
                    Xc = XcC[c]
                    pw = pbank()
                    for h in range(4):
                        P(lambda e: e.matmul(pw[:, h * 64:h * 64 + C], lhsT=kbg[rs_, h, :], rhs=Xc[rs_, h, :C], start=True, stop=True), [kbg, Xc], [pw])
                    A(lambda e: e.mul(out=nwT[:, :, :C], in_=v4(pw, 4, 64)[:, :, :C], mul=-1.0), [pw], [nwT])
                    pv = pbank()
                    for h in range(4):
                        P(lambda e: e.matmul(pv[rs_, h * 128:(h + 1) * 128], lhsT=Xc[rs_, h, :C], rhs=vb[rs_, h, :], start=True, stop=False), [Xc, vb], [pv])
                        P(lambda e: e.matmul(pv[rs_, h * 128:(h + 1) * 128], lhsT=nwT[:, h, :C], rhs=SD[:, h, :], start=False, stop=True), [nwT, SD], [pv])
                    A(lambda e: e.copy(out=vnew[rs_], in_=v4(pv, rows=rs_)), [pv], [vnew])
                    if full:
                        pq = pbank()
                        for h in range(4):
                            P(lambda e: e.matmul(pq[rs_, h * 64:h * 64 + C], lhsT=kTg[:, h, rs_], rhs=qT[:, h, rs_], start=True, stop=True), [kTg, qT], [pq])
                        V(lambda e: e.tensor_tensor(out=ATc[rs_, :, :C], in0=v4(pq, 4, 64, rs_)[:, :, :C], in1=decT[rs_, :, rs_], op=ALU.mult), [pq, decT], [ATc])
                        po2 = pbank()
                        for h in range(4):
                            P(lambda e: e.matmul(po2[rs_, h * 128:(h + 1) * 128], lhsT=qgT[:, h, rs_], rhs=SD[:, h, :], start=True, stop=False), [qgT, SD], [po2])
                            P(lambda e: e.matmul(po2[rs_, h * 128:(h + 1) * 128], lhsT=ATc[rs_, h, :C], rhs=vnew[rs_, h, :], start=False, stop=True), [ATc, vnew], [po2])
                        A(lambda e: e.copy(out=o12h[1][rs_, :, :], in_=v4(po2, rows=rs_)), [po2], [o12h[1]])
                    ps2 = pbank()
                    for h in range(4):
                        P(lambda e: e.matmul(ps2[:, h * 128:(h + 1) * 128], lhsT=kd[rs_, h, :], rhs=vnew[rs_, h, :], start=True, stop=True), [kd, vnew], [ps2])
                    V(lambda e: e.tensor_tensor(out=SD[:], in0=SD[:], in1=egl2[:, :, c:c + 1].to_broadcast([128, 4, 128]), op=ALU.mult), [SD, egl2], [SD])
                    V(lambda e: e.tensor_tensor(out=SD[:], in0=SD[:], in1=v4(ps2), op=ALU.add), [SD, ps2], [SD])
                if not full:
                    return
                gated_norm(1, gnw2, sdz, T, 512)

            fw.co.fork([gla_chain, gdn_chain], [3, 6])
            if True:
                if not full:
                    return
                if dname:
                    dump(dname + "_mix", mixb, mixb[:T, :], [T, D])

            fw.barrier()
            fw.dma("sp", lambda e: e.dma_start(out=wout[:, :, :], in_=wo_b), reads=[scr_wo], writes=[wout], owner=wout)
            for k in range(0, 8, 2):
                fw.dma("sp", lambda e: e.dma_start(out=wq[:, k:k + 2, :], in_=wq_b[:, k:k + 2, :]), reads=[scr_wq], writes=[wq], owner=wq)
            transpose_bf(mixb, mixT, T)
            for n in range(2):
                pp = pbank()
                for k in range(8):
                    P(lambda e: e.matmul(pp[:T, :], lhsT=mixT[:, k, :T], rhs=wout[:, k, n * 512:(n + 1) * 512], start=(k == 0), stop=(k == 7)), [mixT, wout], [pp])
                V(lambda e: e.tensor_tensor(out=x2[:T, n * 512:(n + 1) * 512], in0=xt[:T, n * 512:(n + 1) * 512], in1=pp[:T, :], op=ALU.add), [xt, pp], [x2])
            if dname:
                dump(dname + "_x2", x2, x2[:T, :], [T, D])

            if limit < 3:
                fw.barrier()
                return
            rmsnorm_rs(x2, T)
            V(lambda e: e.scalar_tensor_tensor(out=h2b[:T, :], in0=x2[:T, :], scalar=ss[:T, 0:1], in1=nfw[:T, :], op0=ALU.mult, op1=ALU.mult), [x2, ss, nfw], [h2b])
            transpose_bf(h2b, hT, T)
            for g in range(4):
                pp = pbank()
                for j in range(4):
                    cb = g * 4 + j
                    for k in range(8):
                        P(lambda e: e.matmul(pp[:, j * 128:j * 128 + T], lhsT=wq[:, k, cb * 128:(cb + 1) * 128], rhs=hT[:, k, :T], start=(k == 0), stop=(k == 7)), [wq, hT], [pp])
                (A if g % 2 else V)(lambda e: (e.copy if g % 2 else e.tensor_copy)(pqT[:, g * 4:(g + 1) * 4, :T], v4(pp)[:, :, :T]), [pp], [pqT])
            for half in range(2):
                for g in range(2):
                    pp = pbank()
                    for j in range(4):
                        h = g * 4 + j
                        P(lambda e: e.matmul(pp[:T, j * 128:(j + 1) * 128], lhsT=pqT[:, 2 * h + half, :T], rhs=kT[:, half * 8 + h, :], start=True, stop=True), [pqT, kT], [pp])
                    (A if g % 2 else V)(lambda e: (e.copy if g % 2 else e.tensor_copy)(s12[:T, half * 8 + g * 4:half * 8 + g * 4 + 4, :], v4(pp, rows=slice(0, T))), [pp], [s12])
            for s in range(16):
                V(lambda e: e.max(out=v12[:T, s, 0:8], in_=s12[:T, s, :]), [s12], [v12])
                V(lambda e: e.max_index(out=i12[:T, s, 0:8], in_max=v12[:T, s, 0:8], in_values=s12[:T, s, :]), [s12, v12], [i12])
                V(lambda e: e.match_replace(out=wk[:T, 0:128], in_to_replace=v12[:T, s, 0:8], in_values=s12[:T, s, :], imm_value=NEG), [s12, v12], [wk])
                V(lambda e: e.max(out=v12[:T, s, 8:16], in_=wk[:T, 0:128]), [wk], [v12])
                V(lambda e: e.max_index(out=i12[:T, s, 8:16], in_max=v12[:T, s, 8:16], in_values=wk[:T, 0:128]), [wk, v12], [i12])
            V(lambda e: e.tensor_copy(out=i12f[:T], in_=i12[:T]), [i12], [i12f])
            V(lambda e: e.tensor_scalar(out=i12f[:T, 0:8, :], in0=i12f[:T, 0:8, :], scalar1=128.0, scalar2=None, op0=ALU.mult), [i12f], [i12f])
            c4 = lambda t_: t_[:T].rearrange("p h (a b) -> p h a b", b=16)
            b1 = lambda t_: t_[:T, 0:8, :].unsqueeze(3).to_broadcast([T, 8, 16, 16])
            b2 = lambda t_: t_[:T, 8:16, :].unsqueeze(2).to_broadcast([T, 8, 16, 16])
            V(lambda e: e.tensor_tensor(out=c4(cand), in0=b1(v12), in1=b2(v12), op=ALU.add), [v12], [cand])
            pos = i12[:T, 0:8, :]
            pau = i12[:T, 8:16, :]
            for h in range(8):
                V(lambda e: e.max(out=sc[:T, h, 0:8], in_=cand[:T, h, :]), [cand], [sc])
                V(lambda e: e.max_index(out=i12[:T, h, 0:8], in_max=sc[:T, h, 0:8], in_values=cand[:T, h, :]), [cand, sc], [i12])
                V(lambda e: e.match_replace(out=wk[:T, :], in_to_replace=sc[:T, h, 0:8], in_values=cand[:T, h, :], imm_value=NEG), [cand, sc], [wk])
                V(lambda e: e.max(out=sc[:T, h, 8:16], in_=wk[:T, :]), [wk], [sc])
                V(lambda e: e.max_index(out=i12[:T, h, 8:16], in_max=sc[:T, h, 8:16], in_values=wk[:T, :]), [wk, sc], [i12])
            V(lambda e: e.tensor_single_scalar(out=pau, in_=pos, scalar=4, op=ALU.logical_shift_right), [i12], [i12])
            V(lambda e: e.tensor_single_scalar(out=pos, in_=pos, scalar=15, op=ALU.bitwise_and), [i12], [i12])
            V(lambda e: e.tensor_copy(out=v12[:T, 0:8, :], in_=pau), [i12], [v12])
            V(lambda e: e.tensor_copy(out=v12[:T, 8:16, :], in_=pos), [i12], [v12])
            iota4 = cst[:T, 6, 16:32].unsqueeze(1).unsqueeze(1).to_broadcast([T, 8, 16, 16])
            sel = [eidf[:T, :].rearrange("p (a b) -> p a b", b=16), wk[:T, 0:128].rearrange("p (a b) -> p a b", b=16)]
            for w_ in range(2):
                V(lambda e: e.tensor_tensor(out=c4(cidx), in0=v12[:T, w_ * 8:(w_ + 1) * 8, :].unsqueeze(3).to_broadcast([T, 8, 16, 16]), in1=iota4, op=ALU.is_equal), [v12, cst], [cidx])
                V(lambda e: e.tensor_tensor(out=c4(cidx), in0=c4(cidx), in1=i12f[:T, w_ * 8:(w_ + 1) * 8, :].unsqueeze(2).to_broadcast([T, 8, 16, 16]), op=ALU.mult), [cidx, i12f], [cidx])
                V(lambda e: e.tensor_reduce(out=sel[w_], in_=c4(cidx), axis=AX.X, op=ALU.add), [cidx], [eidf if w_ == 0 else wk])
            V(lambda e: e.tensor_tensor(out=eidf[:T, :], in0=eidf[:T, :], in1=wk[:T, 0:128], op=ALU.add), [eidf, wk], [eidf])
            V(lambda e: e.tensor_scalar(out=eidf[:T, :], in0=eidf[:T, :], scalar1=16383.0, scalar2=None, op0=ALU.min), [eidf], [eidf])
            V(lambda e: e.tensor_copy(out=eid[:T, :], in_=eidf[:T, :]), [eidf], [eid])
            V(lambda e: e.tensor_tensor(out=gate[:T], in0=sc[:T], in1=sc[:T, :, 0:1].to_broadcast([T, 8, 16]), op=ALU.subtract), [sc], [gate])
            A(lambda e: e.activation(out=gate[:T], in_=gate[:T], func=AF.Exp), [gate], [gate])
            V(lambda e: e.tensor_reduce(out=zz[:T, :], in_=gate[:T], axis=AX.X, op=ALU.add), [gate], [zz])
            V(lambda e: e.reciprocal(out=zz[:T, :], in_=zz[:T, :]), [zz], [zz])
            V(lambda e: e.tensor_tensor(out=gate[:T], in0=gate[:T], in1=zz[:T, :].unsqueeze(2).to_broadcast([T, 8, 16]), op=ALU.mult), [gate, zz], [gate])
            fw.barrier()

        def tile_G(par, T, ydst):
            x2, h2b, eid, gate = x2s[par], h2bs[par], eids[par], gates[par]
            LA = NSLOT - 2
            slot = {}

            def gather(i, table):
                r_ = ring[rctr[0] % NSLOT]
                rctr[0] += 1
                slot[i] = r_
                fw.dma("pool", lambda e: e.indirect_dma_start(out=r_[:T, :], out_offset=None, in_=table,
                                                              in_offset=IndirectOffsetOnAxis(ap=eid[:T, i:i + 1], axis=0)), reads=[eid], writes=[r_])
            for i in range(128 + LA):
                if i < 128:
                    gather(i, peer_u)
                j = i - LA
                if j >= 0:
                    u_ = slot.pop(j)
                    V(lambda e: e.scalar_tensor_tensor(out=u_[:T, :], in0=u_[:T, :], scalar=1.0, in1=h2b[:T, :], op0=ALU.mult, op1=ALU.mult,
                                                       accum_out=actv[:T, j:j + 1]), [u_, h2b], [u_, actv])
            V(lambda e: e.tensor_tensor(out=tmpa[:T, :], in0=actv[:T, :], in1=actv[:T, :], op=ALU.mult), [actv], [tmpa])
            V(lambda e: e.tensor_scalar(out=tmpa[:T, :], in0=tmpa[:T, :], scalar1=0.044715, scalar2=1.0, op0=ALU.mult, op1=ALU.add), [tmpa], [tmpa])
            V(lambda e: e.tensor_tensor(out=tmpa[:T, :], in0=tmpa[:T, :], in1=actv[:T, :], op=ALU.mult), [tmpa, actv], [tmpa])
            A(lambda e: e.activation(out=tmpa[:T, :], in_=tmpa[:T, :], func=AF.Sigmoid, scale=1.5957691216057308), [tmpa], [tmpa])
            V(lambda e: e.tensor_tensor(out=wgt[:T, :], in0=tmpa[:T, :], in1=actv[:T, :], op=ALU.mult), [tmpa, actv], [wgt])
            V(lambda e: e.tensor_tensor(out=wgt[:T, :], in0=wgt[:T, :], in1=gate[:T].rearrange("p a b -> p (a b)"), op=ALU.mult), [wgt, gate], [wgt])
            py = [pbanks[5], pbanks[6]]
            for i in range(128 + LA):
                if i < 128:
                    gather(i, peer_v)
                j = i - LA
                if j >= 0:
                    v_ = slot.pop(j)
                    d_ = dg[j % 4]
                    A(lambda e: e.activation(out=d_[:T, :T], in_=cs(0, 0, T), func=AF.Copy, scale=wgt[:T, j:j + 1]), [cst, wgt], [d_])
                    for n in range(2):
                        P(lambda e: e.matmul(py[n][:T, :], lhsT=d_[:T, :T], rhs=v_[:T, n * 512:(n + 1) * 512], start=(j == 0), stop=(j == 127)), [d_, v_], [py[n]])
            for n in range(2):
                V(lambda e: e.tensor_tensor(out=x2[:T, n * 512:(n + 1) * 512], in0=x2[:T, n * 512:(n + 1) * 512], in1=py[n][:T, :], op=ALU.add), [x2, py[n]], [x2])
            A(lambda e: e.activation(out=h2b[:T, :], in_=x2[:T, :], func=AF.Square, accum_out=ssg[:T, :]), [x2], [h2b, ssg])
            A(lambda e: e.activation(out=ssg[:T, :], in_=ssg[:T, :], func=AF.Sqrt, scale=1.0 / D, bias=EPS), [ssg], [ssg])
            V(lambda e: e.reciprocal(out=ssg[:T, :], in_=ssg[:T, :]), [ssg], [ssg])
            V(lambda e: e.scalar_tensor_tensor(out=x2[:T, :], in0=x2[:T, :], scalar=ssg[:T, 0:1], in1=nzw[:T, :], op0=ALU.mult, op1=ALU.mult), [x2, ssg, nzw], [x2])
            fw.dma("sp", lambda e: e.dma_start(out=ydst, in_=x2[:T, :]), reads=[x2])

        def store_states(sk, o_gla, o_gdn, o_conv, T):
            gl = o_gla.rearrange("h k v -> (h k) v")
            for p_ in range(2):
                fw.dma("sp", lambda e: e.dma_start(out=gl[p_ * 128:(p_ + 1) * 128, :], in_=Sgla[sk][:, p_, :]), reads=[Sgla[sk]])
            fw.dma("sp", lambda e: e.dma_start(out=o_gdn.rearrange("h k v -> k h v"), in_=Sgdn[sk][:]), reads=[Sgdn[sk]])
            for r in range(3):
                for b0_ in range(0, 12, 3):
                    fw.dma("sp", lambda e: e.dma_start(out=o_conv[r:r + 1, :].rearrange("o (b p) -> p b o", p=128)[:, b0_:b0_ + 3, :], in_=convin[:, b0_:b0_ + 3, r:r + 1],
                                                       allow_slow_non_contiguous=True), reads=[convin])

        KM, KG = 4, 4
        pend = [None]
        fullctr = [0]

        def run_pair(Mf):
            if pend[0] is not None:
                fw.co.fork([Mf, pend[0]], [KM, KG])
            else:
                Mf()

        for i in range(NPRE):
            do_tile(xpre[i * 128:(i + 1) * 128, :], 128, 64, 2, False, "p", need_q=(i == NPRE - 1))
        for i in range(NMAIN):
            par = fullctr[0] % 2
            fullctr[0] += 1
            run_pair(lambda i=i, par=par: do_tile(xmain[i * 128:(i + 1) * 128, :], 128, 64, 2, True, "p", par, dname=("m%d" % i) if dbg else None))
            pend[0] = (lambda i=i, par=par: tile_G(par, 128, y_main[i * 128:(i + 1) * 128, :])) if limit >= 3 else None
        store_states("p", gla_st, gdn_st, conv_st, 128)
        for j in range(NSAMP):
            sk = "s%d" % j
            gl = sgla_in[j].rearrange("h k v -> (h k) v")
            for p_ in range(2):
                fw.dma("sp", lambda e: e.dma_start(out=Sgla[sk][:, p_, :], in_=gl[p_ * 128:(p_ + 1) * 128, :]), writes=[Sgla[sk]])
            fw.dma("sp", lambda e: e.dma_start(out=Sgdn[sk][:], in_=sgdn_in[j].rearrange("h k v -> k h v")), writes=[Sgdn[sk]])
            for r in range(3):
                for b0_ in range(0, 12, 3):
                    fw.dma("sp", lambda e: e.dma_start(out=convin[:, b0_:b0_ + 3, r:r + 1], in_=sconv_in[j, r:r + 1, :].rearrange("o (b p) -> p b o", p=128)[:, b0_:b0_ + 3, :],
                                                       allow_slow_non_contiguous=True), writes=[convin])
            par = fullctr[0] % 2
            fullctr[0] += 1
            run_pair(lambda j=j, par=par, sk=sk: do_tile(xs[j * 32:(j + 1) * 32, :], 32, 32, 1, True, sk, par, dname=("s%d" % j) if dbg else None))
            pend[0] = (lambda j=j, par=par: tile_G(par, 32, y_s[j * 32:(j + 1) * 32, :])) if limit >= 3 else None
            store_states(sk, gla_s[j], gdn_s[j], conv_s[j], 32)
        if pend[0] is not None:
            pend[0]()
        fw.finish("sp")
        print("instructions:", fw.nins, {e: fw.cnt[e] for e in fw.ENGS})
    return nc


N_CORES = 8
WNAMES = ["norm_mix_w", "w_in", "gla_w_gk2", "gla_b_gk", "gla_norm_w", "gdn_conv_w", "gdn_a_log", "gdn_dt_bias",
          "gdn_norm_w", "w_out", "norm_ffn_w", "peer_wq", "peer_k1", "peer_k2", "peer_u", "peer_v"]


def kernel(x_prompt, x_sample, state_gla, state_gdn, state_gdn_conv, norm_mix_w, w_in, gla_w_gk2, gla_b_gk,
           gla_norm_w, gdn_conv_w, gdn_a_log, gdn_dt_bias, gdn_norm_w, w_out, norm_ffn_w, peer_wq, peer_k1,
           peer_k2, peer_u, peer_v, norm_final_w):
    f = lambda a: np.ascontiguousarray(np.asarray(a, dtype=np.float32))
    x_prompt, x_sample = f(x_prompt), f(x_sample)
    B, L, _ = x_prompt.shape
    HALF = L // 2
    NT = HALF // 128
    loc = dict(norm_mix_w=norm_mix_w, w_in=w_in, gla_w_gk2=gla_w_gk2, gla_b_gk=gla_b_gk, gla_norm_w=gla_norm_w,
               gdn_conv_w=gdn_conv_w, gdn_a_log=gdn_a_log, gdn_dt_bias=gdn_dt_bias, gdn_norm_w=gdn_norm_w, w_out=w_out,
               norm_ffn_w=norm_ffn_w, peer_wq=peer_wq, peer_k1=peer_k1, peer_k2=peer_k2, peer_u=peer_u, peer_v=peer_v)
    shared = {k: f(v)[0] for k, v in loc.items()}
    shared["norm_final_w"] = f(norm_final_w)
    shared["cst"] = make_consts()
    nc = build(NT, NT, 2)
    in_maps = []
    zeros = np.zeros((HALF, D), np.float32)
    for c in range(N_CORES):
        b, half = c // 2, c % 2
        m = dict(shared)
        m["xpre"] = x_prompt[b, :HALF] if half == 1 else zeros
        m["xmain"] = x_prompt[b, half * HALF:(half + 1) * HALF]
        m["xs"] = x_sample[2 * c:2 * c + 2].reshape(64, D)
        m["sgla_in"] = f(state_gla)[0, 2 * c:2 * c + 2]
        m["sgdn_in"] = f(state_gdn)[0, 2 * c:2 * c + 2]
        m["sconv_in"] = f(state_gdn_conv)[0, 2 * c:2 * c + 2]
        in_maps.append({k: np.ascontiguousarray(v) for k, v in m.items()})
    res = run_bass_kernel_spmd(nc, in_maps, core_ids=list(range(N_CORES))).results
    y_prompt = np.zeros((B, L, D), np.float32)
    y_sample = np.zeros((16, 32, D), np.float32)
    gla_p = np.zeros((1, B, 4, 64, 128), np.float32)
    gdn_p = np.zeros((1, B, 4, 128, 128), np.float32)
    conv_p = np.zeros((1, B, 3, 1536), np.float32)
    gla_sm = np.zeros((1, 16, 4, 64, 128), np.float32)
    gdn_sm = np.zeros((1, 16, 4, 128, 128), np.float32)
    conv_sm = np.zeros((1, 16, 3, 1536), np.float32)
    for c in range(N_CORES):
        b, half = c // 2, c % 2
        r = res[c]
        y_prompt[b, half * HALF:(half + 1) * HALF] = r["y_main"]
        y_sample[2 * c:2 * c + 2] = r["y_s"].reshape(2, 32, D)
        if half == 1:
            gla_p[0, b] = r["gla_st"]
            gdn_p[0, b] = r["gdn_st"]
            conv_p[0, b] = r["conv_st"]
        gla_sm[0, 2 * c:2 * c + 2] = r["gla_s"]
        gdn_sm[0, 2 * c:2 * c + 2] = r["gdn_s"]
        conv_sm[0, 2 * c:2 * c + 2] = r["conv_s"]
    return (y_prompt, y_sample, gla_p, gdn_p, conv_p, gla_sm, gdn_sm, conv_sm)
```

```python
from contextlib import ExitStack
import threading
import numpy as np
import concourse.bass as bass
import concourse.mybir as mybir
from concourse.bass import IndirectOffsetOnAxis
from concourse.bass_utils import run_bass_kernel_spmd

F32 = mybir.dt.float32
BF16 = mybir.dt.bfloat16
U32 = mybir.dt.uint32
AF = mybir.ActivationFunctionType
ALU = mybir.AluOpType
AX = mybir.AxisListType

D = 1024
NIN = 3608
O_GQ, O_GK, O_GV, O_GLR, O_GG, O_DQKV, O_DA, O_DB, O_DZ = 0, 256, 512, 1024, 1040, 1552, 3088, 3092, 3096
EPS = 1e-6
NEG = -1e30


class Buf:
    __slots__ = ("name", "w", "r", "dsem", "dcnt", "excl")

    def __init__(self, name):
        self.name = name
        self.excl = False
        self.w = None
        self.r = {}
        self.dsem = None
        self.dcnt = 0


class TL:
    def __init__(self, t, b):
        self.t = t
        self.b = b

    def __getitem__(self, k):
        return self.t[k]


class Co:
    def __init__(self):
        self.slots = {}
        self.order = []
        self.n = 0
        self.nid = 0
        self.err = []
        self.t2s = {}

    def _new(self, k, parent):
        sid = self.nid
        self.nid += 1
        self.slots[sid] = dict(sem=threading.Semaphore(0), k=k, run=True, parent=parent, nchild=0)
        return sid

    def _pick(self, after):
        n = len(self.order)
        st = self.order.index(after)
        for d in range(1, n + 1):
            sid = self.order[(st + d) % n]
            if self.slots[sid]["run"]:
                return sid
        return None

    def fork(self, fns, ks):
        tid = threading.get_ident()
        me = self.t2s.get(tid)
        if me is None:
            me = self._new(1, None)
            self.t2s[tid] = me
            self.order.append(me)
        par = self.slots[me]
        par["run"] = False
        par["nchild"] = len(fns)
        pos = self.order.index(me)
        kids = []
        for i, (f, k) in enumerate(zip(fns, ks)):
            sid = self._new(k, me)
            kids.append(sid)
            self.order.insert(pos + 1 + i, sid)
            threading.Thread(target=self._wrap, args=(sid, f), daemon=True).start()
        self.n = 0
        self.slots[kids[0]]["sem"].release()
        par["sem"].acquire()
        if par["parent"] is None and self.err:
            e = self.err[0]
            self.err = []
            raise e

    def _wrap(self, sid, f):
        sl = self.slots[sid]
        sl["sem"].acquire()
        self.t2s[threading.get_ident()] = sid
        try:
            f()
        except BaseException as ex:
            self.err.append(ex)
        finally:
            sl["run"] = False
            par = self.slots[sl["parent"]]
            par["nchild"] -= 1
            if par["nchild"] == 0:
                par["run"] = True
            nxt = self._pick(sid)
            self.order.remove(sid)
            self.n = 0
            self.slots[nxt]["sem"].release()

    def tick(self):
        sid = self.t2s.get(threading.get_ident())
        if sid is None or len(self.order) <= 1:
            return
        self.n += 1
        if self.n >= self.slots[sid]["k"]:
            self.n = 0
            nxt = self._pick(sid)
            if nxt is not None and nxt != sid:
                self.slots[nxt]["sem"].release()
                self.slots[sid]["sem"].acquire()


class PEProxy:
    def __init__(self, fw, pe):
        self.fw = fw
        self.pe = pe
        self.mode = None

    @staticmethod
    def _r(n):
        return 32 if n <= 32 else (64 if n <= 64 else 128)

    def _sw(self, lhsT):
        shp = lhsT.shape
        mode = (self._r(shp[0]), self._r(int(np.prod(shp[1:]))))
        if self.mode is not None and mode != self.mode:
            fw = self.fw
            if fw.cnt["pe"] > 0 and fw.seen["pe"].get("pe", 0) < fw.cnt["pe"]:
                self.pe.wait_ge(fw.sem["pe"], fw.cnt["pe"])
                fw.seen["pe"]["pe"] = fw.cnt["pe"]
        self.mode = mode

    def matmul(self, out, lhsT, rhs, **kw):
        self._sw(lhsT)
        return self.pe.matmul(out, lhsT=lhsT, rhs=rhs, **kw)

    def transpose(self, out, in_, identity, **kw):
        self._sw(in_)
        return self.pe.transpose(out=out, in_=in_, identity=identity, **kw)


class FW:
    ENGS = ("pe", "dve", "act", "pool", "sp")

    def __init__(self, nc, stack):
        self.nc = nc
        self.stack = stack
        self.eng = {"pe": nc.tensor, "dve": nc.vector, "act": nc.scalar, "pool": nc.gpsimd, "sp": nc.sync}
        self.pex = PEProxy(self, nc.tensor)
        self.sem = {e: stack.enter_context(nc.semaphore("s_" + e)) for e in self.ENGS}
        self.cnt = {e: 0 for e in self.ENGS}
        self.seen = {e: {} for e in self.ENGS}
        self.dseen = {e: {} for e in self.ENGS}
        self.nins = 0
        self.dbufs = []
        self.co = Co()

    def sb(self, name, shape, dt=F32, dma=False):
        t = self.stack.enter_context(self.nc.sbuf_tensor(name, list(shape), dt))
        return TL(t, self.buf(name, dma))

    def ps(self, name, shape, dt=F32):
        t = self.stack.enter_context(self.nc.psum_tensor(name, list(shape), dt))
        tl = TL(t, self.buf(name))
        tl.b.excl = True
        return tl

    def buf(self, name, dma=False):
        b = Buf(name)
        if dma:
            b.dsem = self.stack.enter_context(self.nc.semaphore("d_" + name))
            self.dbufs.append(b)
        return b

    def _wait(self, e, other, idx):
        if idx <= 0 or self.seen[e].get(other, 0) >= idx:
            return
        self.eng[e].wait_ge(self.sem[other], idx)
        self.seen[e][other] = idx

    def _wait_dma(self, e, b):
        if b.dsem is None or b.dcnt == 0 or self.dseen[e].get(b.name, 0) >= b.dcnt:
            return
        self.eng[e].wait_ge(b.dsem, b.dcnt)
        self.dseen[e][b.name] = b.dcnt

    def _deps(self, e, reads, writes):
        for b in reads:
            if b.w is not None:
                self._wait(e, b.w[0], b.w[1])
            if b.excl:
                for re_, ri in b.r.items():
                    if re_ != e:
                        self._wait(e, re_, ri)
            self._wait_dma(e, b)
        for b in writes:
            if b.w is not None:
                self._wait(e, b.w[0], b.w[1])
            for re_, ri in b.r.items():
                self._wait(e, re_, ri)
            self._wait_dma(e, b)

    def op(self, e, fn, reads=(), writes=()):
        reads = [x.b if isinstance(x, TL) else x for x in reads]
        writes = [x.b if isinstance(x, TL) else x for x in writes]
        self._deps(e, reads, writes)
        ins = fn(self.pex if e == "pe" else self.eng[e])
        self.cnt[e] += 1
        idx = self.cnt[e]
        ins.then_inc(self.sem[e], 1)
        self.nins += 1
        for b in reads:
            b.r[e] = idx
        for b in writes:
            b.w = (e, idx)
            b.r = {}
        self.co.tick()
        return ins

    def dma(self, e, fn, reads=(), writes=(), owner=None):
        reads = [x.b if isinstance(x, TL) else x for x in reads]
        writes = [x.b if isinstance(x, TL) else x for x in writes]
        self._deps(e, reads, writes)
        ins = fn(self.eng[e])
        self.nins += 1
        if owner is not None:
            b = owner.b if isinstance(owner, TL) else owner
        else:
            sems = [b for b in reads + writes if b.dsem is not None]
            assert len(sems) == 1, [b.name for b in reads + writes]
            b = sems[0]
        b.dcnt += 16
        ins.then_inc(b.dsem, 16)
        self.co.tick()
        return ins

    def barrier(self):
        for e in self.ENGS:
            for o in self.ENGS:
                if o != e:
                    self._wait(e, o, self.cnt[o])
            for b in self.dbufs:
                self._wait_dma(e, b)

    def finish(self, e="sp"):
        for b in self.dbufs:
            self._wait_dma(e, b)
        for o in self.ENGS:
            if o != e:
                self._wait(e, o, self.cnt[o])


def make_consts():
    c = np.zeros((128, 7, 128), np.float32)
    i = np.arange(128)
    same = (i[:, None] // 64) == (i[None, :] // 64)
    c[:, 0, :] = np.eye(128)
    c[:, 1, :] = (i[:, None] <= i[None, :]) & same
    c[:, 2, :] = (i[:, None] > i[None, :]) & same
    c[:, 3, :] = (i[None, :] < i[:, None]) & same
    c[:, 4, :] = (i[:, None] <= i[None, :]) & same
    c[:, 5, :] = 1.0
    c[:64, 6, 0] = 1.0
    c[64:, 6, 1] = 1.0
    c[:, 6, 16:32] = np.arange(16, dtype=np.float32)[None, :]
    return c


def build(NPRE, NMAIN, NSAMP=2, dbg=False, limit=99):
    nc = bass.Bass("TRN2", target_bir_lowering=False)
    dt_in = lambda n, s: nc.dram_tensor(n, list(s), F32, kind="ExternalInput").ap()
    dt_out = lambda n, s: nc.dram_tensor(n, list(s), F32, kind="ExternalOutput").ap()
    xpre = dt_in("xpre", [max(NPRE, 1) * 128, D])
    xmain = dt_in("xmain", [NMAIN * 128, D])
    xs = dt_in("xs", [NSAMP * 32, D])
    sgla_in = dt_in("sgla_in", [NSAMP, 4, 64, 128])
    sgdn_in = dt_in("sgdn_in", [NSAMP, 4, 128, 128])
    sconv_in = dt_in("sconv_in", [NSAMP, 3, 1536])
    cst_in = dt_in("cst", [128, 7, 128])
    norm_mix_w = dt_in("norm_mix_w", [D])
    w_in = dt_in("w_in", [D, NIN])
    gla_w_gk2 = dt_in("gla_w_gk2", [16, 256])
    gla_b_gk = dt_in("gla_b_gk", [256])
    gla_norm_w = dt_in("gla_norm_w", [128])
    gdn_conv_w = dt_in("gdn_conv_w", [4, 1536])
    gdn_a_log = dt_in("gdn_a_log", [4])
    gdn_dt_bias = dt_in("gdn_dt_bias", [4])
    gdn_norm_w = dt_in("gdn_norm_w", [128])
    w_out = dt_in("w_out", [D, D])
    norm_ffn_w = dt_in("norm_ffn_w", [D])
    peer_wq = dt_in("peer_wq", [D, 2048])
    peer_k1 = dt_in("peer_k1", [8, 128, 128])
    peer_k2 = dt_in("peer_k2", [8, 128, 128])
    peer_u = dt_in("peer_u", [16384, D])
    peer_v = dt_in("peer_v", [16384, D])
    norm_final_w = dt_in("norm_final_w", [D])

    y_main = dt_out("y_main", [NMAIN * 128, D])
    y_s = dt_out("y_s", [NSAMP * 32, D])
    gla_st = dt_out("gla_st", [4, 64, 128])
    gdn_st = dt_out("gdn_st", [4, 128, 128])
    conv_st = dt_out("conv_st", [3, 1536])
    gla_s = dt_out("gla_s", [NSAMP, 4, 64, 128])
    gdn_s = dt_out("gdn_s", [NSAMP, 4, 128, 128])
    conv_s = dt_out("conv_s", [NSAMP, 3, 1536])
    dbg_outs = {}

    with ExitStack() as st:
        fw = FW(nc, st)
        V = lambda fn, r, w: fw.op("dve", fn, r, w)
        A = lambda fn, r, w: fw.op("act", fn, r, w)
        P = lambda fn, r, w: fw.op("pe", fn, r, w)
        GP = lambda fn, r, w: fw.op("pool", fn, r, w)

        cst = fw.sb("cstsb", [128, 7, 128], dma=True)
        fw.dma("sp", lambda e: e.dma_start(out=cst[:], in_=cst_in), writes=[cst])
        IDN, U1, LST, STRICT, TRIU, ONES, BLK = (cst[:, i, :] for i in range(7))

        def cs(i, r0, n, c0=None, m=None):
            c0 = r0 if c0 is None else c0
            m = n if m is None else m
            return cst[r0:r0 + n, i, c0:c0 + m]

        win = fw.sb("win", [128, 8, NIN], BF16)
        wi_v = w_in.rearrange("(k p) n -> p k n", p=128)
        wo_v = w_out.rearrange("(k p) n -> p k n", p=128)
        wq_v = peer_wq.rearrange("(k p) n -> p k n", p=128)
        nmwT = fw.sb("nmwT", [128, 8], dma=True)
        for k in range(0, 8, 2):
            fw.dma("sp", lambda e: e.dma_start(out=nmwT[:, k:k + 2], in_=norm_mix_w.rearrange("(k p) -> p k", p=128)[:, k:k + 2], allow_slow_non_contiguous=True), writes=[nmwT])
        nfw = fw.sb("nfw", [128, D], dma=True)
        nzw = fw.sb("nzw", [128, D], dma=True)
        for t_, src in ((nfw, norm_ffn_w), (nzw, norm_final_w)):
            fw.dma("sp", lambda e: e.dma_start(out=t_[:], in_=src.partition_broadcast(128)), writes=[t_])
        gnw1 = fw.sb("gnw1", [128, 128], dma=True)
        gnw2 = fw.sb("gnw2", [128, 128], dma=True)
        fw.dma("sp", lambda e: e.dma_start(out=gnw1[:], in_=gla_norm_w.partition_broadcast(128)), writes=[gnw1])
        fw.dma("sp", lambda e: e.dma_start(out=gnw2[:], in_=gdn_norm_w.partition_broadcast(128)), writes=[gnw2])
        wgk = fw.sb("wgk", [16, 256], dma=True)
        fw.dma("sp", lambda e: e.dma_start(out=wgk[:], in_=gla_w_gk2), writes=[wgk])
        bgk = fw.sb("bgk", [1, 256], dma=True)
        fw.dma("sp", lambda e: e.dma_start(out=bgk[:], in_=gla_b_gk.rearrange("(o n) -> o n", o=1)), writes=[bgk])
        wc = fw.sb("wc", [128, 12, 4], dma=True)
        for i in range(4):
            for b0_ in range(0, 12, 3):
                fw.dma("sp", lambda e: e.dma_start(out=wc[:, b0_:b0_ + 3, i:i + 1],
                                                   in_=gdn_conv_w[i:i + 1, :].rearrange("o (b p) -> p b o", p=128)[:, b0_:b0_ + 3, :],
                                                   allow_slow_non_contiguous=True), writes=[wc])
        dtb = fw.sb("dtb", [128, 4], dma=True)
        fw.dma("sp", lambda e: e.dma_start(out=dtb[:], in_=gdn_dt_bias.partition_broadcast(128)), writes=[dtb])
        alog = fw.sb("alog", [128, 4], dma=True)
        fw.dma("sp", lambda e: e.dma_start(out=alog[:], in_=gdn_a_log.partition_broadcast(128)), writes=[alog])
        negA = fw.sb("negA", [128, 4])
        A(lambda e: e.activation(out=negA[:], in_=alog[:], func=AF.Exp), [alog], [negA])
        V(lambda e: e.tensor_scalar(out=negA[:], in0=negA[:], scalar1=-1.0, scalar2=None, op0=ALU.mult), [negA], [negA])

        pbanks = [fw.ps("pb%d" % i, [128, 512]) for i in range(7)]
        pbf = fw.ps("pbf", [128, 8, 128], BF16)
        pctr = [0]

        def pbank():
            p = pbanks[pctr[0] % 5]
            pctr[0] += 1
            return p

        def v4(p, a=4, b=128, rows=slice(None)):
            return p[rows, 0:a * b].rearrange("p (a b) -> p a b", b=b)

        ARW = 17408
        arena = st.enter_context(nc.sbuf_tensor("arena", [128, ARW], F32))
        aoff = {"m": 0, "p": 0}

        def av(phase, name, shape, dt=F32, buf=None, dma=False):
            n = int(np.prod(shape[1:]))
            words = n if dt == F32 else n // 2
            o = aoff[phase]
            aoff[phase] += words
            assert aoff[phase] <= ARW, (phase, name, aoff[phase])
            ap = arena[:, o:o + words]
            if dt != F32:
                ap = ap.bitcast(dt)
            if len(shape) == 3:
                ap = ap.rearrange("p (a b) -> p a b", b=shape[2])
            return TL(ap, buf if buf is not None else fw.buf(name, dma))

        M = lambda name, shape, dt=F32: av("m", name, shape, dt)
        gqk = M("gqk", [128, 512]); gv = M("gv", [128, 512]); sgg = M("sgg", [128, 512]); sdz = M("sdz", [128, 512])
        a1 = M("a1", [128, 256]); eG = M("eG", [128, 256]); enG = M("enG", [128, 256]); edG = M("edG", [128, 256])
        qks = M("qks", [128, 512]); kh = M("kh", [128, 256]); qkT = M("qkT", [128, 4, 128]); attS = M("attS", [128, 4, 64])
        o12 = M("o12", [128, 8, 128]); cv = M("cv", [128, 12, 128]); cvt = M("cvt", [128, 12, 128]); rq = M("rq", [128, 8, 128])
        qT = M("qT", [128, 4, 128]); kTg = M("kTg", [128, 4, 128]); qgT = M("qgT", [128, 4, 128])
        ktok = M("ktok", [128, 4, 128]); vtok = M("vtok", [128, 4, 128])
        big = M("big", [128, 4, 128]); decS = M("decS", [128, 4, 128]); decT = M("decT", [128, 4, 128])
        Mc = M("Mc", [128, 4, 64]); MTc = M("MTc", [128, 4, 64]); Xc = M("Xc", [128, 4, 64])
        vb = M("vb", [128, 4, 128]); kbg = M("kbg", [128, 4, 128]); kd = M("kd", [128, 4, 128])
        nwT = M("nwT", [128, 4, 64]); vnew = M("vnew", [128, 4, 128]); ATc = M("ATc", [128, 4, 64])
        wq = av("p", "wq", [128, 8, 2048], BF16, dma=True)
        pqT = av("p", "pqT", [128, 16, 128]); s12 = av("p", "s12", [128, 16, 128])
        wob = fw.buf("wout_cand", dma=True)
        o_c = aoff["p"]
        cand = av("p", "cand", [128, 8, 256], buf=wob); cidx = av("p", "cidx", [128, 8, 256], buf=wob)
        wout = TL(arena[:, o_c:o_c + 4096].bitcast(BF16).rearrange("p (a b) -> p a b", b=D), wob)

        xt = fw.sb("xt", [128, D], dma=True)
        kT = fw.sb("kT", [128, 16, 128])
        junk = fw.sb("junk", [128, D], BF16)
        ss = fw.sb("ss", [128, 1])
        ssg = fw.sb("ssg", [128, 1])
        xn = fw.sb("xn", [128, D], BF16)
        hT = fw.sb("hT", [128, 8, 128], BF16)
        dab = fw.sb("dab", [128, 8])
        glrT = fw.sb("glrT", [16, 128])
        convin = fw.sb("convin", [128, 12, 131], dma=True)
        egl = fw.sb("egl", [128, 2, 2])
        mixb = xn
        mixT = hT
        g4 = fw.sb("g4", [128, 10, 4])
        a2blk = fw.sb("a2blk", [128, 4, 2])
        egl2 = fw.sb("egl2", [128, 4, 2])
        x2s = [fw.sb("x2_%d" % i, [128, D], dma=True) for i in range(2)]
        h2bs = [fw.sb("h2b_%d" % i, [128, D], BF16) for i in range(2)]
        eids = [fw.sb("eid_%d" % i, [128, 128], U32) for i in range(2)]
        gates = [fw.sb("gate_%d" % i, [128, 8, 16]) for i in range(2)]
        wk = fw.sb("wk", [128, 256])
        v12 = fw.sb("v12", [128, 16, 16])
        i12 = fw.sb("i12", [128, 16, 16], U32)
        i12f = fw.sb("i12f", [128, 16, 16])
        sc = fw.sb("sc", [128, 8, 16])
        eidf = fw.sb("eidf", [128, 128])
        zz = fw.sb("zz", [128, 8])
        actv = fw.sb("actv", [128, 128])
        wgt = fw.sb("wgt", [128, 128])
        tmpa = fw.sb("tmpa", [128, 128])
        NSLOT = 8 if dbg else 9
        ring = [fw.sb("ring%d" % i, [128, D], BF16, dma=True) for i in range(NSLOT)]
        rctr = [0]
        dg = [fw.sb("dg%d" % i, [128, 128], BF16) for i in range(4)]
        _sg = fw.sb("Sgla_p", [128, 2, 128], dma=True)
        _sd = fw.sb("Sgdn_p", [128, 4, 128], dma=True)
        Sgla = {k: _sg for k in ["p"] + ["s%d" % i for i in range(NSAMP)]}
        Sgdn = {k: _sd for k in ["p"] + ["s%d" % i for i in range(NSAMP)]}
        GP(lambda e: e.memset(Sgla["p"][:], 0.0), [], [Sgla["p"]])
        GP(lambda e: e.memset(Sgdn["p"][:], 0.0), [], [Sgdn["p"]])
        GP(lambda e: e.memset(convin[:], 0.0), [], [convin])
        wstg = [TL(arena[:, i * NIN:(i + 1) * NIN], fw.buf("wstg%d" % i, dma=True)) for i in range(2)]
        for k in range(8):
            ws = wstg[k % 2]
            fw.dma("sp", lambda e: e.dma_start(out=ws[:, :], in_=wi_v[:, k, :]), writes=[ws])
            V(lambda e: e.tensor_scalar(out=win[:, k, :], in0=ws[:, :], scalar1=nmwT[:, k:k + 1], scalar2=None, op0=ALU.mult), [ws, nmwT], [win])
        fw.barrier()
        wq_b = nc.dram_tensor("wq_b", [128, 8, 2048], BF16, kind="Internal").ap()
        wo_b = nc.dram_tensor("wo_b", [128, 8, D], BF16, kind="Internal").ap()
        wq_sw = TL(wq.t, fw.buf("wq_sw", dma=True))
        wout_sw = TL(wout.t, fw.buf("wout_sw", dma=True))
        scr_wq = fw.buf("scr_wq", dma=True)
        scr_wo = fw.buf("scr_wo", dma=True)
        for k in range(8):
            fw.dma("pool", lambda e: e.dma_start(out=wq_sw[:, k, :], in_=wq_v[:, k, :]), writes=[wq_sw])
        fw.dma("sp", lambda e: e.dma_start(out=wq_b, in_=wq_sw[:, :, :]), reads=[wq_sw], writes=[scr_wq], owner=scr_wq)
        for k in range(8):
            fw.dma("pool", lambda e: e.dma_start(out=wout_sw[:, k, :], in_=wo_v[:, k, :]), writes=[wout_sw])
        fw.dma("sp", lambda e: e.dma_start(out=wo_b, in_=wout_sw[:, :, :]), reads=[wout_sw], writes=[scr_wo], owner=scr_wo)
        fw.barrier()
        kraw = TL(arena[:, 0:2048].rearrange("p (a b) -> p a b", b=128), fw.buf("kraw", dma=True))
        fw.dma("sp", lambda e: e.dma_start(out=kraw[:, 0:8, :], in_=peer_k1.rearrange("h n d -> n h d")), writes=[kraw])
        fw.dma("sp", lambda e: e.dma_start(out=kraw[:, 8:16, :], in_=peer_k2.rearrange("h n d -> n h d")), writes=[kraw])
        for g in range(4):
            pk = pbank()
            for j in range(4):
                P(lambda e: e.transpose(out=pk[:, j * 128:(j + 1) * 128], in_=kraw[:, g * 4 + j, :], identity=IDN), [kraw, cst], [pk])
            V(lambda e: e.tensor_copy(out=kT[:, g * 4:(g + 1) * 4, :], in_=v4(pk)), [pk], [kT])

        fw.barrier()

        stg = fw.sb("stg", [128, D], BF16, dma=True) if dbg else None

        def dump(name, tl, ap, shape):
            if not dbg:
                return
            if tl in x2s:
                o = nc.dram_tensor("dbg_" + name, list(shape), F32, kind="ExternalOutput").ap()
                fw.dma("sp", lambda e: e.dma_start(out=o, in_=ap), reads=[tl])
                return
            o = nc.dram_tensor("dbg_" + name, list(shape), BF16, kind="ExternalOutput").ap()
            V(lambda e: e.tensor_copy(out=stg[:shape[0], :shape[1]], in_=ap), [tl], [stg])
            fw.dma("sp", lambda e: e.dma_start(out=o, in_=stg[:shape[0], :shape[1]]), reads=[stg])

        def rmsnorm_rs(src, T):
            A(lambda e: e.activation(out=junk[:T, :], in_=src[:T, :], func=AF.Square, accum_out=ss[:T, :]), [src], [junk, ss])
            A(lambda e: e.activation(out=ss[:T, :], in_=ss[:T, :], func=AF.Sqrt, scale=1.0 / D, bias=EPS), [ss], [ss])
            V(lambda e: e.reciprocal(out=ss[:T, :], in_=ss[:T, :]), [ss], [ss])

        def transpose_bf(src, dst, T):
            for k in range(8):
                P(lambda e: e.transpose(out=pbf[:, k, :T], in_=src[:T, k * 128:(k + 1) * 128], identity=idb[:T, :T]), [src, idb], [pbf])
            A(lambda e: e.copy(out=dst[:, :, :T], in_=pbf[:, :, :T]), [pbf], [dst])

        idb = fw.sb("idb", [128, 128], BF16)
        V(lambda e: e.tensor_copy(out=idb[:], in_=IDN), [cst], [idb])

        jscr = [fw.sb("jscr%d" % i, [128, 128], BF16) for i in range(2)]
        ssq2 = [fw.sb("ssq%d" % i, [128, 4]) for i in range(2)]
        o12h = [TL(o12.t[:, 0:4, :], fw.buf("o12a")), TL(o12.t[:, 4:8, :], fw.buf("o12b"))]

        def gated_norm(which, gw, sg, T, col0):
            oo, sq_, js = o12h[which], ssq2[which], jscr[which]
            for h in range(4):
                A(lambda e: e.activation(out=js[:T, :], in_=oo[:T, h, :], func=AF.Square, accum_out=sq_[:T, h:h + 1]), [oo], [js, sq_])
            A(lambda e: e.activation(out=sq_[:T, :], in_=sq_[:T, :], func=AF.Sqrt, scale=1.0 / 128, bias=EPS), [sq_], [sq_])
            V(lambda e: e.reciprocal(out=sq_[:T, :], in_=sq_[:T, :]), [sq_], [sq_])
            mv = oo[:T, :, :]
            V(lambda e: e.tensor_tensor(out=mv, in0=mv, in1=sq_[:T, :].unsqueeze(2).to_broadcast([T, 4, 128]), op=ALU.mult), [oo, sq_], [oo])
            V(lambda e: e.tensor_tensor(out=mv, in0=mv, in1=gw[:T, :].unsqueeze(1).to_broadcast([T, 4, 128]), op=ALU.mult), [oo, gw], [oo])
            V(lambda e: e.tensor_tensor(out=mixb[:T, col0:col0 + 512].rearrange("p (a b) -> p a b", b=128), in0=mv,
                                        in1=sg[:T, :].rearrange("p (a b) -> p a b", b=128), op=ALU.mult), [oo, sg], [mixb])

        McC = [TL(Mc.t, fw.buf("Mc%d" % c)) for c in range(2)]
        MTcC = [TL(MTc.t, fw.buf("MTc%d" % c)) for c in range(2)]
        XcC = [TL(Xc.t, fw.buf("Xc%d" % c)) for c in range(2)]

        tile_ctr = [0]

        def do_tile(xsrc, T, C, nch, full, sk, par=0, dname=None, need_q=True):
            tile_ctr[0] += 1
            SG, SD = Sgla[sk], Sgdn[sk]
            x2, h2b, eid, gate = x2s[par], h2bs[par], eids[par], gates[par]
            fw.dma("sp", lambda e: e.dma_start(out=xt[:T, :], in_=xsrc), writes=[xt])
            rmsnorm_rs(xt, T)
            V(lambda e: e.tensor_scalar(out=xn[:T, :], in0=xt[:T, :], scalar1=ss[:T, 0:1], scalar2=None, op0=ALU.mult), [xt, ss], [xn])
            transpose_bf(xn, hT, T)

            def proj_tok(c0, ncols, evac):
                pp = pbank()
                for k in range(8):
                    P(lambda e: e.matmul(pp[:T, 0:ncols], lhsT=hT[:, k, :T], rhs=win[:, k, c0:c0 + ncols], start=(k == 0), stop=(k == 7)), [hT, win], [pp])
                evac(pp)
            if full:
                proj_tok(O_GQ, 512, lambda pp: A(lambda e: e.copy(out=gqk[:T, :], in_=pp[:T, :]), [pp], [gqk]))
            else:
                proj_tok(O_GK, 256, lambda pp: A(lambda e: e.copy(out=gqk[:T, 256:512], in_=pp[:T, 0:256]), [pp], [gqk]))
            proj_tok(O_GV, 512, lambda pp: A(lambda e: e.copy(out=gv[:T, :], in_=pp[:T, :]), [pp], [gv]))
            proj_tok(O_DA, 8, lambda pp: A(lambda e: e.copy(out=dab[:T, :], in_=pp[:T, 0:8]), [pp], [dab]))
            if full:
                proj_tok(O_GG, 512, lambda pp: A(lambda e: e.activation(out=sgg[:T, :], in_=pp[:T, :], func=AF.Silu), [pp], [sgg]))
                proj_tok(O_DZ, 512, lambda pp: A(lambda e: e.activation(out=sdz[:T, :], in_=pp[:T, :], func=AF.Silu), [pp], [sdz]))
            pp = pbank()
            for k in range(8):
                P(lambda e: e.matmul(pp[0:16, :T], lhsT=win[:, k, O_GLR:O_GLR + 16], rhs=hT[:, k, :T], start=(k == 0), stop=(k == 7)), [hT, win], [pp])
            A(lambda e: e.copy(out=glrT[:, :T], in_=pp[0:16, :T]), [pp], [glrT])
            for g in range(0 if need_q else 1, 3):
                pp = pbank()
                for j in range(4):
                    blk = g * 4 + j
                    for k in range(8):
                        P(lambda e: e.matmul(pp[:, j * 128:j * 128 + T], lhsT=win[:, k, O_DQKV + blk * 128:O_DQKV + (blk + 1) * 128], rhs=hT[:, k, :T],
                                             start=(k == 0), stop=(k == 7)), [hT, win], [pp])
                A(lambda e: e.copy(convin[:, g * 4:(g + 1) * 4, 3:3 + T], v4(pp)[:, :, :T]), [pp], [convin])

            def gla_chain():
                pp = pbank()
                P(lambda e: e.matmul(pp[:T, 0:256], lhsT=glrT[:, :T], rhs=wgk[:], start=True, stop=False), [glrT, wgk], [pp])
                P(lambda e: e.matmul(pp[:T, 0:256], lhsT=cst[0:1, 5, :T], rhs=bgk[:], start=False, stop=True), [cst, bgk], [pp])
                A(lambda e: e.activation(out=a1[:T, :], in_=pp[:T, 0:256], func=AF.Exp, scale=-1.0), [pp], [a1])
                A(lambda e: e.activation(out=a1[:T, :], in_=a1[:T, :], func=AF.Ln, bias=1.0), [a1], [a1])
                V(lambda e: e.tensor_scalar(out=a1[:T, :], in0=a1[:T, :], scalar1=-1.0 / 16, scalar2=None, op0=ALU.mult), [a1], [a1])
                pp = pbank()
                P(lambda e: e.matmul(pp[:T, 0:256], lhsT=cs(1, 0, T), rhs=a1[:T, :], start=True, stop=True), [cst, a1], [pp])
                P(lambda e: e.matmul(pp[:T, 256:512], lhsT=cs(2, 0, T), rhs=a1[:T, :], start=True, stop=True), [cst, a1], [pp])
                if full:
                    A(lambda e: e.activation(out=eG[:T, :], in_=pp[:T, 0:256], func=AF.Exp), [pp], [eG])
                A(lambda e: e.activation(out=enG[:T, :], in_=pp[:T, 0:256], func=AF.Exp, scale=-1.0), [pp], [enG])
                A(lambda e: e.activation(out=edG[:T, :], in_=pp[:T, 256:512], func=AF.Exp), [pp], [edG])
                if full:
                    V(lambda e: e.scalar_tensor_tensor(out=qks[:T, 0:256], in0=gqk[:T, 0:256], scalar=0.125, in1=eG[:T, :], op0=ALU.mult, op1=ALU.mult), [gqk, eG], [qks])
                    V(lambda e: e.tensor_tensor(out=qks[:T, 256:512], in0=gqk[:T, 256:512], in1=enG[:T, :], op=ALU.mult), [gqk, enG], [qks])
                V(lambda e: e.tensor_tensor(out=kh[:T, :], in0=gqk[:T, 256:512], in1=edG[:T, :], op=ALU.mult), [gqk, edG], [kh])
                pp = pbank()
                for p_ in range(2):
                    P(lambda e: e.matmul(pp[:, p_ * 2:p_ * 2 + nch], lhsT=a1[:T, p_ * 128:(p_ + 1) * 128], rhs=cst[:T, 6, 0:nch], start=True, stop=True), [a1, cst], [pp])
                A(lambda e: e.activation(out=egl[:, :, 0:nch], in_=pp[:, 0:4].rearrange("p (a b) -> p a b", b=2)[:, :, 0:nch], func=AF.Exp), [pp], [egl])
                if full:
                    pp = pbank()
                    for j in range(4):
                        P(lambda e: e.transpose(out=pp[:, j * 128:j * 128 + T], in_=qks[:T, j * 128:(j + 1) * 128], identity=cs(0, 0, T)), [qks, cst], [pp])
                    A(lambda e: e.copy(out=qkT[:, :, :T], in_=v4(pp)[:, :, :T]), [pp], [qkT])
                for c in range(nch):
                    r0 = c * C
                    rs_ = slice(r0, r0 + C)
                    if full:
                        pa = pbank()
                        for h in range(4):
                            hs = slice((h % 2) * 64, (h % 2) * 64 + 64)
                            P(lambda e: e.matmul(pa[rs_, h * 64:h * 64 + C], lhsT=qkT[hs, 2 + h // 2, rs_], rhs=qkT[hs, h // 2, rs_], start=True, stop=True), [qkT], [pa])
                        V(lambda e: e.tensor_tensor(out=attS[rs_, :, :C], in0=v4(pa, 4, 64, rs_)[:, :, :C],
                                                    in1=cs(4, r0, C).unsqueeze(1).to_broadcast([C, 4, C]), op=ALU.mult), [pa, cst], [attS])
                        po = pbank()
                        for h in range(4):
                            hs = slice((h % 2) * 64, (h % 2) * 64 + 64)
                            P(lambda e: e.matmul(po[rs_, h * 128:(h + 1) * 128], lhsT=qkT[hs, h // 2, rs_], rhs=SG[hs, h // 2, :], start=True, stop=False), [qkT, SG], [po])
                            P(lambda e: e.matmul(po[rs_, h * 128:(h + 1) * 128], lhsT=attS[rs_, h, :C], rhs=gv[rs_, h * 128:(h + 1) * 128], start=False, stop=True), [attS, gv], [po])
                        A(lambda e: e.copy(out=o12h[0][rs_, :, :], in_=v4(po, rows=rs_)), [po], [o12h[0]])
                    pst = pbank()
                    for h in range(4):
                        hs = slice((h % 2) * 64, (h % 2) * 64 + 64)
                        P(lambda e: e.matmul(pst[hs, (h // 2) * 128:(h // 2 + 1) * 128], lhsT=kh[rs_, h * 64:(h + 1) * 64], rhs=gv[rs_, h * 128:(h + 1) * 128], start=True, stop=True), [kh, gv], [pst])
                    for p_ in range(2):
                        V(lambda e: e.scalar_tensor_tensor(out=SG[:, p_, :], in0=SG[:, p_, :], scalar=egl[:, p_, c:c + 1], in1=pst[:, p_ * 128:(p_ + 1) * 128],
                                                           op0=ALU.mult, op1=ALU.add), [SG, egl, pst], [SG])
                if full:
                    gated_norm(0, gnw1, sgg, T, 0)

            def gdn_chain():
                b0 = 0 if full else 4
                nb = 12 - b0
                V(lambda e: e.tensor_tensor(out=cv[:, b0:, :T], in0=convin[:, b0:, 0:T], in1=wc[:, b0:, 0:1].to_broadcast([128, nb, T]), op=ALU.mult), [convin, wc], [cv])
                for i in range(1, 4):
                    V(lambda e: e.tensor_tensor(out=cvt[:, b0:, :T], in0=convin[:, b0:, i:i + T], in1=wc[:, b0:, i:i + 1].to_broadcast([128, nb, T]), op=ALU.mult), [convin, wc], [cvt])
                    V(lambda e: e.tensor_tensor(out=cv[:, b0:, :T], in0=cv[:, b0:, :T], in1=cvt[:, b0:, :T], op=ALU.add), [cv, cvt], [cv])
                A(lambda e: e.copy(out=cvt[:, :, 0:3], in_=convin[:, :, T:T + 3]), [convin], [cvt])
                A(lambda e: e.copy(out=convin[:, :, 0:3], in_=cvt[:, :, 0:3]), [cvt], [convin])
                A(lambda e: e.activation(out=cv[:, b0:, :T], in_=cv[:, b0:, :T], func=AF.Silu), [cv], [cv])
                if dname:
                    dump(dname + "_convin", convin, convin[:, 0, 3:3 + T], [128, T])
                    dump(dname + "_cv", cv, cv[:, 4, :T], [128, T])
                V(lambda e: e.tensor_tensor(out=cvt[:, b0:8, :T], in0=cv[:, b0:8, :T], in1=cv[:, b0:8, :T], op=ALU.mult), [cv], [cvt])
                for g in range(0 if full else 1, 2):
                    pp = pbank()
                    for j in range(4):
                        P(lambda e: e.matmul(pp[:, j * 128:j * 128 + T], lhsT=ONES, rhs=cvt[:, g * 4 + j, :T], start=True, stop=True), [cst, cvt], [pp])
                    A(lambda e: e.activation(out=rq[:, g * 4:(g + 1) * 4, :T], in_=v4(pp)[:, :, :T], func=AF.Sqrt, bias=EPS), [pp], [rq])
                V(lambda e: e.reciprocal(out=rq[:, b0:8, :T], in_=rq[:, b0:8, :T]), [rq], [rq])
                if full:
                    V(lambda e: e.scalar_tensor_tensor(out=qT[:, :, :T], in0=cv[:, 0:4, :T], scalar=128.0 ** -0.5, in1=rq[:, 0:4, :T], op0=ALU.mult, op1=ALU.mult), [cv, rq], [qT])
                V(lambda e: e.tensor_tensor(out=kTg[:, :, :T], in0=cv[:, 4:8, :T], in1=rq[:, 4:8, :T], op=ALU.mult), [cv, rq], [kTg])
                pp = pbank()
                for h in range(4):
                    P(lambda e: e.transpose(out=pp[:T, h * 128:(h + 1) * 128], in_=kTg[:, h, :T], identity=IDN), [kTg, cst], [pp])
                A(lambda e: e.copy(out=ktok[:T], in_=v4(pp, rows=slice(0, T))), [pp], [ktok])
                pp = pbank()
                for h in range(4):
                    P(lambda e: e.transpose(out=pp[:T, h * 128:(h + 1) * 128], in_=cv[:, 8 + h, :T], identity=IDN), [cv, cst], [pp])
                A(lambda e: e.copy(out=vtok[:T], in_=v4(pp, rows=slice(0, T))), [pp], [vtok])
                if dname:
                    dump(dname + "_kTg", kTg, kTg[:, 0, :T], [128, T])
                    dump(dname + "_ktok", ktok, ktok[:T, 0, :], [T, 128])
                V(lambda e: e.tensor_tensor(out=g4[:T, 0, :], in0=dab[:T, 0:4], in1=dtb[:T, :], op=ALU.add), [dab, dtb], [g4])
                A(lambda e: e.activation(out=g4[:T, 0, :], in_=g4[:T, 0, :], func=AF.Exp), [g4], [g4])
                A(lambda e: e.activation(out=g4[:T, 0, :], in_=g4[:T, 0, :], func=AF.Ln, bias=1.0), [g4], [g4])
                V(lambda e: e.tensor_tensor(out=g4[:T, 1, :], in0=g4[:T, 0, :], in1=negA[:T, :], op=ALU.mult), [g4, negA], [g4])
                A(lambda e: e.activation(out=g4[:T, 2, :], in_=dab[:T, 4:8], func=AF.Sigmoid), [dab], [g4])
                V(lambda e: e.tensor_scalar(out=g4[:T, 3, :], in0=g4[:T, 2, :], scalar1=-1.0, scalar2=None, op0=ALU.mult), [g4], [g4])
                pp = pbank()
                P(lambda e: e.matmul(pp[:T, 0:4], lhsT=cs(1, 0, T), rhs=g4[:T, 1, :], start=True, stop=True), [cst, g4], [pp])
                P(lambda e: e.matmul(pp[:T, 4:8], lhsT=cs(2, 0, T), rhs=g4[:T, 1, :], start=True, stop=True), [cst, g4], [pp])
                A(lambda e: e.activation(out=g4[:T, 4:6, :], in_=pp[:T, 0:8].rearrange("p (a b) -> p a b", b=4), func=AF.Exp), [pp], [g4])
                V(lambda e: e.tensor_tensor(out=g4[:T, 6, :], in0=g4[:T, 2, :], in1=g4[:T, 4, :], op=ALU.mult), [g4], [g4])
                V(lambda e: e.tensor_tensor(out=a2blk[:T, :, 0:nch], in0=g4[:T, 1, :].unsqueeze(2).to_broadcast([T, 4, nch]),
                                            in1=cst[:T, 6, 0:nch].unsqueeze(1).to_broadcast([T, 4, nch]), op=ALU.mult), [g4, cst], [a2blk])
                pp = pbank()
                for h in range(4):
                    P(lambda e: e.matmul(pp[:, h * 2:h * 2 + nch], lhsT=cst[:T, 5, :], rhs=a2blk[:T, h, 0:nch], start=True, stop=True), [cst, a2blk], [pp])
                A(lambda e: e.activation(out=egl2[:, :, 0:nch], in_=pp[:, 0:8].rearrange("p (a b) -> p a b", b=2)[:, :, 0:nch], func=AF.Exp), [pp], [egl2])
                if full:
                    V(lambda e: e.tensor_tensor(out=big[:T, :, :T], in0=cs(0, 0, T).unsqueeze(1).to_broadcast([T, 4, T]),
                                                in1=g4[:T, 4, :].unsqueeze(2).to_broadcast([T, 4, T]), op=ALU.mult), [cst, g4], [big])
                    pp = pbank()
                    for h in range(4):
                        P(lambda e: e.matmul(pp[:, h * 128:h * 128 + T], lhsT=cst[:T, 5, :], rhs=big[:T, h, :T], start=True, stop=True), [cst, big], [pp])
                    V(lambda e: e.tensor_tensor(out=qgT[:, :, :T], in0=qT[:, :, :T], in1=v4(pp)[:, :, :T], op=ALU.mult), [qT, pp], [qgT])
                V(lambda e: e.tensor_tensor(out=big[:T, :, :T], in0=cs(2, 0, T).unsqueeze(1).to_broadcast([T, 4, T]),
                                            in1=g4[:T, 1, :].unsqueeze(2).to_broadcast([T, 4, T]), op=ALU.mult), [cst, g4], [big])
                pp = pbank()
                for h in range(4):
                    P(lambda e: e.matmul(pp[:T, h * 128:h * 128 + T], lhsT=cs(1, 0, T), rhs=big[:T, h, :T], start=True, stop=True), [cst, big], [pp])
                A(lambda e: e.activation(out=decS[:T, :, :T], in_=v4(pp, rows=slice(0, T))[:, :, :T], func=AF.Exp), [pp], [decS])
                V(lambda e: e.tensor_tensor(out=decS[:T, :, :T], in0=decS[:T, :, :T], in1=cs(3, 0, T).unsqueeze(1).to_broadcast([T, 4, T]), op=ALU.mult), [decS, cst], [decS])
                if full:
                    V(lambda e: e.tensor_tensor(out=big[:T, :, :T], in0=cs(1, 0, T).unsqueeze(1).to_broadcast([T, 4, T]),
                                                in1=g4[:T, 1, :].unsqueeze(2).to_broadcast([T, 4, T]), op=ALU.mult), [cst, g4], [big])
                    pp = pbank()
                    for h in range(4):
                        P(lambda e: e.matmul(pp[:T, h * 128:h * 128 + T], lhsT=cs(2, 0, T), rhs=big[:T, h, :T], start=True, stop=True), [cst, big], [pp])
                    A(lambda e: e.activation(out=decT[:T, :, :T], in_=v4(pp, rows=slice(0, T))[:, :, :T], func=AF.Exp), [pp], [decT])
                    V(lambda e: e.tensor_tensor(out=decT[:T, :, :T], in0=decT[:T, :, :T], in1=cs(4, 0, T).unsqueeze(1).to_broadcast([T, 4, T]), op=ALU.mult), [decT, cst], [decT])
                if dname:
                    dump(dname + "_g4", g4, g4[:T, 0:7, :].rearrange("p a b -> p (a b)"), [T, 28])
                    dump(dname + "_decS", decS, decS[:T, 0, :T], [T, T])
                bc4 = lambda row: g4[:T, row, :].unsqueeze(2).to_broadcast([T, 4, 128])
                V(lambda e: e.tensor_tensor(out=vb[:T], in0=vtok[:T], in1=bc4(2), op=ALU.mult), [vtok, g4], [vb])
                V(lambda e: e.tensor_tensor(out=kbg[:T], in0=ktok[:T], in1=bc4(6), op=ALU.mult), [ktok, g4], [kbg])
                V(lambda e: e.tensor_tensor(out=kd[:T], in0=ktok[:T], in1=bc4(5), op=ALU.mult), [ktok, g4], [kd])
                nlev = {64: 5, 32: 4}[C]

                def solve_chunk(c):
                    r0 = c * C
                    rs_ = slice(r0, r0 + C)
                    Mc, MTc, Xc = McC[c], MTcC[c], XcC[c]
                    pk = pbank()
                    for h in range(4):
                        P(lambda e: e.matmul(pk[rs_, h * 64:h * 64 + C], lhsT=kTg[:, h, rs_], rhs=kTg[:, h, rs_], start=True, stop=True), [kTg], [pk])
                    V(lambda e: e.tensor_tensor(out=Mc[rs_, :, :C], in0=v4(pk, 4, 64, rs_)[:, :, :C], in1=decS[rs_, :, rs_], op=ALU.mult), [pk, decS], [Mc])
                    V(lambda e: e.tensor_tensor(out=Mc[rs_, :, :C], in0=Mc[rs_, :, :C], in1=g4[rs_, 3, :].unsqueeze(2).to_broadcast([C, 4, C]), op=ALU.mult), [Mc, g4], [Mc])
                    pt = pbank()
                    for h in range(4):
                        P(lambda e: e.matmul(pt[rs_, h * 64:h * 64 + C], lhsT=Mc[rs_, h, :C], rhs=cs(0, r0, C), start=True, stop=True), [Mc, cst], [pt])
                    A(lambda e: e.copy(out=MTc[rs_, :, :C], in_=v4(pt, 4, 64, rs_)[:, :, :C]), [pt], [MTc])
                    V(lambda e: e.tensor_tensor(out=Xc[rs_, :, :C], in0=MTc[rs_, :, :C], in1=cs(0, r0, C).unsqueeze(1).to_broadcast([C, 4, C]), op=ALU.add), [MTc, cst], [Xc])
                    for lev in range(nlev):
                        last = lev == nlev - 1
                        p1 = pbank()
                        for h in range(4):
                            P(lambda e: e.matmul(p1[rs_, h * 64:h * 64 + C], lhsT=MTc[rs_, h, :C], rhs=Mc[rs_, h, :C], start=True, stop=True), [MTc, Mc], [p1])
                        if not last:
                            p2 = pbank()
                            for h in range(4):
                                P(lambda e: e.matmul(p2[rs_, h * 64:h * 64 + C], lhsT=Mc[rs_, h, :C], rhs=MTc[rs_, h, :C], start=True, stop=True), [MTc, Mc], [p2])
                        A(lambda e: e.copy(out=Mc[rs_, :, :C], in_=v4(p1, 4, 64, rs_)[:, :, :C]), [p1], [Mc])
                        if not last:
                            V(lambda e: e.tensor_copy(out=MTc[rs_, :, :C], in_=v4(p2, 4, 64, rs_)[:, :, :C]), [p2], [MTc])
                        p3 = pbank()
                        for h in range(4):
                            P(lambda e: e.matmul(p3[rs_, h * 64:h * 64 + C], lhsT=Mc[rs_, h, :C], rhs=Xc[rs_, h, :C], start=True, stop=True), [Mc, Xc], [p3])
                        V(lambda e: e.tensor_tensor(out=Xc[rs_, :, :C], in0=Xc[rs_, :, :C], in1=v4(p3, 4, 64, rs_)[:, :, :C], op=ALU.add), [Xc, p3], [Xc])

                if nch == 2:
                    fw.co.fork([lambda: solve_chunk(0), lambda: solve_chunk(1)], [4, 4])
                else:
                    solve_chunk(0)
                for c in range(nch):
                    r0 = c * C
                    rs_ = slice(r0, r0 + C)
                    Xc = XcC[c]
                    pw = pbank()
                    for h in range(4):
                        P(lambda e: e.matmul(pw[:, h * 64:h * 64 + C], lhsT=kbg[rs_, h, :], rhs=Xc[rs_, h, :C], start=True, stop=True), [kbg, Xc], [pw])
                    A(lambda e: e.mul(out=nwT[:, :, :C], in_=v4(pw, 4, 64)[:, :, :C], mul=-1.0), [pw], [nwT])
                    pv = pbank()
                    for h in range(4):
                        P(lambda e: e.matmul(pv[rs_, h * 128:(h + 1) * 128], lhsT=Xc[rs_, h, :C], rhs=vb[rs_, h, :], start=True, stop=False), [Xc, vb], [pv])
                        P(lambda e: e.matmul(pv[rs_, h * 128:(h + 1) * 128], lhsT=nwT[:, h, :C], rhs=SD[:, h, :], start=False, stop=True), [nwT, SD], [pv])
                    A(lambda e: e.copy(out=vnew[rs_], in_=v4(pv, rows=rs_)), [pv], [vnew])
                    if full:
                        pq = pbank()
                        for h in range(4):
                            P(lambda e: e.matmul(pq[rs_, h * 64:h * 64 + C], lhsT=kTg[:, h, rs_], rhs=qT[:, h, rs_], start=True, stop=True), [kTg, qT], [pq])
                        V(lambda e: e.tensor_tensor(out=ATc[rs_, :, :C], in0=v4(pq, 4, 64, rs_)[:, :, :C], in1=decT[rs_, :, rs_], op=ALU.mult), [pq, decT], [ATc])
                        po2 = pbank()
                        for h in range(4):
                            P(lambda e: e.matmul(po2[rs_, h * 128:(h + 1) * 128], lhsT=qgT[:, h, rs_], rhs=SD[:, h, :], start=True, stop=False), [qgT, SD], [po2])
                            P(lambda e: e.matmul(po2[rs_, h * 128:(h + 1) * 128], lhsT=ATc[rs_, h, :C], rhs=vnew[rs_, h, :], start=False, stop=True), [ATc, vnew], [po2])
                        A(lambda e: e.copy(out=o12h[1][rs_, :, :], in_=v4(po2, rows=rs_)), [po2], [o12h[1]])
                    ps2 = pbank()
                    for h in range(4):
                        P(lambda e: e.matmul(ps2[:, h * 128:(h + 1) * 128], lhsT=kd[rs_, h, :], rhs=vnew[rs_, h, :], start=True, stop=True), [kd, vnew], [ps2])
                    V(lambda e: e.tensor_tensor(out=SD[:], in0=SD[:], in1=egl2[:, :, c:c + 1].to_broadcast([128, 4, 128]), op=ALU.mult), [SD, egl2], [SD])
                    V(lambda e: e.tensor_tensor(out=SD[:], in0=SD[:], in1=v4(ps2), op=ALU.add), [SD, ps2], [SD])
                if not full:
                    return
                gated_norm(1, gnw2, sdz, T, 512)

            fw.co.fork([gla_chain, gdn_chain], [3, 6])
            if True:
                if not full:
                    return
                if dname:
                    dump(dname + "_mix", mixb, mixb[:T, :], [T, D])

            fw.barrier()
            fw.dma("sp", lambda e: e.dma_start(out=wout[:, :, :], in_=wo_b), reads=[scr_wo], writes=[wout], owner=wout)
            for k in range(0, 8, 2):
                fw.dma("sp", lambda e: e.dma_start(out=wq[:, k:k + 2, :], in_=wq_b[:, k:k + 2, :]), reads=[scr_wq], writes=[wq], owner=wq)
            transpose_bf(mixb, mixT, T)
            for n in range(2):
                pp = pbank()
                for k in range(8):
                    P(lambda e: e.matmul(pp[:T, :], lhsT=mixT[:, k, :T], rhs=wout[:, k, n * 512:(n + 1) * 512], start=(k == 0), stop=(k == 7)), [mixT, wout], [pp])
                V(lambda e: e.tensor_tensor(out=x2[:T, n * 512:(n + 1) * 512], in0=xt[:T, n * 512:(n + 1) * 512], in1=pp[:T, :], op=ALU.add), [xt, pp], [x2])
            if dname:
                dump(dname + "_x2", x2, x2[:T, :], [T, D])

            if limit < 3:
                fw.barrier()
                return
            rmsnorm_rs(x2, T)
            V(lambda e: e.scalar_tensor_tensor(out=h2b[:T, :], in0=x2[:T, :], scalar=ss[:T, 0:1], in1=nfw[:T, :], op0=ALU.mult, op1=ALU.mult), [x2, ss, nfw], [h2b])
            transpose_bf(h2b, hT, T)
            for g in range(4):
                pp = pbank()
                for j in range(4):
                    cb = g * 4 + j
                    for k in range(8):
                        P(lambda e: e.matmul(pp[:, j * 128:j * 128 + T], lhsT=wq[:, k, cb * 128:(cb + 1) * 128], rhs=hT[:, k, :T], start=(k == 0), stop=(k == 7)), [wq, hT], [pp])
                A(lambda e: e.copy(pqT[:, g * 4:(g + 1) * 4, :T], v4(pp)[:, :, :T]), [pp], [pqT])
            for half in range(2):
                for g in range(2):
                    pp = pbank()
                    for j in range(4):
                        h = g * 4 + j
                        P(lambda e: e.matmul(pp[:T, j * 128:(j + 1) * 128], lhsT=pqT[:, 2 * h + half, :T], rhs=kT[:, half * 8 + h, :], start=True, stop=True), [pqT, kT], [pp])
                    A(lambda e: e.copy(s12[:T, half * 8 + g * 4:half * 8 + g * 4 + 4, :], v4(pp, rows=slice(0, T))), [pp], [s12])
            for s in range(16):
                V(lambda e: e.max(out=v12[:T, s, 0:8], in_=s12[:T, s, :]), [s12], [v12])
                V(lambda e: e.max_index(out=i12[:T, s, 0:8], in_max=v12[:T, s, 0:8], in_values=s12[:T, s, :]), [s12, v12], [i12])
                V(lambda e: e.match_replace(out=wk[:T, 0:128], in_to_replace=v12[:T, s, 0:8], in_values=s12[:T, s, :], imm_value=NEG), [s12, v12], [wk])
                V(lambda e: e.max(out=v12[:T, s, 8:16], in_=wk[:T, 0:128]), [wk], [v12])
                V(lambda e: e.max_index(out=i12[:T, s, 8:16], in_max=v12[:T, s, 8:16], in_values=wk[:T, 0:128]), [wk, v12], [i12])
            V(lambda e: e.tensor_copy(out=i12f[:T], in_=i12[:T]), [i12], [i12f])
            V(lambda e: e.tensor_scalar(out=i12f[:T, 0:8, :], in0=i12f[:T, 0:8, :], scalar1=128.0, scalar2=None, op0=ALU.mult), [i12f], [i12f])
            c4 = lambda t_: t_[:T].rearrange("p h (a b) -> p h a b", b=16)
            b1 = lambda t_: t_[:T, 0:8, :].unsqueeze(3).to_broadcast([T, 8, 16, 16])
            b2 = lambda t_: t_[:T, 8:16, :].unsqueeze(2).to_broadcast([T, 8, 16, 16])
            V(lambda e: e.tensor_tensor(out=c4(cand), in0=b1(v12), in1=b2(v12), op=ALU.add), [v12], [cand])
            pos = i12[:T, 0:8, :]
            pau = i12[:T, 8:16, :]
            for h in range(8):
                V(lambda e: e.max(out=sc[:T, h, 0:8], in_=cand[:T, h, :]), [cand], [sc])
                V(lambda e: e.max_index(out=i12[:T, h, 0:8], in_max=sc[:T, h, 0:8], in_values=cand[:T, h, :]), [cand, sc], [i12])
                V(lambda e: e.match_replace(out=wk[:T, :], in_to_replace=sc[:T, h, 0:8], in_values=cand[:T, h, :], imm_value=NEG), [cand, sc], [wk])
                V(lambda e: e.max(out=sc[:T, h, 8:16], in_=wk[:T, :]), [wk], [sc])
                V(lambda e: e.max_index(out=i12[:T, h, 8:16], in_max=sc[:T, h, 8:16], in_values=wk[:T, :]), [wk, sc], [i12])
            V(lambda e: e.tensor_single_scalar(out=pau, in_=pos, scalar=4, op=ALU.logical_shift_right), [i12], [i12])
            V(lambda e: e.tensor_single_scalar(out=pos, in_=pos, scalar=15, op=ALU.bitwise_and), [i12], [i12])
            V(lambda e: e.tensor_copy(out=v12[:T, 0:8, :], in_=pau), [i12], [v12])
            V(lambda e: e.tensor_copy(out=v12[:T, 8:16, :], in_=pos), [i12], [v12])
            iota4 = cst[:T, 6, 16:32].unsqueeze(1).unsqueeze(1).to_broadcast([T, 8, 16, 16])
            sel = [eidf[:T, :].rearrange("p (a b) -> p a b", b=16), wk[:T, 0:128].rearrange("p (a b) -> p a b", b=16)]
            for w_ in range(2):
                V(lambda e: e.tensor_tensor(out=c4(cidx), in0=v12[:T, w_ * 8:(w_ + 1) * 8, :].unsqueeze(3).to_broadcast([T, 8, 16, 16]), in1=iota4, op=ALU.is_equal), [v12, cst], [cidx])
                V(lambda e: e.tensor_tensor(out=c4(cidx), in0=c4(cidx), in1=i12f[:T, w_ * 8:(w_ + 1) * 8, :].unsqueeze(2).to_broadcast([T, 8, 16, 16]), op=ALU.mult), [cidx, i12f], [cidx])
                V(lambda e: e.tensor_reduce(out=sel[w_], in_=c4(cidx), axis=AX.X, op=ALU.add), [cidx], [eidf if w_ == 0 else wk])
            V(lambda e: e.tensor_tensor(out=eidf[:T, :], in0=eidf[:T, :], in1=wk[:T, 0:128], op=ALU.add), [eidf, wk], [eidf])
            V(lambda e: e.tensor_scalar(out=eidf[:T, :], in0=eidf[:T, :], scalar1=16383.0, scalar2=None, op0=ALU.min), [eidf], [eidf])
            V(lambda e: e.tensor_copy(out=eid[:T, :], in_=eidf[:T, :]), [eidf], [eid])
            V(lambda e: e.tensor_tensor(out=gate[:T], in0=sc[:T], in1=sc[:T, :, 0:1].to_broadcast([T, 8, 16]), op=ALU.subtract), [sc], [gate])
            A(lambda e: e.activation(out=gate[:T], in_=gate[:T], func=AF.Exp), [gate], [gate])
            V(lambda e: e.tensor_reduce(out=zz[:T, :], in_=gate[:T], axis=AX.X, op=ALU.add), [gate], [zz])
            V(lambda e: e.reciprocal(out=zz[:T, :], in_=zz[:T, :]), [zz], [zz])
            V(lambda e: e.tensor_tensor(out=gate[:T], in0=gate[:T], in1=zz[:T, :].unsqueeze(2).to_broadcast([T, 8, 16]), op=ALU.mult), [gate, zz], [gate])
            fw.barrier()

        def tile_G(par, T, ydst):
            x2, h2b, eid, gate = x2s[par], h2bs[par], eids[par], gates[par]
            LA = NSLOT - 2
            slot = {}

            def gather(i, table):
                r_ = ring[rctr[0] % NSLOT]
                rctr[0] += 1
                slot[i] = r_
                fw.dma("pool", lambda e: e.indirect_dma_start(out=r_[:T, :], out_offset=None, in_=table,
                                                              in_offset=IndirectOffsetOnAxis(ap=eid[:T, i:i + 1], axis=0)), reads=[eid], writes=[r_])
            for i in range(128 + LA):
                if i < 128:
                    gather(i, peer_u)
                j = i - LA
                if j >= 0:
                    u_ = slot.pop(j)
                    V(lambda e: e.scalar_tensor_tensor(out=u_[:T, :], in0=u_[:T, :], scalar=1.0, in1=h2b[:T, :], op0=ALU.mult, op1=ALU.mult,
                                                       accum_out=actv[:T, j:j + 1]), [u_, h2b], [u_, actv])
            V(lambda e: e.tensor_tensor(out=tmpa[:T, :], in0=actv[:T, :], in1=actv[:T, :], op=ALU.mult), [actv], [tmpa])
            V(lambda e: e.tensor_scalar(out=tmpa[:T, :], in0=tmpa[:T, :], scalar1=0.044715, scalar2=1.0, op0=ALU.mult, op1=ALU.add), [tmpa], [tmpa])
            V(lambda e: e.tensor_tensor(out=tmpa[:T, :], in0=tmpa[:T, :], in1=actv[:T, :], op=ALU.mult), [tmpa, actv], [tmpa])
            A(lambda e: e.activation(out=tmpa[:T, :], in_=tmpa[:T, :], func=AF.Sigmoid, scale=1.5957691216057308), [tmpa], [tmpa])
            V(lambda e: e.tensor_tensor(out=wgt[:T, :], in0=tmpa[:T, :], in1=actv[:T, :], op=ALU.mult), [tmpa, actv], [wgt])
            V(lambda e: e.tensor_tensor(out=wgt[:T, :], in0=wgt[:T, :], in1=gate[:T].rearrange("p a b -> p (a b)"), op=ALU.mult), [wgt, gate], [wgt])
            py = [pbanks[5], pbanks[6]]
            for i in range(128 + LA):
                if i < 128:
                    gather(i, peer_v)
                j = i - LA
                if j >= 0:
                    v_ = slot.pop(j)
                    d_ = dg[j % 4]
                    A(lambda e: e.activation(out=d_[:T, :T], in_=cs(0, 0, T), func=AF.Copy, scale=wgt[:T, j:j + 1]), [cst, wgt], [d_])
                    for n in range(2):
                        P(lambda e: e.matmul(py[n][:T, :], lhsT=d_[:T, :T], rhs=v_[:T, n * 512:(n + 1) * 512], start=(j == 0), stop=(j == 127)), [d_, v_], [py[n]])
            for n in range(2):
                V(lambda e: e.tensor_tensor(out=x2[:T, n * 512:(n + 1) * 512], in0=x2[:T, n * 512:(n + 1) * 512], in1=py[n][:T, :], op=ALU.add), [x2, py[n]], [x2])
            A(lambda e: e.activation(out=h2b[:T, :], in_=x2[:T, :], func=AF.Square, accum_out=ssg[:T, :]), [x2], [h2b, ssg])
            A(lambda e: e.activation(out=ssg[:T, :], in_=ssg[:T, :], func=AF.Sqrt, scale=1.0 / D, bias=EPS), [ssg], [ssg])
            V(lambda e: e.reciprocal(out=ssg[:T, :], in_=ssg[:T, :]), [ssg], [ssg])
            V(lambda e: e.scalar_tensor_tensor(out=x2[:T, :], in0=x2[:T, :], scalar=ssg[:T, 0:1], in1=nzw[:T, :], op0=ALU.mult, op1=ALU.mult), [x2, ssg, nzw], [x2])
            fw.dma("sp", lambda e: e.dma_start(out=ydst, in_=x2[:T, :]), reads=[x2])

        def store_states(sk, o_gla, o_gdn, o_conv, T):
            gl = o_gla.rearrange("h k v -> (h k) v")
            for p_ in range(2):
                fw.dma("sp", lambda e: e.dma_start(out=gl[p_ * 128:(p_ + 1) * 128, :], in_=Sgla[sk][:, p_, :]), reads=[Sgla[sk]])
            fw.dma("sp", lambda e: e.dma_start(out=o_gdn.rearrange("h k v -> k h v"), in_=Sgdn[sk][:]), reads=[Sgdn[sk]])
            for r in range(3):
                for b0_ in range(0, 12, 3):
                    fw.dma("sp", lambda e: e.dma_start(out=o_conv[r:r + 1, :].rearrange("o (b p) -> p b o", p=128)[:, b0_:b0_ + 3, :], in_=convin[:, b0_:b0_ + 3, r:r + 1],
                                                       allow_slow_non_contiguous=True), reads=[convin])

        KM, KG = 5, 4
        pend = [None]
        fullctr = [0]

        def run_pair(Mf):
            if pend[0] is not None:
                fw.co.fork([Mf, pend[0]], [KM, KG])
            else:
                Mf()

        for i in range(NPRE):
            do_tile(xpre[i * 128:(i + 1) * 128, :], 128, 64, 2, False, "p", need_q=(i == NPRE - 1))
        for i in range(NMAIN):
            par = fullctr[0] % 2
            fullctr[0] += 1
            run_pair(lambda i=i, par=par: do_tile(xmain[i * 128:(i + 1) * 128, :], 128, 64, 2, True, "p", par, dname=("m%d" % i) if dbg else None))
            pend[0] = (lambda i=i, par=par: tile_G(par, 128, y_main[i * 128:(i + 1) * 128, :])) if limit >= 3 else None
        store_states("p", gla_st, gdn_st, conv_st, 128)
        for j in range(NSAMP):
            sk = "s%d" % j
            gl = sgla_in[j].rearrange("h k v -> (h k) v")
            for p_ in range(2):
                fw.dma("sp", lambda e: e.dma_start(out=Sgla[sk][:, p_, :], in_=gl[p_ * 128:(p_ + 1) * 128, :]), writes=[Sgla[sk]])
            fw.dma("sp", lambda e: e.dma_start(out=Sgdn[sk][:], in_=sgdn_in[j].rearrange("h k v -> k h v")), writes=[Sgdn[sk]])
            for r in range(3):
                for b0_ in range(0, 12, 3):
                    fw.dma("sp", lambda e: e.dma_start(out=convin[:, b0_:b0_ + 3, r:r + 1], in_=sconv_in[j, r:r + 1, :].rearrange("o (b p) -> p b o", p=128)[:, b0_:b0_ + 3, :],
                                                       allow_slow_non_contiguous=True), writes=[convin])
            par = fullctr[0] % 2
            fullctr[0] += 1
            run_pair(lambda j=j, par=par, sk=sk: do_tile(xs[j * 32:(j + 1) * 32, :], 32, 32, 1, True, sk, par, dname=("s%d" % j) if dbg else None))
            pend[0] = (lambda j=j, par=par: tile_G(par, 32, y_s[j * 32:(j + 1) * 32, :])) if limit >= 3 else None
            store_states(sk, gla_s[j], gdn_s[j], conv_s[j], 32)
        if pend[0] is not None:
            pend[0]()
        fw.finish("sp")
        print("instructions:", fw.nins, {e: fw.cnt[e] for e in fw.ENGS})
    return nc


N_CORES = 8
WNAMES = ["norm_mix_w", "w_in", "gla_w_gk2", "gla_b_gk", "gla_norm_w", "gdn_conv_w", "gdn_a_log", "gdn_dt_bias",
          "gdn_norm_w", "w_out", "norm_ffn_w", "peer_wq", "peer_k1", "peer_k2", "peer_u", "peer_v"]


def kernel(x_prompt, x_sample, state_gla, state_gdn, state_gdn_conv, norm_mix_w, w_in, gla_w_gk2, gla_b_gk,
           gla_norm_w, gdn_conv_w, gdn_a_log, gdn_dt_bias, gdn_norm_w, w_out, norm_ffn_w, peer_wq, peer_k1,
           peer_k2, peer_u, peer_v, norm_final_w):
    f = lambda a: np.ascontiguousarray(np.asarray(a, dtype=np.float32))
    x_prompt, x_sample = f(x_prompt), f(x_sample)
    B, L, _ = x_prompt.shape
    HALF = L // 2
    NT = HALF // 128
    loc = dict(norm_mix_w=norm_mix_w, w_in=w_in, gla_w_gk2=gla_w_gk2, gla_b_gk=gla_b_gk, gla_norm_w=gla_norm_w,
               gdn_conv_w=gdn_conv_w, gdn_a_log=gdn_a_log, gdn_dt_bias=gdn_dt_bias, gdn_norm_w=gdn_norm_w, w_out=w_out,
               norm_ffn_w=norm_ffn_w, peer_wq=peer_wq, peer_k1=peer_k1, peer_k2=peer_k2, peer_u=peer_u, peer_v=peer_v)
    shared = {k: f(v)[0] for k, v in loc.items()}
    shared["norm_final_w"] = f(norm_final_w)
    shared["cst"] = make_consts()
    nc = build(NT, NT, 2)
    in_maps = []
    zeros = np.zeros((HALF, D), np.float32)
    for c in range(N_CORES):
        b, half = c // 2, c % 2
        m = dict(shared)
        m["xpre"] = x_prompt[b, :HALF] if half == 1 else zeros
        m["xmain"] = x_prompt[b, half * HALF:(half + 1) * HALF]
        m["xs"] = x_sample[2 * c:2 * c + 2].reshape(64, D)
        m["sgla_in"] = f(state_gla)[0, 2 * c:2 * c + 2]
        m["sgdn_in"] = f(state_gdn)[0, 2 * c:2 * c + 2]
        m["sconv_in"] = f(state_gdn_conv)[0, 2 * c:2 * c + 2]
        in_maps.append({k: np.ascontiguousarray(v) for k, v in m.items()})
    res = run_bass_kernel_spmd(nc, in_maps, core_ids=list(range(N_CORES))).results
    y_prompt = np.zeros((B, L, D), np.float32)
    y_sample = np.zeros((16, 32, D), np.float32)
    gla_p = np.zeros((1, B, 4, 64, 128), np.float32)
    gdn_p = np.zeros((1, B, 4, 128, 128), np.float32)
    conv_p = np.zeros((1, B, 3, 1536), np.float32)
    gla_sm = np.zeros((1, 16, 4, 64, 128), np.float32)
    gdn_sm = np.zeros((1, 16, 4, 128, 128), np.float32)
    conv_sm = np.zeros((1, 16, 3, 1536), np.float32)
    for c in range(N_CORES):
        b, half = c // 2, c % 2
        r = res[c]
        y_prompt[b, half * HALF:(half + 1) * HALF] = r["y_main"]
        y_sample[2 * c:2 * c + 2] = r["y_s"].reshape(2, 32, D)
        if half == 1:
            gla_p[0, b] = r["gla_st"]
            gdn_p[0, b] = r["gdn_st"]
            conv_p[0, b] = r["conv_st"]
        gla_sm[0, 2 * c:2 * c + 2] = r["gla_s"]
        gdn_sm[0, 2 * c:2 * c + 2] = r["gdn_s"]
        conv_sm[0, 2 * c:2 * c + 2] = r["conv_s"]
    return (y_prompt, y_sample, gla_p, gdn_p, conv_p, gla_sm, gdn_sm, conv_sm)
```

```python
from contextlib import ExitStack
import threading
import numpy as np
import concourse.bass as bass
import concourse.mybir as mybir
from concourse.bass import IndirectOffsetOnAxis
from concourse.bass_utils import run_bass_kernel_spmd

F32 = mybir.dt.float32
BF16 = mybir.dt.bfloat16
U32 = mybir.dt.uint32
AF = mybir.ActivationFunctionType
ALU = mybir.AluOpType
AX = mybir.AxisListType

D = 1024
NIN = 3608
O_GQ, O_GK, O_GV, O_GLR, O_GG, O_DQKV, O_DA, O_DB, O_DZ = 0, 256, 512, 1024, 1040, 1552, 3088, 3092, 3096
EPS = 1e-6
NEG = -1e30


class Buf:
    __slots__ = ("name", "w", "r", "dsem", "dcnt", "excl")

    def __init__(self, name):
        self.name = name
        self.excl = False
        self.w = None
        self.r = {}
        self.dsem = None
        self.dcnt = 0


class TL:
    def __init__(self, t, b):
        self.t = t
        self.b = b

    def __getitem__(self, k):
        return self.t[k]


class Co:
    def __init__(self):
        self.slots = {}
        self.order = []
        self.n = 0
        self.nid = 0
        self.err = []
        self.t2s = {}

    def _new(self, k, parent):
        sid = self.nid
        self.nid += 1
        self.slots[sid] = dict(sem=threading.Semaphore(0), k=k, run=True, parent=parent, nchild=0)
        return sid

    def _pick(self, after):
        n = len(self.order)
        st = self.order.index(after)
        for d in range(1, n + 1):
            sid = self.order[(st + d) % n]
            if self.slots[sid]["run"]:
                return sid
        return None

    def fork(self, fns, ks):
        tid = threading.get_ident()
        me = self.t2s.get(tid)
        if me is None:
            me = self._new(1, None)
            self.t2s[tid] = me
            self.order.append(me)
        par = self.slots[me]
        par["run"] = False
        par["nchild"] = len(fns)
        pos = self.order.index(me)
        kids = []
        for i, (f, k) in enumerate(zip(fns, ks)):
            sid = self._new(k, me)
            kids.append(sid)
            self.order.insert(pos + 1 + i, sid)
            threading.Thread(target=self._wrap, args=(sid, f), daemon=True).start()
        self.n = 0
        self.slots[kids[0]]["sem"].release()
        par["sem"].acquire()
        if par["parent"] is None and self.err:
            e = self.err[0]
            self.err = []
            raise e

    def _wrap(self, sid, f):
        sl = self.slots[sid]
        sl["sem"].acquire()
        self.t2s[threading.get_ident()] = sid
        try:
            f()
        except BaseException as ex:
            self.err.append(ex)
        finally:
            sl["run"] = False
            par = self.slots[sl["parent"]]
            par["nchild"] -= 1
            if par["nchild"] == 0:
                par["run"] = True
            nxt = self._pick(sid)
            self.order.remove(sid)
            self.n = 0
            self.slots[nxt]["sem"].release()

    def tick(self):
        sid = self.t2s.get(threading.get_ident())
        if sid is None or len(self.order) <= 1:
            return
        self.n += 1
        if self.n >= self.slots[sid]["k"]:
            self.n = 0
            nxt = self._pick(sid)
            if nxt is not None and nxt != sid:
                self.slots[nxt]["sem"].release()
                self.slots[sid]["sem"].acquire()


class PEProxy:
    def __init__(self, fw, pe):
        self.fw = fw
        self.pe = pe
        self.mode = None

    @staticmethod
    def _r(n):
        return 32 if n <= 32 else (64 if n <= 64 else 128)

    def _sw(self, lhsT):
        shp = lhsT.shape
        mode = (self._r(shp[0]), self._r(int(np.prod(shp[1:]))))
        if self.mode is not None and mode != self.mode:
            fw = self.fw
            if fw.cnt["pe"] > 0 and fw.seen["pe"].get("pe", 0) < fw.cnt["pe"]:
                self.pe.wait_ge(fw.sem["pe"], fw.cnt["pe"])
                fw.seen["pe"]["pe"] = fw.cnt["pe"]
        self.mode = mode

    def matmul(self, out, lhsT, rhs, **kw):
        self._sw(lhsT)
        return self.pe.matmul(out, lhsT=lhsT, rhs=rhs, **kw)

    def transpose(self, out, in_, identity, **kw):
        self._sw(in_)
        return self.pe.transpose(out=out, in_=in_, identity=identity, **kw)


class FW:
    ENGS = ("pe", "dve", "act", "pool", "sp")

    def __init__(self, nc, stack):
        self.nc = nc
        self.stack = stack
        self.eng = {"pe": nc.tensor, "dve": nc.vector, "act": nc.scalar, "pool": nc.gpsimd, "sp": nc.sync}
        self.pex = PEProxy(self, nc.tensor)
        self.sem = {e: stack.enter_context(nc.semaphore("s_" + e)) for e in self.ENGS}
        self.cnt = {e: 0 for e in self.ENGS}
        self.seen = {e: {} for e in self.ENGS}
        self.dseen = {e: {} for e in self.ENGS}
        self.nins = 0
        self.dbufs = []
        self.co = Co()

    def sb(self, name, shape, dt=F32, dma=False):
        t = self.stack.enter_context(self.nc.sbuf_tensor(name, list(shape), dt))
        return TL(t, self.buf(name, dma))

    def ps(self, name, shape, dt=F32):
        t = self.stack.enter_context(self.nc.psum_tensor(name, list(shape), dt))
        tl = TL(t, self.buf(name))
        tl.b.excl = True
        return tl

    def buf(self, name, dma=False):
        b = Buf(name)
        if dma:
            b.dsem = self.stack.enter_context(self.nc.semaphore("d_" + name))
            self.dbufs.append(b)
        return b

    def _wait(self, e, other, idx):
        if idx <= 0 or self.seen[e].get(other, 0) >= idx:
            return
        self.eng[e].wait_ge(self.sem[other], idx)
        self.seen[e][other] = idx

    def _wait_dma(self, e, b):
        if b.dsem is None or b.dcnt == 0 or self.dseen[e].get(b.name, 0) >= b.dcnt:
            return
        self.eng[e].wait_ge(b.dsem, b.dcnt)
        self.dseen[e][b.name] = b.dcnt

    def _deps(self, e, reads, writes):
        for b in reads:
            if b.w is not None:
                self._wait(e, b.w[0], b.w[1])
            if b.excl:
                for re_, ri in b.r.items():
                    if re_ != e:
                        self._wait(e, re_, ri)
            self._wait_dma(e, b)
        for b in writes:
            if b.w is not None:
                self._wait(e, b.w[0], b.w[1])
            for re_, ri in b.r.items():
                self._wait(e, re_, ri)
            self._wait_dma(e, b)

    def op(self, e, fn, reads=(), writes=()):
        reads = [x.b if isinstance(x, TL) else x for x in reads]
        writes = [x.b if isinstance(x, TL) else x for x in writes]
        self._deps(e, reads, writes)
        ins = fn(self.pex if e == "pe" else self.eng[e])
        self.cnt[e] += 1
        idx = self.cnt[e]
        ins.then_inc(self.sem[e], 1)
        self.nins += 1
        for b in reads:
            b.r[e] = idx
        for b in writes:
            b.w = (e, idx)
            b.r = {}
        self.co.tick()
        return ins

    def dma(self, e, fn, reads=(), writes=(), owner=None):
        reads = [x.b if isinstance(x, TL) else x for x in reads]
        writes = [x.b if isinstance(x, TL) else x for x in writes]
        self._deps(e, reads, writes)
        ins = fn(self.eng[e])
        self.nins += 1
        if owner is not None:
            b = owner.b if isinstance(owner, TL) else owner
        else:
            sems = [b for b in reads + writes if b.dsem is not None]
            assert len(sems) == 1, [b.name for b in reads + writes]
            b = sems[0]
        b.dcnt += 16
        ins.then_inc(b.dsem, 16)
        self.co.tick()
        return ins

    def barrier(self):
        for e in self.ENGS:
            for o in self.ENGS:
                if o != e:
                    self._wait(e, o, self.cnt[o])
            for b in self.dbufs:
                self._wait_dma(e, b)

    def finish(self, e="sp"):
        for b in self.dbufs:
            self._wait_dma(e, b)
        for o in self.ENGS:
            if o != e:
                self._wait(e, o, self.cnt[o])


def make_consts():
    c = np.zeros((128, 7, 128), np.float32)
    i = np.arange(128)
    same = (i[:, None] // 64) == (i[None, :] // 64)
    c[:, 0, :] = np.eye(128)
    c[:, 1, :] = (i[:, None] <= i[None, :]) & same
    c[:, 2, :] = (i[:, None] > i[None, :]) & same
    c[:, 3, :] = (i[None, :] < i[:, None]) & same
    c[:, 4, :] = (i[:, None] <= i[None, :]) & same
    c[:, 5, :] = 1.0
    c[:64, 6, 0] = 1.0
    c[64:, 6, 1] = 1.0
    c[:, 6, 16:32] = np.arange(16, dtype=np.float32)[None, :]
    return c


def build(NPRE, NMAIN, NSAMP=2, dbg=False, limit=99):
    nc = bass.Bass("TRN2", target_bir_lowering=False)
    dt_in = lambda n, s: nc.dram_tensor(n, list(s), F32, kind="ExternalInput").ap()
    dt_out = lambda n, s: nc.dram_tensor(n, list(s), F32, kind="ExternalOutput").ap()
    xpre = dt_in("xpre", [max(NPRE, 1) * 128, D])
    xmain = dt_in("xmain", [NMAIN * 128, D])
    xs = dt_in("xs", [NSAMP * 32, D])
    sgla_in = dt_in("sgla_in", [NSAMP, 4, 64, 128])
    sgdn_in = dt_in("sgdn_in", [NSAMP, 4, 128, 128])
    sconv_in = dt_in("sconv_in", [NSAMP, 3, 1536])
    cst_in = dt_in("cst", [128, 7, 128])
    norm_mix_w = dt_in("norm_mix_w", [D])
    w_in = dt_in("w_in", [D, NIN])
    gla_w_gk2 = dt_in("gla_w_gk2", [16, 256])
    gla_b_gk = dt_in("gla_b_gk", [256])
    gla_norm_w = dt_in("gla_norm_w", [128])
    gdn_conv_w = dt_in("gdn_conv_w", [4, 1536])
    gdn_a_log = dt_in("gdn_a_log", [4])
    gdn_dt_bias = dt_in("gdn_dt_bias", [4])
    gdn_norm_w = dt_in("gdn_norm_w", [128])
    w_out = dt_in("w_out", [D, D])
    norm_ffn_w = dt_in("norm_ffn_w", [D])
    peer_wq = dt_in("peer_wq", [D, 2048])
    peer_k1 = dt_in("peer_k1", [8, 128, 128])
    peer_k2 = dt_in("peer_k2", [8, 128, 128])
    peer_u = dt_in("peer_u", [16384, D])
    peer_v = dt_in("peer_v", [16384, D])
    norm_final_w = dt_in("norm_final_w", [D])

    y_main = dt_out("y_main", [NMAIN * 128, D])
    y_s = dt_out("y_s", [NSAMP * 32, D])
    gla_st = dt_out("gla_st", [4, 64, 128])
    gdn_st = dt_out("gdn_st", [4, 128, 128])
    conv_st = dt_out("conv_st", [3, 1536])
    gla_s = dt_out("gla_s", [NSAMP, 4, 64, 128])
    gdn_s = dt_out("gdn_s", [NSAMP, 4, 128, 128])
    conv_s = dt_out("conv_s", [NSAMP, 3, 1536])
    dbg_outs = {}

    with ExitStack() as st:
        fw = FW(nc, st)
        V = lambda fn, r, w: fw.op("dve", fn, r, w)
        A = lambda fn, r, w: fw.op("act", fn, r, w)
        P = lambda fn, r, w: fw.op("pe", fn, r, w)
        GP = lambda fn, r, w: fw.op("pool", fn, r, w)

        cst = fw.sb("cstsb", [128, 7, 128], dma=True)
        fw.dma("sp", lambda e: e.dma_start(out=cst[:], in_=cst_in), writes=[cst])
        IDN, U1, LST, STRICT, TRIU, ONES, BLK = (cst[:, i, :] for i in range(7))

        def cs(i, r0, n, c0=None, m=None):
            c0 = r0 if c0 is None else c0
            m = n if m is None else m
            return cst[r0:r0 + n, i, c0:c0 + m]

        win = fw.sb("win", [128, 8, NIN], BF16)
        wi_v = w_in.rearrange("(k p) n -> p k n", p=128)
        wo_v = w_out.rearrange("(k p) n -> p k n", p=128)
        wq_v = peer_wq.rearrange("(k p) n -> p k n", p=128)
        nmwT = fw.sb("nmwT", [128, 8], dma=True)
        for k in range(0, 8, 2):
            fw.dma("sp", lambda e: e.dma_start(out=nmwT[:, k:k + 2], in_=norm_mix_w.rearrange("(k p) -> p k", p=128)[:, k:k + 2], allow_slow_non_contiguous=True), writes=[nmwT])
        nfw = fw.sb("nfw", [128, D], dma=True)
        nzw = fw.sb("nzw", [128, D], dma=True)
        for t_, src in ((nfw, norm_ffn_w), (nzw, norm_final_w)):
            fw.dma("sp", lambda e: e.dma_start(out=t_[:], in_=src.partition_broadcast(128)), writes=[t_])
        gnw1 = fw.sb("gnw1", [128, 128], dma=True)
        gnw2 = fw.sb("gnw2", [128, 128], dma=True)
        fw.dma("sp", lambda e: e.dma_start(out=gnw1[:], in_=gla_norm_w.partition_broadcast(128)), writes=[gnw1])
        fw.dma("sp", lambda e: e.dma_start(out=gnw2[:], in_=gdn_norm_w.partition_broadcast(128)), writes=[gnw2])
        wgk = fw.sb("wgk", [16, 256], dma=True)
        fw.dma("sp", lambda e: e.dma_start(out=wgk[:], in_=gla_w_gk2), writes=[wgk])
        bgk = fw.sb("bgk", [1, 256], dma=True)
        fw.dma("sp", lambda e: e.dma_start(out=bgk[:], in_=gla_b_gk.rearrange("(o n) -> o n", o=1)), writes=[bgk])
        wc = fw.sb("wc", [128, 12, 4], dma=True)
        for i in range(4):
            for b0_ in range(0, 12, 3):
                fw.dma("sp", lambda e: e.dma_start(out=wc[:, b0_:b0_ + 3, i:i + 1],
                                                   in_=gdn_conv_w[i:i + 1, :].rearrange("o (b p) -> p b o", p=128)[:, b0_:b0_ + 3, :],
                                                   allow_slow_non_contiguous=True), writes=[wc])
        dtb = fw.sb("dtb", [128, 4], dma=True)
        fw.dma("sp", lambda e: e.dma_start(out=dtb[:], in_=gdn_dt_bias.partition_broadcast(128)), writes=[dtb])
        alog = fw.sb("alog", [128, 4], dma=True)
        fw.dma("sp", lambda e: e.dma_start(out=alog[:], in_=gdn_a_log.partition_broadcast(128)), writes=[alog])
        negA = fw.sb("negA", [128, 4])
        A(lambda e: e.activation(out=negA[:], in_=alog[:], func=AF.Exp), [alog], [negA])
        V(lambda e: e.tensor_scalar(out=negA[:], in0=negA[:], scalar1=-1.0, scalar2=None, op0=ALU.mult), [negA], [negA])

        pbanks = [fw.ps("pb%d" % i, [128, 512]) for i in range(7)]
        pbf = fw.ps("pbf", [128, 8, 128], BF16)
        NROT = 5
        tls = threading.local()

        def set_pool(pool):
            tls.pool = list(pool)
            tls.ctr = 0

        def pbank():
            pool = getattr(tls, "pool", None)
            if pool is None:
                set_pool(range(NROT))
                pool = tls.pool
            p = pbanks[pool[tls.ctr % len(pool)]]
            tls.ctr += 1
            return p

        def v4(p, a=4, b=128, rows=slice(None), c0=0):
            return p[rows, c0:c0 + a * b].rearrange("p (a b) -> p a b", b=b)

        ARW = 17408
        arena = st.enter_context(nc.sbuf_tensor("arena", [128, ARW], F32))
        aoff = {"m": 0, "p": 0}

        def av(phase, name, shape, dt=F32, buf=None, dma=False):
            n = int(np.prod(shape[1:]))
            words = n if dt == F32 else n // 2
            o = aoff[phase]
            aoff[phase] += words
            assert aoff[phase] <= ARW, (phase, name, aoff[phase])
            ap = arena[:, o:o + words]
            if dt != F32:
                ap = ap.bitcast(dt)
            if len(shape) == 3:
                ap = ap.rearrange("p (a b) -> p a b", b=shape[2])
            return TL(ap, buf if buf is not None else fw.buf(name, dma))

        M = lambda name, shape, dt=F32: av("m", name, shape, dt)
        gqk = M("gqk", [128, 512]); gv = M("gv", [128, 512]); sgg = M("sgg", [128, 512]); sdz = M("sdz", [128, 512])
        a1 = M("a1", [128, 256]); eG = M("eG", [128, 256]); enG = M("enG", [128, 256]); edG = M("edG", [128, 256])
        qks = M("qks", [128, 512]); kh = M("kh", [128, 256]); qkT = M("qkT", [128, 4, 128]); attS = M("attS", [128, 4, 64])
        o12 = M("o12", [128, 8, 128]); cv = M("cv", [128, 12, 128]); cvt = M("cvt", [128, 12, 128]); rq = M("rq", [128, 8, 128])
        qT = M("qT", [128, 4, 128]); kTg = M("kTg", [128, 4, 128]); qgT = M("qgT", [128, 4, 128])
        ktok = M("ktok", [128, 4, 128]); vtok = M("vtok", [128, 4, 128])
        big = M("big", [128, 4, 128]); decS = M("decS", [128, 4, 128]); decT = M("decT", [128, 4, 128])
        Mc = M("Mc", [128, 4, 64]); MTc = M("MTc", [128, 4, 64]); Xc = M("Xc", [128, 4, 64])
        vb = M("vb", [128, 4, 128]); kbg = M("kbg", [128, 4, 128]); kd = M("kd", [128, 4, 128])
        nwT = M("nwT", [128, 4, 64]); vnew = M("vnew", [128, 4, 128]); ATc = M("ATc", [128, 4, 64])
        wq = av("p", "wq", [128, 8, 2048], BF16, dma=True)
        pqT = av("p", "pqT", [128, 16, 128]); s12 = av("p", "s12", [128, 16, 128])
        wob = fw.buf("wout_cand", dma=True)
        o_c = aoff["p"]
        cand = av("p", "cand", [128, 8, 256], buf=wob); cidx = av("p", "cidx", [128, 8, 256], buf=wob)
        wout = TL(arena[:, o_c:o_c + 4096].bitcast(BF16).rearrange("p (a b) -> p a b", b=D), wob)

        xt = fw.sb("xt", [128, D], dma=True)
        kT = fw.sb("kT", [128, 16, 128])
        junk = fw.sb("junk", [128, D], BF16)
        ss = fw.sb("ss", [128, 1])
        ssg = fw.sb("ssg", [128, 1])
        xn = fw.sb("xn", [128, D], BF16)
        hT = fw.sb("hT", [128, 8, 128], BF16)
        dab = fw.sb("dab", [128, 8])
        glrT = fw.sb("glrT", [16, 128])
        convin = fw.sb("convin", [128, 12, 131], dma=True)
        egl = fw.sb("egl", [128, 2, 2])
        mixb = xn
        mixT = hT
        g4 = fw.sb("g4", [128, 10, 4])
        a2blk = fw.sb("a2blk", [128, 4, 2])
        egl2 = fw.sb("egl2", [128, 4, 2])
        x2s = [fw.sb("x2_%d" % i, [128, D], dma=True) for i in range(2)]
        h2bs = [fw.sb("h2b_%d" % i, [128, D], BF16) for i in range(2)]
        eids = [fw.sb("eid_%d" % i, [128, 128], U32) for i in range(2)]
        gates = [fw.sb("gate_%d" % i, [128, 8, 16]) for i in range(2)]
        wk = fw.sb("wk", [128, 256])
        v12 = fw.sb("v12", [128, 16, 16])
        i12 = fw.sb("i12", [128, 16, 16], U32)
        i12f = fw.sb("i12f", [128, 16, 16])
        sc = fw.sb("sc", [128, 8, 16])
        eidf = fw.sb("eidf", [128, 128])
        zz = fw.sb("zz", [128, 8])
        actv = fw.sb("actv", [128, 128])
        wgt = fw.sb("wgt", [128, 128])
        tmpa = fw.sb("tmpa", [128, 128])
        NSLOT = 8 if dbg else 9
        ring = [fw.sb("ring%d" % i, [128, D], BF16, dma=True) for i in range(NSLOT)]
        rctr = [0]
        dg = [fw.sb("dg%d" % i, [128, 128], BF16) for i in range(4)]
        _sg = fw.sb("Sgla_p", [128, 2, 128], dma=True)
        _sd = fw.sb("Sgdn_p", [128, 4, 128], dma=True)
        Sgla = {k: _sg for k in ["p"] + ["s%d" % i for i in range(NSAMP)]}
        Sgdn = {k: _sd for k in ["p"] + ["s%d" % i for i in range(NSAMP)]}
        GP(lambda e: e.memset(Sgla["p"][:], 0.0), [], [Sgla["p"]])
        GP(lambda e: e.memset(Sgdn["p"][:], 0.0), [], [Sgdn["p"]])
        GP(lambda e: e.memset(convin[:], 0.0), [], [convin])
        wstg = [TL(arena[:, i * NIN:(i + 1) * NIN], fw.buf("wstg%d" % i, dma=True)) for i in range(2)]
        for k in range(8):
            ws = wstg[k % 2]
            fw.dma("sp", lambda e: e.dma_start(out=ws[:, :], in_=wi_v[:, k, :]), writes=[ws])
            V(lambda e: e.tensor_scalar(out=win[:, k, :], in0=ws[:, :], scalar1=nmwT[:, k:k + 1], scalar2=None, op0=ALU.mult), [ws, nmwT], [win])
        fw.barrier()
        wq_b = nc.dram_tensor("wq_b", [128, 8, 2048], BF16, kind="Internal").ap()
        wo_b = nc.dram_tensor("wo_b", [128, 8, D], BF16, kind="Internal").ap()
        wq_sw = TL(wq.t, fw.buf("wq_sw", dma=True))
        wout_sw = TL(wout.t, fw.buf("wout_sw", dma=True))
        scr_wq = fw.buf("scr_wq", dma=True)
        scr_wo = fw.buf("scr_wo", dma=True)
        for k in range(8):
            fw.dma("pool", lambda e: e.dma_start(out=wq_sw[:, k, :], in_=wq_v[:, k, :]), writes=[wq_sw])
        fw.dma("sp", lambda e: e.dma_start(out=wq_b, in_=wq_sw[:, :, :]), reads=[wq_sw], writes=[scr_wq], owner=scr_wq)
        for k in range(8):
            fw.dma("pool", lambda e: e.dma_start(out=wout_sw[:, k, :], in_=wo_v[:, k, :]), writes=[wout_sw])
        fw.dma("sp", lambda e: e.dma_start(out=wo_b, in_=wout_sw[:, :, :]), reads=[wout_sw], writes=[scr_wo], owner=scr_wo)
        fw.barrier()
        kraw = TL(arena[:, 0:2048].rearrange("p (a b) -> p a b", b=128), fw.buf("kraw", dma=True))
        fw.dma("sp", lambda e: e.dma_start(out=kraw[:, 0:8, :], in_=peer_k1.rearrange("h n d -> n h d")), writes=[kraw])
        fw.dma("sp", lambda e: e.dma_start(out=kraw[:, 8:16, :], in_=peer_k2.rearrange("h n d -> n h d")), writes=[kraw])
        for g in range(4):
            pk = pbank()
            for j in range(4):
                P(lambda e: e.transpose(out=pk[:, j * 128:(j + 1) * 128], in_=kraw[:, g * 4 + j, :], identity=IDN), [kraw, cst], [pk])
            V(lambda e: e.tensor_copy(out=kT[:, g * 4:(g + 1) * 4, :], in_=v4(pk)), [pk], [kT])

        fw.barrier()

        stg = fw.sb("stg", [128, D], BF16, dma=True) if dbg else None

        def dump(name, tl, ap, shape):
            if not dbg:
                return
            if tl in x2s:
                o = nc.dram_tensor("dbg_" + name, list(shape), F32, kind="ExternalOutput").ap()
                fw.dma("sp", lambda e: e.dma_start(out=o, in_=ap), reads=[tl])
                return
            o = nc.dram_tensor("dbg_" + name, list(shape), BF16, kind="ExternalOutput").ap()
            V(lambda e: e.tensor_copy(out=stg[:shape[0], :shape[1]], in_=ap), [tl], [stg])
            fw.dma("sp", lambda e: e.dma_start(out=o, in_=stg[:shape[0], :shape[1]]), reads=[stg])

        def rmsnorm_rs(src, T):
            A(lambda e: e.activation(out=junk[:T, :], in_=src[:T, :], func=AF.Square, accum_out=ss[:T, :]), [src], [junk, ss])
            A(lambda e: e.activation(out=ss[:T, :], in_=ss[:T, :], func=AF.Sqrt, scale=1.0 / D, bias=EPS), [ss], [ss])
            V(lambda e: e.reciprocal(out=ss[:T, :], in_=ss[:T, :]), [ss], [ss])

        def transpose_bf(src, dst, T):
            for k in range(8):
                P(lambda e: e.transpose(out=pbf[:, k, :T], in_=src[:T, k * 128:(k + 1) * 128], identity=idb[:T, :T]), [src, idb], [pbf])
            A(lambda e: e.copy(out=dst[:, :, :T], in_=pbf[:, :, :T]), [pbf], [dst])

        idb = fw.sb("idb", [128, 128], BF16)
        V(lambda e: e.tensor_copy(out=idb[:], in_=IDN), [cst], [idb])

        jscr = [fw.sb("jscr%d" % i, [128, 128], BF16) for i in range(2)]
        ssq2 = [fw.sb("ssq%d" % i, [128, 4]) for i in range(2)]
        o12h = [TL(o12.t[:, 0:4, :], fw.buf("o12a")), TL(o12.t[:, 4:8, :], fw.buf("o12b"))]

        def gated_norm(which, gw, sg, T, col0):
            oo, sq_, js = o12h[which], ssq2[which], jscr[which]
            for h in range(4):
                A(lambda e: e.activation(out=js[:T, :], in_=oo[:T, h, :], func=AF.Square, accum_out=sq_[:T, h:h + 1]), [oo], [js, sq_])
            A(lambda e: e.activation(out=sq_[:T, :], in_=sq_[:T, :], func=AF.Sqrt, scale=1.0 / 128, bias=EPS), [sq_], [sq_])
            V(lambda e: e.reciprocal(out=sq_[:T, :], in_=sq_[:T, :]), [sq_], [sq_])
            mv = oo[:T, :, :]
            V(lambda e: e.tensor_tensor(out=mv, in0=mv, in1=sq_[:T, :].unsqueeze(2).to_broadcast([T, 4, 128]), op=ALU.mult), [oo, sq_], [oo])
            V(lambda e: e.tensor_tensor(out=mv, in0=mv, in1=gw[:T, :].unsqueeze(1).to_broadcast([T, 4, 128]), op=ALU.mult), [oo, gw], [oo])
            V(lambda e: e.tensor_tensor(out=mixb[:T, col0:col0 + 512].rearrange("p (a b) -> p a b", b=128), in0=mv,
                                        in1=sg[:T, :].rearrange("p (a b) -> p a b", b=128), op=ALU.mult), [oo, sg], [mixb])

        McC = [TL(Mc.t, fw.buf("Mc%d" % c)) for c in range(2)]
        MTcC = [TL(MTc.t, fw.buf("MTc%d" % c)) for c in range(2)]
        XcC = [TL(Xc.t, fw.buf("Xc%d" % c)) for c in range(2)]

        tile_ctr = [0]

        def do_tile(xsrc, T, C, nch, full, sk, par=0, dname=None, need_q=True):
            tile_ctr[0] += 1
            SG, SD = Sgla[sk], Sgdn[sk]
            x2, h2b, eid, gate = x2s[par], h2bs[par], eids[par], gates[par]
            fw.dma("sp", lambda e: e.dma_start(out=xt[:T, :], in_=xsrc), writes=[xt])
            rmsnorm_rs(xt, T)
            V(lambda e: e.tensor_scalar(out=xn[:T, :], in0=xt[:T, :], scalar1=ss[:T, 0:1], scalar2=None, op0=ALU.mult), [xt, ss], [xn])
            transpose_bf(xn, hT, T)

            def proj_tok(c0, ncols, evac):
                pp = pbank()
                for k in range(8):
                    P(lambda e: e.matmul(pp[:T, 0:ncols], lhsT=hT[:, k, :T], rhs=win[:, k, c0:c0 + ncols], start=(k == 0), stop=(k == 7)), [hT, win], [pp])
                evac(pp)
            if full:
                proj_tok(O_GQ, 512, lambda pp: A(lambda e: e.copy(out=gqk[:T, :], in_=pp[:T, :]), [pp], [gqk]))
            else:
                proj_tok(O_GK, 256, lambda pp: A(lambda e: e.copy(out=gqk[:T, 256:512], in_=pp[:T, 0:256]), [pp], [gqk]))
            proj_tok(O_GV, 512, lambda pp: A(lambda e: e.copy(out=gv[:T, :], in_=pp[:T, :]), [pp], [gv]))
            proj_tok(O_DA, 8, lambda pp: A(lambda e: e.copy(out=dab[:T, :], in_=pp[:T, 0:8]), [pp], [dab]))
            if full:
                proj_tok(O_GG, 512, lambda pp: A(lambda e: e.activation(out=sgg[:T, :], in_=pp[:T, :], func=AF.Silu), [pp], [sgg]))
                proj_tok(O_DZ, 512, lambda pp: A(lambda e: e.activation(out=sdz[:T, :], in_=pp[:T, :], func=AF.Silu), [pp], [sdz]))
            pp = pbank()
            for k in range(8):
                P(lambda e: e.matmul(pp[0:16, :T], lhsT=win[:, k, O_GLR:O_GLR + 16], rhs=hT[:, k, :T], start=(k == 0), stop=(k == 7)), [hT, win], [pp])
            A(lambda e: e.copy(out=glrT[:, :T], in_=pp[0:16, :T]), [pp], [glrT])
            for g in range(0 if need_q else 1, 3):
                pp = pbank()
                for j in range(4):
                    blk = g * 4 + j
                    for k in range(8):
                        P(lambda e: e.matmul(pp[:, j * 128:j * 128 + T], lhsT=win[:, k, O_DQKV + blk * 128:O_DQKV + (blk + 1) * 128], rhs=hT[:, k, :T],
                                             start=(k == 0), stop=(k == 7)), [hT, win], [pp])
                A(lambda e: e.copy(convin[:, g * 4:(g + 1) * 4, 3:3 + T], v4(pp)[:, :, :T]), [pp], [convin])

            def gla_chain():
                set_pool([0, 1])
                pp = pbank()
                P(lambda e: e.matmul(pp[:T, 0:256], lhsT=glrT[:, :T], rhs=wgk[:], start=True, stop=False), [glrT, wgk], [pp])
                P(lambda e: e.matmul(pp[:T, 0:256], lhsT=cst[0:1, 5, :T], rhs=bgk[:], start=False, stop=True), [cst, bgk], [pp])
                A(lambda e: e.activation(out=a1[:T, :], in_=pp[:T, 0:256], func=AF.Exp, scale=-1.0), [pp], [a1])
                A(lambda e: e.activation(out=a1[:T, :], in_=a1[:T, :], func=AF.Ln, bias=1.0), [a1], [a1])
                V(lambda e: e.tensor_scalar(out=a1[:T, :], in0=a1[:T, :], scalar1=-1.0 / 16, scalar2=None, op0=ALU.mult), [a1], [a1])
                pp = pbank()
                P(lambda e: e.matmul(pp[:T, 0:256], lhsT=cs(1, 0, T), rhs=a1[:T, :], start=True, stop=True), [cst, a1], [pp])
                P(lambda e: e.matmul(pp[:T, 256:512], lhsT=cs(2, 0, T), rhs=a1[:T, :], start=True, stop=True), [cst, a1], [pp])
                if full:
                    A(lambda e: e.activation(out=eG[:T, :], in_=pp[:T, 0:256], func=AF.Exp), [pp], [eG])
                A(lambda e: e.activation(out=enG[:T, :], in_=pp[:T, 0:256], func=AF.Exp, scale=-1.0), [pp], [enG])
                A(lambda e: e.activation(out=edG[:T, :], in_=pp[:T, 256:512], func=AF.Exp), [pp], [edG])
                if full:
                    V(lambda e: e.scalar_tensor_tensor(out=qks[:T, 0:256], in0=gqk[:T, 0:256], scalar=0.125, in1=eG[:T, :], op0=ALU.mult, op1=ALU.mult), [gqk, eG], [qks])
                    V(lambda e: e.tensor_tensor(out=qks[:T, 256:512], in0=gqk[:T, 256:512], in1=enG[:T, :], op=ALU.mult), [gqk, enG], [qks])
                V(lambda e: e.tensor_tensor(out=kh[:T, :], in0=gqk[:T, 256:512], in1=edG[:T, :], op=ALU.mult), [gqk, edG], [kh])
                pp = pbank()
                for p_ in range(2):
                    P(lambda e: e.matmul(pp[:, p_ * 2:p_ * 2 + nch], lhsT=a1[:T, p_ * 128:(p_ + 1) * 128], rhs=cst[:T, 6, 0:nch], start=True, stop=True), [a1, cst], [pp])
                A(lambda e: e.activation(out=egl[:, :, 0:nch], in_=pp[:, 0:4].rearrange("p (a b) -> p a b", b=2)[:, :, 0:nch], func=AF.Exp), [pp], [egl])
                if full:
                    pp = pbank()
                    for j in range(4):
                        P(lambda e: e.transpose(out=pp[:, j * 128:j * 128 + T], in_=qks[:T, j * 128:(j + 1) * 128], identity=cs(0, 0, T)), [qks, cst], [pp])
                    A(lambda e: e.copy(out=qkT[:, :, :T], in_=v4(pp)[:, :, :T]), [pp], [qkT])
                for c in range(nch):
                    r0 = c * C
                    rs_ = slice(r0, r0 + C)
                    if full:
                        pa = pbank()
                        for h in range(4):
                            hs = slice((h % 2) * 64, (h % 2) * 64 + 64)
                            P(lambda e: e.matmul(pa[rs_, h * 64:h * 64 + C], lhsT=qkT[hs, 2 + h // 2, rs_], rhs=qkT[hs, h // 2, rs_], start=True, stop=True), [qkT], [pa])
                        V(lambda e: e.tensor_tensor(out=attS[rs_, :, :C], in0=v4(pa, 4, 64, rs_)[:, :, :C],
                                                    in1=cs(4, r0, C).unsqueeze(1).to_broadcast([C, 4, C]), op=ALU.mult), [pa, cst], [attS])
                        po = pbank()
                        for h in range(4):
                            hs = slice((h % 2) * 64, (h % 2) * 64 + 64)
                            P(lambda e: e.matmul(po[rs_, h * 128:(h + 1) * 128], lhsT=qkT[hs, h // 2, rs_], rhs=SG[hs, h // 2, :], start=True, stop=False), [qkT, SG], [po])
                            P(lambda e: e.matmul(po[rs_, h * 128:(h + 1) * 128], lhsT=attS[rs_, h, :C], rhs=gv[rs_, h * 128:(h + 1) * 128], start=False, stop=True), [attS, gv], [po])
                        A(lambda e: e.copy(out=o12h[0][rs_, :, :], in_=v4(po, rows=rs_)), [po], [o12h[0]])
                    pst = pbank()
                    for h in range(4):
                        hs = slice((h % 2) * 64, (h % 2) * 64 + 64)
                        P(lambda e: e.matmul(pst[hs, (h // 2) * 128:(h // 2 + 1) * 128], lhsT=kh[rs_, h * 64:(h + 1) * 64], rhs=gv[rs_, h * 128:(h + 1) * 128], start=True, stop=True), [kh, gv], [pst])
                    for p_ in range(2):
                        V(lambda e: e.scalar_tensor_tensor(out=SG[:, p_, :], in0=SG[:, p_, :], scalar=egl[:, p_, c:c + 1], in1=pst[:, p_ * 128:(p_ + 1) * 128],
                                                           op0=ALU.mult, op1=ALU.add), [SG, egl, pst], [SG])
                if full:
                    gated_norm(0, gnw1, sgg, T, 0)

            def gdn_chain():
                set_pool(range(2, NROT))
                b0 = 0 if full else 4
                nb = 12 - b0
                V(lambda e: e.tensor_tensor(out=cv[:, b0:, :T], in0=convin[:, b0:, 0:T], in1=wc[:, b0:, 0:1].to_broadcast([128, nb, T]), op=ALU.mult), [convin, wc], [cv])
                for i in range(1, 4):
                    V(lambda e: e.tensor_tensor(out=cvt[:, b0:, :T], in0=convin[:, b0:, i:i + T], in1=wc[:, b0:, i:i + 1].to_broadcast([128, nb, T]), op=ALU.mult), [convin, wc], [cvt])
                    V(lambda e: e.tensor_tensor(out=cv[:, b0:, :T], in0=cv[:, b0:, :T], in1=cvt[:, b0:, :T], op=ALU.add), [cv, cvt], [cv])
                A(lambda e: e.copy(out=cvt[:, :, 0:3], in_=convin[:, :, T:T + 3]), [convin], [cvt])
                A(lambda e: e.copy(out=convin[:, :, 0:3], in_=cvt[:, :, 0:3]), [cvt], [convin])
                A(lambda e: e.activation(out=cv[:, b0:, :T], in_=cv[:, b0:, :T], func=AF.Silu), [cv], [cv])
                if dname:
                    dump(dname + "_convin", convin, convin[:, 0, 3:3 + T], [128, T])
                    dump(dname + "_cv", cv, cv[:, 4, :T], [128, T])
                V(lambda e: e.tensor_tensor(out=cvt[:, b0:8, :T], in0=cv[:, b0:8, :T], in1=cv[:, b0:8, :T], op=ALU.mult), [cv], [cvt])
                for g in range(0 if full else 1, 2):
                    pp = pbank()
                    for j in range(4):
                        P(lambda e: e.matmul(pp[:, j * 128:j * 128 + T], lhsT=ONES, rhs=cvt[:, g * 4 + j, :T], start=True, stop=True), [cst, cvt], [pp])
                    A(lambda e: e.activation(out=rq[:, g * 4:(g + 1) * 4, :T], in_=v4(pp)[:, :, :T], func=AF.Sqrt, bias=EPS), [pp], [rq])
                V(lambda e: e.reciprocal(out=rq[:, b0:8, :T], in_=rq[:, b0:8, :T]), [rq], [rq])
                if full:
                    V(lambda e: e.scalar_tensor_tensor(out=qT[:, :, :T], in0=cv[:, 0:4, :T], scalar=128.0 ** -0.5, in1=rq[:, 0:4, :T], op0=ALU.mult, op1=ALU.mult), [cv, rq], [qT])
                V(lambda e: e.tensor_tensor(out=kTg[:, :, :T], in0=cv[:, 4:8, :T], in1=rq[:, 4:8, :T], op=ALU.mult), [cv, rq], [kTg])
                pp = pbank()
                for h in range(4):
                    P(lambda e: e.transpose(out=pp[:T, h * 128:(h + 1) * 128], in_=kTg[:, h, :T], identity=IDN), [kTg, cst], [pp])
                A(lambda e: e.copy(out=ktok[:T], in_=v4(pp, rows=slice(0, T))), [pp], [ktok])
                pp = pbank()
                for h in range(4):
                    P(lambda e: e.transpose(out=pp[:T, h * 128:(h + 1) * 128], in_=cv[:, 8 + h, :T], identity=IDN), [cv, cst], [pp])
                A(lambda e: e.copy(out=vtok[:T], in_=v4(pp, rows=slice(0, T))), [pp], [vtok])
                if dname:
                    dump(dname + "_kTg", kTg, kTg[:, 0, :T], [128, T])
                    dump(dname + "_ktok", ktok, ktok[:T, 0, :], [T, 128])
                V(lambda e: e.tensor_tensor(out=g4[:T, 0, :], in0=dab[:T, 0:4], in1=dtb[:T, :], op=ALU.add), [dab, dtb], [g4])
                A(lambda e: e.activation(out=g4[:T, 0, :], in_=g4[:T, 0, :], func=AF.Exp), [g4], [g4])
                A(lambda e: e.activation(out=g4[:T, 0, :], in_=g4[:T, 0, :], func=AF.Ln, bias=1.0), [g4], [g4])
                V(lambda e: e.tensor_tensor(out=g4[:T, 1, :], in0=g4[:T, 0, :], in1=negA[:T, :], op=ALU.mult), [g4, negA], [g4])
                A(lambda e: e.activation(out=g4[:T, 2, :], in_=dab[:T, 4:8], func=AF.Sigmoid), [dab], [g4])
                V(lambda e: e.tensor_scalar(out=g4[:T, 3, :], in0=g4[:T, 2, :], scalar1=-1.0, scalar2=None, op0=ALU.mult), [g4], [g4])
                pp = pbank()
                P(lambda e: e.matmul(pp[:T, 0:4], lhsT=cs(1, 0, T), rhs=g4[:T, 1, :], start=True, stop=True), [cst, g4], [pp])
                P(lambda e: e.matmul(pp[:T, 4:8], lhsT=cs(2, 0, T), rhs=g4[:T, 1, :], start=True, stop=True), [cst, g4], [pp])
                A(lambda e: e.activation(out=g4[:T, 4:6, :], in_=pp[:T, 0:8].rearrange("p (a b) -> p a b", b=4), func=AF.Exp), [pp], [g4])
                V(lambda e: e.tensor_tensor(out=g4[:T, 6, :], in0=g4[:T, 2, :], in1=g4[:T, 4, :], op=ALU.mult), [g4], [g4])
                V(lambda e: e.tensor_tensor(out=a2blk[:T, :, 0:nch], in0=g4[:T, 1, :].unsqueeze(2).to_broadcast([T, 4, nch]),
                                            in1=cst[:T, 6, 0:nch].unsqueeze(1).to_broadcast([T, 4, nch]), op=ALU.mult), [g4, cst], [a2blk])
                pp = pbank()
                for h in range(4):
                    P(lambda e: e.matmul(pp[:, h * 2:h * 2 + nch], lhsT=cst[:T, 5, :], rhs=a2blk[:T, h, 0:nch], start=True, stop=True), [cst, a2blk], [pp])
                A(lambda e: e.activation(out=egl2[:, :, 0:nch], in_=pp[:, 0:8].rearrange("p (a b) -> p a b", b=2)[:, :, 0:nch], func=AF.Exp), [pp], [egl2])
                if full:
                    V(lambda e: e.tensor_tensor(out=big[:T, :, :T], in0=cs(0, 0, T).unsqueeze(1).to_broadcast([T, 4, T]),
                                                in1=g4[:T, 4, :].unsqueeze(2).to_broadcast([T, 4, T]), op=ALU.mult), [cst, g4], [big])
                    pp = pbank()
                    for h in range(4):
                        P(lambda e: e.matmul(pp[:, h * 128:h * 128 + T], lhsT=cst[:T, 5, :], rhs=big[:T, h, :T], start=True, stop=True), [cst, big], [pp])
                    V(lambda e: e.tensor_tensor(out=qgT[:, :, :T], in0=qT[:, :, :T], in1=v4(pp)[:, :, :T], op=ALU.mult), [qT, pp], [qgT])
                V(lambda e: e.tensor_tensor(out=big[:T, :, :T], in0=cs(2, 0, T).unsqueeze(1).to_broadcast([T, 4, T]),
                                            in1=g4[:T, 1, :].unsqueeze(2).to_broadcast([T, 4, T]), op=ALU.mult), [cst, g4], [big])
                pp = pbank()
                for h in range(4):
                    P(lambda e: e.matmul(pp[:T, h * 128:h * 128 + T], lhsT=cs(1, 0, T), rhs=big[:T, h, :T], start=True, stop=True), [cst, big], [pp])
                A(lambda e: e.activation(out=decS[:T, :, :T], in_=v4(pp, rows=slice(0, T))[:, :, :T], func=AF.Exp), [pp], [decS])
                V(lambda e: e.tensor_tensor(out=decS[:T, :, :T], in0=decS[:T, :, :T], in1=cs(3, 0, T).unsqueeze(1).to_broadcast([T, 4, T]), op=ALU.mult), [decS, cst], [decS])
                if full:
                    V(lambda e: e.tensor_tensor(out=big[:T, :, :T], in0=cs(1, 0, T).unsqueeze(1).to_broadcast([T, 4, T]),
                                                in1=g4[:T, 1, :].unsqueeze(2).to_broadcast([T, 4, T]), op=ALU.mult), [cst, g4], [big])
                    pp = pbank()
                    for h in range(4):
                        P(lambda e: e.matmul(pp[:T, h * 128:h * 128 + T], lhsT=cs(2, 0, T), rhs=big[:T, h, :T], start=True, stop=True), [cst, big], [pp])
                    A(lambda e: e.activation(out=decT[:T, :, :T], in_=v4(pp, rows=slice(0, T))[:, :, :T], func=AF.Exp), [pp], [decT])
                    V(lambda e: e.tensor_tensor(out=decT[:T, :, :T], in0=decT[:T, :, :T], in1=cs(4, 0, T).unsqueeze(1).to_broadcast([T, 4, T]), op=ALU.mult), [decT, cst], [decT])
                if dname:
                    dump(dname + "_g4", g4, g4[:T, 0:7, :].rearrange("p a b -> p (a b)"), [T, 28])
                    dump(dname + "_decS", decS, decS[:T, 0, :T], [T, T])
                bc4 = lambda row: g4[:T, row, :].unsqueeze(2).to_broadcast([T, 4, 128])
                V(lambda e: e.tensor_tensor(out=vb[:T], in0=vtok[:T], in1=bc4(2), op=ALU.mult), [vtok, g4], [vb])
                V(lambda e: e.tensor_tensor(out=kbg[:T], in0=ktok[:T], in1=bc4(6), op=ALU.mult), [ktok, g4], [kbg])
                V(lambda e: e.tensor_tensor(out=kd[:T], in0=ktok[:T], in1=bc4(5), op=ALU.mult), [ktok, g4], [kd])
                nlev = {64: 5, 32: 4}[C]

                def solve_chunk(c):
                    set_pool([2 + c])
                    r0 = c * C
                    rs_ = slice(r0, r0 + C)
                    Mc, MTc, Xc = McC[c], MTcC[c], XcC[c]
                    pb = pbank()
                    for h in range(4):
                        P(lambda e: e.matmul(pb[rs_, h * 64:h * 64 + C], lhsT=kTg[:, h, rs_], rhs=kTg[:, h, rs_], start=True, stop=True), [kTg], [pb])
                    V(lambda e: e.tensor_tensor(out=Mc[rs_, :, :C], in0=v4(pb, 4, 64, rs_)[:, :, :C], in1=decS[rs_, :, rs_], op=ALU.mult), [pb, decS], [Mc])
                    V(lambda e: e.tensor_tensor(out=Mc[rs_, :, :C], in0=Mc[rs_, :, :C], in1=g4[rs_, 3, :].unsqueeze(2).to_broadcast([C, 4, C]), op=ALU.mult), [Mc, g4], [Mc])
                    for h in range(4):
                        P(lambda e: e.matmul(pb[rs_, 256 + h * 64:256 + h * 64 + C], lhsT=Mc[rs_, h, :C], rhs=cs(0, r0, C), start=True, stop=True), [Mc, cst], [pb])
                    A(lambda e: e.copy(out=MTc[rs_, :, :C], in_=v4(pb, 4, 64, rs_, 256)[:, :, :C]), [pb], [MTc])
                    V(lambda e: e.tensor_tensor(out=Xc[rs_, :, :C], in0=MTc[rs_, :, :C], in1=cs(0, r0, C).unsqueeze(1).to_broadcast([C, 4, C]), op=ALU.add), [MTc, cst], [Xc])
                    for lev in range(nlev):
                        last = lev == nlev - 1
                        for h in range(4):
                            P(lambda e: e.matmul(pb[rs_, h * 64:h * 64 + C], lhsT=MTc[rs_, h, :C], rhs=Mc[rs_, h, :C], start=True, stop=True), [MTc, Mc], [pb])
                        if not last:
                            for h in range(4):
                                P(lambda e: e.matmul(pb[rs_, 256 + h * 64:256 + h * 64 + C], lhsT=Mc[rs_, h, :C], rhs=MTc[rs_, h, :C], start=True, stop=True), [MTc, Mc], [pb])
                        A(lambda e: e.copy(out=Mc[rs_, :, :C], in_=v4(pb, 4, 64, rs_)[:, :, :C]), [pb], [Mc])
                        if not last:
                            A(lambda e: e.copy(out=MTc[rs_, :, :C], in_=v4(pb, 4, 64, rs_, 256)[:, :, :C]), [pb], [MTc])
                        for h in range(4):
                            P(lambda e: e.matmul(pb[rs_, h * 64:h * 64 + C], lhsT=Mc[rs_, h, :C], rhs=Xc[rs_, h, :C], start=True, stop=True), [Mc, Xc], [pb])
                        V(lambda e: e.tensor_tensor(out=Xc[rs_, :, :C], in0=Xc[rs_, :, :C], in1=v4(pb, 4, 64, rs_)[:, :, :C], op=ALU.add), [Xc, pb], [Xc])

                if nch == 2:
                    fw.co.fork([lambda: solve_chunk(0), lambda: solve_chunk(1)], [4, 4])
                else:
                    solve_chunk(0)
                for c in range(nch):
                    r0 = c * C
                    rs_ = slice(r0, r0 + C)
                    Xc = XcC[c]
                    pw = pbank()
                    for h in range(4):
                        P(lambda e: e.matmul(pw[:, h * 64:h * 64 + C], lhsT=kbg[rs_, h, :], rhs=Xc[rs_, h, :C], start=True, stop=True), [kbg, Xc], [pw])
                    A(lambda e: e.mul(out=nwT[:, :, :C], in_=v4(pw, 4, 64)[:, :, :C], mul=-1.0), [pw], [nwT])
                    pv = pbank()
                    for h in range(4):
                        P(lambda e: e.matmul(pv[rs_, h * 128:(h + 1) * 128], lhsT=Xc[rs_, h, :C], rhs=vb[rs_, h, :], start=True, stop=False), [Xc, vb], [pv])
                        P(lambda e: e.matmul(pv[rs_, h * 128:(h + 1) * 128], lhsT=nwT[:, h, :C], rhs=SD[:, h, :], start=False, stop=True), [nwT, SD], [pv])
                    A(lambda e: e.copy(out=vnew[rs_], in_=v4(pv, rows=rs_)), [pv], [vnew])
                    if full:
                        pq = pbank()
                        for h in range(4):
                            P(lambda e: e.matmul(pq[rs_, h * 64:h * 64 + C], lhsT=kTg[:, h, rs_], rhs=qT[:, h, rs_], start=True, stop=True), [kTg, qT], [pq])
                        V(lambda e: e.tensor_tensor(out=ATc[rs_, :, :C], in0=v4(pq, 4, 64, rs_)[:, :, :C], in1=decT[rs_, :, rs_], op=ALU.mult), [pq, decT], [ATc])
                        po2 = pbank()
                        for h in range(4):
                            P(lambda e: e.matmul(po2[rs_, h * 128:(h + 1) * 128], lhsT=qgT[:, h, rs_], rhs=SD[:, h, :], start=True, stop=False), [qgT, SD], [po2])
                            P(lambda e: e.matmul(po2[rs_, h * 128:(h + 1) * 128], lhsT=ATc[rs_, h, :C], rhs=vnew[rs_, h, :], start=False, stop=True), [ATc, vnew], [po2])
                        A(lambda e: e.copy(out=o12h[1][rs_, :, :], in_=v4(po2, rows=rs_)), [po2], [o12h[1]])
                    ps2 = pbank()
                    for h in range(4):
                        P(lambda e: e.matmul(ps2[:, h * 128:(h + 1) * 128], lhsT=kd[rs_, h, :], rhs=vnew[rs_, h, :], start=True, stop=True), [kd, vnew], [ps2])
                    V(lambda e: e.tensor_tensor(out=SD[:], in0=SD[:], in1=egl2[:, :, c:c + 1].to_broadcast([128, 4, 128]), op=ALU.mult), [SD, egl2], [SD])
                    V(lambda e: e.tensor_tensor(out=SD[:], in0=SD[:], in1=v4(ps2), op=ALU.add), [SD, ps2], [SD])
                if not full:
                    return
                gated_norm(1, gnw2, sdz, T, 512)

            fw.co.fork([gla_chain, gdn_chain], [2, 8])
            if True:
                if not full:
                    return
                if dname:
                    dump(dname + "_mix", mixb, mixb[:T, :], [T, D])

            fw.barrier()
            fw.dma("sp", lambda e: e.dma_start(out=wout[:, :, :], in_=wo_b), reads=[scr_wo], writes=[wout], owner=wout)
            for k in range(0, 8, 2):
                fw.dma("sp", lambda e: e.dma_start(out=wq[:, k:k + 2, :], in_=wq_b[:, k:k + 2, :]), reads=[scr_wq], writes=[wq], owner=wq)
            transpose_bf(mixb, mixT, T)
            for n in range(2):
                pp = pbank()
                for k in range(8):
                    P(lambda e: e.matmul(pp[:T, :], lhsT=mixT[:, k, :T], rhs=wout[:, k, n * 512:(n + 1) * 512], start=(k == 0), stop=(k == 7)), [mixT, wout], [pp])
                V(lambda e: e.tensor_tensor(out=x2[:T, n * 512:(n + 1) * 512], in0=xt[:T, n * 512:(n + 1) * 512], in1=pp[:T, :], op=ALU.add), [xt, pp], [x2])
            if dname:
                dump(dname + "_x2", x2, x2[:T, :], [T, D])

            if limit < 3:
                fw.barrier()
                return
            rmsnorm_rs(x2, T)
            V(lambda e: e.scalar_tensor_tensor(out=h2b[:T, :], in0=x2[:T, :], scalar=ss[:T, 0:1], in1=nfw[:T, :], op0=ALU.mult, op1=ALU.mult), [x2, ss, nfw], [h2b])
            transpose_bf(h2b, hT, T)
            for g in range(4):
                pp = pbank()
                for j in range(4):
                    cb = g * 4 + j
                    for k in range(8):
                        P(lambda e: e.matmul(pp[:, j * 128:j * 128 + T], lhsT=wq[:, k, cb * 128:(cb + 1) * 128], rhs=hT[:, k, :T], start=(k == 0), stop=(k == 7)), [wq, hT], [pp])
                A(lambda e: e.copy(pqT[:, g * 4:(g + 1) * 4, :T], v4(pp)[:, :, :T]), [pp], [pqT])
            for half in range(2):
                for g in range(2):
                    pp = pbank()
                    for j in range(4):
                        h = g * 4 + j
                        P(lambda e: e.matmul(pp[:T, j * 128:(j + 1) * 128], lhsT=pqT[:, 2 * h + half, :T], rhs=kT[:, half * 8 + h, :], start=True, stop=True), [pqT, kT], [pp])
                    A(lambda e: e.copy(s12[:T, half * 8 + g * 4:half * 8 + g * 4 + 4, :], v4(pp, rows=slice(0, T))), [pp], [s12])
            for s in range(16):
                V(lambda e: e.max(out=v12[:T, s, 0:8], in_=s12[:T, s, :]), [s12], [v12])
                V(lambda e: e.max_index(out=i12[:T, s, 0:8], in_max=v12[:T, s, 0:8], in_values=s12[:T, s, :]), [s12, v12], [i12])
                V(lambda e: e.match_replace(out=wk[:T, 0:128], in_to_replace=v12[:T, s, 0:8], in_values=s12[:T, s, :], imm_value=NEG), [s12, v12], [wk])
                V(lambda e: e.max(out=v12[:T, s, 8:16], in_=wk[:T, 0:128]), [wk], [v12])
                V(lambda e: e.max_index(out=i12[:T, s, 8:16], in_max=v12[:T, s, 8:16], in_values=wk[:T, 0:128]), [wk, v12], [i12])
            V(lambda e: e.tensor_copy(out=i12f[:T], in_=i12[:T]), [i12], [i12f])
            V(lambda e: e.tensor_scalar(out=i12f[:T, 0:8, :], in0=i12f[:T, 0:8, :], scalar1=128.0, scalar2=None, op0=ALU.mult), [i12f], [i12f])
            c4 = lambda t_: t_[:T].rearrange("p h (a b) -> p h a b", b=16)
            b1 = lambda t_: t_[:T, 0:8, :].unsqueeze(3).to_broadcast([T, 8, 16, 16])
            b2 = lambda t_: t_[:T, 8:16, :].unsqueeze(2).to_broadcast([T, 8, 16, 16])
            V(lambda e: e.tensor_tensor(out=c4(cand), in0=b1(v12), in1=b2(v12), op=ALU.add), [v12], [cand])
            pos = i12[:T, 0:8, :]
            pau = i12[:T, 8:16, :]
            for h in range(8):
                V(lambda e: e.max(out=sc[:T, h, 0:8], in_=cand[:T, h, :]), [cand], [sc])
                V(lambda e: e.max_index(out=i12[:T, h, 0:8], in_max=sc[:T, h, 0:8], in_values=cand[:T, h, :]), [cand, sc], [i12])
                V(lambda e: e.match_replace(out=wk[:T, :], in_to_replace=sc[:T, h, 0:8], in_values=cand[:T, h, :], imm_value=NEG), [cand, sc], [wk])
                V(lambda e: e.max(out=sc[:T, h, 8:16], in_=wk[:T, :]), [wk], [sc])
                V(lambda e: e.max_index(out=i12[:T, h, 8:16], in_max=sc[:T, h, 8:16], in_values=wk[:T, :]), [wk, sc], [i12])
            V(lambda e: e.tensor_single_scalar(out=pau, in_=pos, scalar=4, op=ALU.logical_shift_right), [i12], [i12])
            V(lambda e: e.tensor_single_scalar(out=pos, in_=pos, scalar=15, op=ALU.bitwise_and), [i12], [i12])
            V(lambda e: e.tensor_copy(out=v12[:T, 0:8, :], in_=pau), [i12], [v12])
            V(lambda e: e.tensor_copy(out=v12[:T, 8:16, :], in_=pos), [i12], [v12])
            iota4 = cst[:T, 6, 16:32].unsqueeze(1).unsqueeze(1).to_broadcast([T, 8, 16, 16])
            sel = [eidf[:T, :].rearrange("p (a b) -> p a b", b=16), wk[:T, 0:128].rearrange("p (a b) -> p a b", b=16)]
            for w_ in range(2):
                V(lambda e: e.tensor_tensor(out=c4(cidx), in0=v12[:T, w_ * 8:(w_ + 1) * 8, :].unsqueeze(3).to_broadcast([T, 8, 16, 16]), in1=iota4, op=ALU.is_equal), [v12, cst], [cidx])
                V(lambda e: e.tensor_tensor(out=c4(cidx), in0=c4(cidx), in1=i12f[:T, w_ * 8:(w_ + 1) * 8, :].unsqueeze(2).to_broadcast([T, 8, 16, 16]), op=ALU.mult), [cidx, i12f], [cidx])
                V(lambda e: e.tensor_reduce(out=sel[w_], in_=c4(cidx), axis=AX.X, op=ALU.add), [cidx], [eidf if w_ == 0 else wk])
            V(lambda e: e.tensor_tensor(out=eidf[:T, :], in0=eidf[:T, :], in1=wk[:T, 0:128], op=ALU.add), [eidf, wk], [eidf])
            V(lambda e: e.tensor_scalar(out=eidf[:T, :], in0=eidf[:T, :], scalar1=16383.0, scalar2=None, op0=ALU.min), [eidf], [eidf])
            V(lambda e: e.tensor_copy(out=eid[:T, :], in_=eidf[:T, :]), [eidf], [eid])
            V(lambda e: e.tensor_tensor(out=gate[:T], in0=sc[:T], in1=sc[:T, :, 0:1].to_broadcast([T, 8, 16]), op=ALU.subtract), [sc], [gate])
            A(lambda e: e.activation(out=gate[:T], in_=gate[:T], func=AF.Exp), [gate], [gate])
            V(lambda e: e.tensor_reduce(out=zz[:T, :], in_=gate[:T], axis=AX.X, op=ALU.add), [gate], [zz])
            V(lambda e: e.reciprocal(out=zz[:T, :], in_=zz[:T, :]), [zz], [zz])
            V(lambda e: e.tensor_tensor(out=gate[:T], in0=gate[:T], in1=zz[:T, :].unsqueeze(2).to_broadcast([T, 8, 16]), op=ALU.mult), [gate, zz], [gate])
            fw.barrier()

        def tile_G(par, T, ydst):
            x2, h2b, eid, gate = x2s[par], h2bs[par], eids[par], gates[par]
            LA = NSLOT - 2
            slot = {}

            def gather(i, table):
                r_ = ring[rctr[0] % NSLOT]
                rctr[0] += 1
                slot[i] = r_
                fw.dma("pool", lambda e: e.indirect_dma_start(out=r_[:T, :], out_offset=None, in_=table,
                                                              in_offset=IndirectOffsetOnAxis(ap=eid[:T, i:i + 1], axis=0)), reads=[eid], writes=[r_])
            for i in range(128 + LA):
                if i < 128:
                    gather(i, peer_u)
                j = i - LA
                if j >= 0:
                    u_ = slot.pop(j)
                    V(lambda e: e.scalar_tensor_tensor(out=u_[:T, :], in0=u_[:T, :], scalar=1.0, in1=h2b[:T, :], op0=ALU.mult, op1=ALU.mult,
                                                       accum_out=actv[:T, j:j + 1]), [u_, h2b], [u_, actv])
            V(lambda e: e.tensor_tensor(out=tmpa[:T, :], in0=actv[:T, :], in1=actv[:T, :], op=ALU.mult), [actv], [tmpa])
            V(lambda e: e.tensor_scalar(out=tmpa[:T, :], in0=tmpa[:T, :], scalar1=0.044715, scalar2=1.0, op0=ALU.mult, op1=ALU.add), [tmpa], [tmpa])
            V(lambda e: e.tensor_tensor(out=tmpa[:T, :], in0=tmpa[:T, :], in1=actv[:T, :], op=ALU.mult), [tmpa, actv], [tmpa])
            A(lambda e: e.activation(out=tmpa[:T, :], in_=tmpa[:T, :], func=AF.Sigmoid, scale=1.5957691216057308), [tmpa], [tmpa])
            V(lambda e: e.tensor_tensor(out=wgt[:T, :], in0=tmpa[:T, :], in1=actv[:T, :], op=ALU.mult), [tmpa, actv], [wgt])
            V(lambda e: e.tensor_tensor(out=wgt[:T, :], in0=wgt[:T, :], in1=gate[:T].rearrange("p a b -> p (a b)"), op=ALU.mult), [wgt, gate], [wgt])
            py = [pbanks[5], pbanks[6]]
            for i in range(128 + LA):
                if i < 128:
                    gather(i, peer_v)
                j = i - LA
                if j >= 0:
                    v_ = slot.pop(j)
                    d_ = dg[j % 4]
                    A(lambda e: e.activation(out=d_[:T, :T], in_=cs(0, 0, T), func=AF.Copy, scale=wgt[:T, j:j + 1]), [cst, wgt], [d_])
                    for n in range(2):
                        P(lambda e: e.matmul(py[n][:T, :], lhsT=d_[:T, :T], rhs=v_[:T, n * 512:(n + 1) * 512], start=(j == 0), stop=(j == 127)), [d_, v_], [py[n]])
            for n in range(2):
                V(lambda e: e.tensor_tensor(out=x2[:T, n * 512:(n + 1) * 512], in0=x2[:T, n * 512:(n + 1) * 512], in1=py[n][:T, :], op=ALU.add), [x2, py[n]], [x2])
            A(lambda e: e.activation(out=h2b[:T, :], in_=x2[:T, :], func=AF.Square, accum_out=ssg[:T, :]), [x2], [h2b, ssg])
            A(lambda e: e.activation(out=ssg[:T, :], in_=ssg[:T, :], func=AF.Sqrt, scale=1.0 / D, bias=EPS), [ssg], [ssg])
            V(lambda e: e.reciprocal(out=ssg[:T, :], in_=ssg[:T, :]), [ssg], [ssg])
            V(lambda e: e.scalar_tensor_tensor(out=x2[:T, :], in0=x2[:T, :], scalar=ssg[:T, 0:1], in1=nzw[:T, :], op0=ALU.mult, op1=ALU.mult), [x2, ssg, nzw], [x2])
            fw.dma("sp", lambda e: e.dma_start(out=ydst, in_=x2[:T, :]), reads=[x2])

        def store_states(sk, o_gla, o_gdn, o_conv, T):
            gl = o_gla.rearrange("h k v -> (h k) v")
            for p_ in range(2):
                fw.dma("sp", lambda e: e.dma_start(out=gl[p_ * 128:(p_ + 1) * 128, :], in_=Sgla[sk][:, p_, :]), reads=[Sgla[sk]])
            fw.dma("sp", lambda e: e.dma_start(out=o_gdn.rearrange("h k v -> k h v"), in_=Sgdn[sk][:]), reads=[Sgdn[sk]])
            for r in range(3):
                for b0_ in range(0, 12, 3):
                    fw.dma("sp", lambda e: e.dma_start(out=o_conv[r:r + 1, :].rearrange("o (b p) -> p b o", p=128)[:, b0_:b0_ + 3, :], in_=convin[:, b0_:b0_ + 3, r:r + 1],
                                                       allow_slow_non_contiguous=True), reads=[convin])

        KM, KG = 5, 4
        pend = [None]
        fullctr = [0]

        def run_pair(Mf):
            if pend[0] is not None:
                fw.co.fork([Mf, pend[0]], [KM, KG])
            else:
                Mf()

        for i in range(NPRE):
            do_tile(xpre[i * 128:(i + 1) * 128, :], 128, 64, 2, False, "p", need_q=(i == NPRE - 1))
        for i in range(NMAIN):
            par = fullctr[0] % 2
            fullctr[0] += 1
            run_pair(lambda i=i, par=par: do_tile(xmain[i * 128:(i + 1) * 128, :], 128, 64, 2, True, "p", par, dname=("m%d" % i) if dbg else None))
            pend[0] = (lambda i=i, par=par: tile_G(par, 128, y_main[i * 128:(i + 1) * 128, :])) if limit >= 3 else None
        store_states("p", gla_st, gdn_st, conv_st, 128)
        for j in range(NSAMP):
            sk = "s%d" % j
            gl = sgla_in[j].rearrange("h k v -> (h k) v")
            for p_ in range(2):
                fw.dma("sp", lambda e: e.dma_start(out=Sgla[sk][:, p_, :], in_=gl[p_ * 128:(p_ + 1) * 128, :]), writes=[Sgla[sk]])
            fw.dma("sp", lambda e: e.dma_start(out=Sgdn[sk][:], in_=sgdn_in[j].rearrange("h k v -> k h v")), writes=[Sgdn[sk]])
            for r in range(3):
                for b0_ in range(0, 12, 3):
                    fw.dma("sp", lambda e: e.dma_start(out=convin[:, b0_:b0_ + 3, r:r + 1], in_=sconv_in[j, r:r + 1, :].rearrange("o (b p) -> p b o", p=128)[:, b0_:b0_ + 3, :],
                                                       allow_slow_non_contiguous=True), writes=[convin])
            par = fullctr[0] % 2
            fullctr[0] += 1
            run_pair(lambda j=j, par=par, sk=sk: do_tile(xs[j * 32:(j + 1) * 32, :], 32, 32, 1, True, sk, par, dname=("s%d" % j) if dbg else None))
            pend[0] = (lambda j=j, par=par: tile_G(par, 32, y_s[j * 32:(j + 1) * 32, :])) if limit >= 3 else None
            store_states(sk, gla_s[j], gdn_s[j], conv_s[j], 32)
        if pend[0] is not None:
            pend[0]()
        fw.finish("sp")
        print("instructions:", fw.nins, {e: fw.cnt[e] for e in fw.ENGS})
    return nc


N_CORES = 8
WNAMES = ["norm_mix_w", "w_in", "gla_w_gk2", "gla_b_gk", "gla_norm_w", "gdn_conv_w", "gdn_a_log", "gdn_dt_bias",
          "gdn_norm_w", "w_out", "norm_ffn_w", "peer_wq", "peer_k1", "peer_k2", "peer_u", "peer_v"]


def kernel(x_prompt, x_sample, state_gla, state_gdn, state_gdn_conv, norm_mix_w, w_in, gla_w_gk2, gla_b_gk,
           gla_norm_w, gdn_conv_w, gdn_a_log, gdn_dt_bias, gdn_norm_w, w_out, norm_ffn_w, peer_wq, peer_k1,
           peer_k2, peer_u, peer_v, norm_final_w):
    f = lambda a: np.ascontiguousarray(np.asarray(a, dtype=np.float32))
    x_prompt, x_sample = f(x_prompt), f(x_sample)
    B, L, _ = x_prompt.shape
    HALF = L // 2
    NT = HALF // 128
    loc = dict(norm_mix_w=norm_mix_w, w_in=w_in, gla_w_gk2=gla_w_gk2, gla_b_gk=gla_b_gk, gla_norm_w=gla_norm_w,
               gdn_conv_w=gdn_conv_w, gdn_a_log=gdn_a_log, gdn_dt_bias=gdn_dt_bias, gdn_norm_w=gdn_norm_w, w_out=w_out,
               norm_ffn_w=norm_ffn_w, peer_wq=peer_wq, peer_k1=peer_k1, peer_k2=peer_k2, peer_u=peer_u, peer_v=peer_v)
    shared = {k: f(v)[0] for k, v in loc.items()}
    shared["norm_final_w"] = f(norm_final_w)
    shared["cst"] = make_consts()
    nc = build(NT, NT, 2)
    in_maps = []
    zeros = np.zeros((HALF, D), np.float32)
    for c in range(N_CORES):
        b, half = c // 2, c % 2
        m = dict(shared)
        m["xpre"] = x_prompt[b, :HALF] if half == 1 else zeros
        m["xmain"] = x_prompt[b, half * HALF:(half + 1) * HALF]
        m["xs"] = x_sample[2 * c:2 * c + 2].reshape(64, D)
        m["sgla_in"] = f(state_gla)[0, 2 * c:2 * c + 2]
        m["sgdn_in"] = f(state_gdn)[0, 2 * c:2 * c + 2]
        m["sconv_in"] = f(state_gdn_conv)[0, 2 * c:2 * c + 2]
        in_maps.append({k: np.ascontiguousarray(v) for k, v in m.items()})
    res = run_bass_kernel_spmd(nc, in_maps, core_ids=list(range(N_CORES))).results
    y_prompt = np.zeros((B, L, D), np.float32)
    y_sample = np.zeros((16, 32, D), np.float32)
    gla_p = np.zeros((1, B, 4, 64, 128), np.float32)
    gdn_p = np.zeros((1, B, 4, 128, 128), np.float32)
    conv_p = np.zeros((1, B, 3, 1536), np.float32)
    gla_sm = np.zeros((1, 16, 4, 64, 128), np.float32)
    gdn_sm = np.zeros((1, 16, 4, 128, 128), np.float32)
    conv_sm = np.zeros((1, 16, 3, 1536), np.float32)
    for c in range(N_CORES):
        b, half = c // 2, c % 2
        r = res[c]
        y_prompt[b, half * HALF:(half + 1) * HALF] = r["y_main"]
        y_sample[2 * c:2 * c + 2] = r["y_s"].reshape(2, 32, D)
        if half == 1:
            gla_p[0, b] = r["gla_st"]
            gdn_p[0, b] = r["gdn_st"]
            conv_p[0, b] = r["conv_st"]
        gla_sm[0, 2 * c:2 * c + 2] = r["gla_s"]
        gdn_sm[0, 2 * c:2 * c + 2] = r["gdn_s"]
        conv_sm[0, 2 * c:2 * c + 2] = r["conv_s"]
    return (y_prompt, y_sample, gla_p, gdn_p, conv_p, gla_sm, gdn_sm, conv_sm)
```
